# Optimizing a Trainium2 kernel written in Bass

```python
import math
import jax, jax.numpy as jnp
from jax import lax
import numpy as np

D_MODEL = 1024
BATCH = 16
SEQ = 4096
DEPTH = 1

SSD_HEADS = 16
SSD_HEAD_DIM = 64
SSD_INNER = SSD_HEADS * SSD_HEAD_DIM
SSD_GROUPS = 2
SSD_STATE = 128
SSD_BC = SSD_GROUPS * SSD_STATE
SSD_CONV = 4
SSD_CONV_DIM = SSD_INNER + 2 * SSD_BC
SSD_CHUNK = 128
DT_MIN = 0.001
DT_MAX = 0.1
ATT_HEADS = 16
ATT_KV_HEADS = 4
ATT_HEAD_DIM = 64
ATT_INNER = ATT_HEADS * ATT_HEAD_DIM
IDX_HEADS = 8
IDX_DIM = 64
TOPK_MAX = 256
Q_BLOCK = 128
MIX_WIDTH = SSD_INNER + ATT_INNER
IN_SPLITS = (SSD_INNER, SSD_CONV_DIM, SSD_HEADS,
             ATT_INNER, ATT_KV_HEADS * ATT_HEAD_DIM, ATT_KV_HEADS * ATT_HEAD_DIM,
             IDX_HEADS * IDX_DIM, IDX_DIM, IDX_HEADS)
D_IN_PROJ = sum(IN_SPLITS)
D_FF = 2816
FFN_CONV = 3
LN_EPS = 1e-5
RMS_EPS = 1e-5
DEEPNORM_ALPHA = (2 * DEPTH) ** 0.25
DEEPNORM_BETA = (8 * DEPTH) ** -0.25

kernel_name = "hymba_ssd_dsa_convffn_deepnorm"


def _layer_norm(x, g, b):
    xf = x.astype(jnp.float32)
    mu = jnp.mean(xf, axis=-1, keepdims=True)
    var = jnp.mean(jnp.square(xf - mu), axis=-1, keepdims=True)
    return ((xf - mu) * lax.rsqrt(var + LN_EPS) * g + b).astype(x.dtype)


def _causal_dwconv(x, w, b):
    width = w.shape[0]
    y = lax.conv_general_dilated(
        x, w[:, None, :].astype(x.dtype), window_strides=(1,),
        padding=[(width - 1, 0)], dimension_numbers=("NWC", "WIO", "NWC"),
        feature_group_count=x.shape[-1])
    return y + b


def _ssd_scan(xs, dt, a, bm, cm):
    bsz, seqlen, nh, hp = xs.shape
    q = SSD_CHUNK
    nc = seqlen // q
    hg = nh // SSD_GROUPS
    xdt = (xs.astype(jnp.float32) * dt[..., None]).reshape(bsz, nc, q, SSD_GROUPS, hg, hp)
    acs = jnp.cumsum((dt * a).reshape(bsz, nc, q, SSD_GROUPS, hg), axis=2)
    bc = bm.astype(jnp.float32).reshape(bsz, nc, q, SSD_GROUPS, SSD_STATE)
    cc = cm.astype(jnp.float32).reshape(bsz, nc, q, SSD_GROUPS, SSD_STATE)
    causal = jnp.tril(jnp.ones((q, q), dtype=bool))
    seg = acs[:, :, :, None] - acs[:, :, None, :]
    lmat = jnp.exp(jnp.where(causal[:, :, None, None], seg, -jnp.inf))
    cb = jnp.einsum("bclgn,bcsgn->bclsg", cc, bc)
    y_diag = jnp.einsum("bclsg,bclsgh,bcsghp->bclghp", cb, lmat, xdt)
    decay_to_end = jnp.exp(acs[:, :, -1:] - acs)
    states = jnp.einsum("bclgn,bclgh,bclghp->bcghpn", bc, decay_to_end, xdt)
    chunk_decay = jnp.exp(acs[:, :, -1])

    def step(carry, inp):
        st, dec = inp
        return carry * dec[..., None, None] + st, carry

    init = jnp.zeros((bsz, SSD_GROUPS, hg, hp, SSD_STATE), jnp.float32)
    _, prev = lax.scan(step, init, (jnp.moveaxis(states, 1, 0), jnp.moveaxis(chunk_decay, 1, 0)))
    prev = jnp.moveaxis(prev, 0, 1)
    y_off = jnp.einsum("bclgn,bcghpn,bclgh->bclghp", cc, prev, jnp.exp(acs))
    return (y_diag + y_off).reshape(bsz, seqlen, nh, hp)


def _dsa_attention(q, k, v, q_idx, k_idx, w_idx):
    bsz, seqlen = q.shape[:2]
    topk = min(TOPK_MAX, seqlen // 4)
    nb = seqlen // Q_BLOCK
    n_rep = ATT_HEADS // ATT_KV_HEADS
    key_pos = jnp.arange(seqlen)
    gather = jax.vmap(lambda t, idx: t[idx])

    def to_blocks(t):
        return jnp.moveaxis(t.reshape(bsz, nb, Q_BLOCK, *t.shape[2:]), 1, 0)

    def block(args):
        qb, qib, wb, start = args
        q_pos = start + jnp.arange(Q_BLOCK)
        visible = key_pos[None, :] <= q_pos[:, None]
        s = jnp.einsum("bthd,bsd->bths", qib, k_idx).astype(jnp.float32) * IDX_DIM ** -0.5
        score = jnp.einsum("bths,bth->bts", jax.nn.relu(s), wb.astype(jnp.float32))
        score = jnp.where(visible[None], score, -jnp.inf)
        _, top_idx = lax.top_k(score, topk)
        keep = top_idx <= q_pos[None, :, None]
        k_sel = gather(k, top_idx)
        v_sel = gather(v, top_idx)
        qg = qb.reshape(bsz, Q_BLOCK, ATT_KV_HEADS, n_rep, ATT_HEAD_DIM)
        logits = jnp.einsum("btngd,btsnd->btngs", qg, k_sel).astype(jnp.float32) * ATT_HEAD_DIM ** -0.5
        logits = jnp.where(keep[:, :, None, None, :], logits, -jnp.inf)
        p = jax.nn.softmax(logits, axis=-1).astype(v.dtype)
        o = jnp.einsum("btngs,btsnd->btngd", p, v_sel)
        return o.reshape(bsz, Q_BLOCK, ATT_INNER)

    starts = jnp.arange(nb, dtype=jnp.int32) * Q_BLOCK
    out = lax.map(block, (to_blocks(q), to_blocks(q_idx), to_blocks(w_idx), starts))
    return jnp.moveaxis(out, 0, 1).reshape(bsz, seqlen, ATT_INNER)


def _hybrid_mixer(x, w_in, ssd_conv_w, ssd_conv_b, dt_bias, a_log, d_skip, ssd_norm_g,
                  idx_k_norm_g, idx_k_norm_b, w_out):
    bsz, seqlen, _ = x.shape
    proj = x @ w_in
    offsets = np.cumsum(IN_SPLITS)[:-1].tolist()
    z, xbc, dt_raw, q, k, v, qi, ki, wi = jnp.split(proj, offsets, axis=-1)
    xbc = jax.nn.silu(_causal_dwconv(xbc, ssd_conv_w, ssd_conv_b))
    xs, bm, cm = jnp.split(xbc, [SSD_INNER, SSD_INNER + SSD_BC], axis=-1)
    xs = xs.reshape(bsz, seqlen, SSD_HEADS, SSD_HEAD_DIM)
    dt = jax.nn.softplus(dt_raw.astype(jnp.float32) + dt_bias.astype(jnp.float32))
    a = -jnp.exp(a_log.astype(jnp.float32))
    y = _ssd_scan(xs, dt, a,
                  bm.reshape(bsz, seqlen, SSD_GROUPS, SSD_STATE),
                  cm.reshape(bsz, seqlen, SSD_GROUPS, SSD_STATE))
    y = y + d_skip.astype(jnp.float32)[:, None] * xs.astype(jnp.float32)
    gy = (y.reshape(bsz, seqlen, SSD_INNER) * jax.nn.silu(z.astype(jnp.float32)))
    gy = gy.reshape(bsz, seqlen, SSD_GROUPS, SSD_INNER // SSD_GROUPS)
    gy = gy * lax.rsqrt(jnp.mean(jnp.square(gy), axis=-1, keepdims=True) + RMS_EPS)
    y_ssd = (gy.reshape(bsz, seqlen, SSD_INNER) * ssd_norm_g).astype(x.dtype)
    q = q.reshape(bsz, seqlen, ATT_HEADS, ATT_HEAD_DIM)
    k = k.reshape(bsz, seqlen, ATT_KV_HEADS, ATT_HEAD_DIM)
    v = v.reshape(bsz, seqlen, ATT_KV_HEADS, ATT_HEAD_DIM)
    qi = qi.reshape(bsz, seqlen, IDX_HEADS, IDX_DIM)
    ki = _layer_norm(ki, idx_k_norm_g, idx_k_norm_b)
    wi = wi * IDX_HEADS ** -0.5
    y_att = _dsa_attention(q, k, v, qi, ki, wi)
    return jnp.concatenate([y_ssd, y_att.astype(x.dtype)], axis=-1) @ w_out


def _conv_ffn(h, w_up, conv_w, conv_b, w_down):
    u = _causal_dwconv(h @ w_up, conv_w, conv_b)
    gate, up = jnp.split(u, 2, axis=-1)
    return (jax.nn.silu(gate) * up) @ w_down


def setup_inputs(seed: int = 0) -> dict:
    key = jax.random.key(seed)
    ks = jax.random.split(key, 20)
    f32 = jnp.float32

    def nrm(k, shape, scale):
        return jax.random.normal(k, shape, f32) * scale

    x = nrm(ks[0], (BATCH, SEQ, D_MODEL), 1.0)
    v_off = sum(IN_SPLITS[:5])
    col_scale = jnp.ones((D_IN_PROJ,), f32).at[v_off:v_off + IN_SPLITS[5]].set(DEEPNORM_BETA)
    w_in = nrm(ks[1], (DEPTH, D_MODEL, D_IN_PROJ), D_MODEL ** -0.5) * col_scale
    ssd_conv_w = nrm(ks[2], (DEPTH, SSD_CONV, SSD_CONV_DIM), SSD_CONV ** -0.5)
    ssd_conv_b = nrm(ks[3], (DEPTH, SSD_CONV_DIM), 0.02)
    dt0 = jnp.exp(jax.random.uniform(ks[4], (DEPTH, SSD_HEADS), f32, math.log(DT_MIN), math.log(DT_MAX)))
    dt_bias = dt0 + jnp.log(-jnp.expm1(-dt0))
    a_log = jnp.log(jax.random.uniform(ks[5], (DEPTH, SSD_HEADS), f32, 1.0, 16.0))
    d_skip = 1.0 + nrm(ks[6], (DEPTH, SSD_HEADS), 0.1)
    ssd_norm_g = 1.0 + nrm(ks[7], (DEPTH, SSD_INNER), 0.02)
    idx_k_norm_g = 1.0 + nrm(ks[8], (DEPTH, IDX_DIM), 0.02)
    idx_k_norm_b = nrm(ks[9], (DEPTH, IDX_DIM), 0.02)
    w_out = nrm(ks[10], (DEPTH, MIX_WIDTH, D_MODEL), MIX_WIDTH ** -0.5 * DEEPNORM_BETA)
    ln1_g = 1.0 + nrm(ks[11], (DEPTH, D_MODEL), 0.02)
    ln1_b = nrm(ks[12], (DEPTH, D_MODEL), 0.02)
    ffn_w_up = nrm(ks[13], (DEPTH, D_MODEL, 2 * D_FF), D_MODEL ** -0.5)
    ffn_conv_w = nrm(ks[14], (DEPTH, FFN_CONV, 2 * D_FF), FFN_CONV ** -0.5)
    ffn_conv_b = nrm(ks[15], (DEPTH, 2 * D_FF), 0.02)
    ffn_w_down = nrm(ks[16], (DEPTH, D_FF, D_MODEL), D_FF ** -0.5 * DEEPNORM_BETA)
    ln2_g = 1.0 + nrm(ks[17], (DEPTH, D_MODEL), 0.02)
    ln2_b = nrm(ks[18], (DEPTH, D_MODEL), 0.02)
    return {"x": x, "w_in": w_in, "ssd_conv_w": ssd_conv_w, "ssd_conv_b": ssd_conv_b,
            "dt_bias": dt_bias, "a_log": a_log, "d_skip": d_skip, "ssd_norm_g": ssd_norm_g,
            "idx_k_norm_g": idx_k_norm_g, "idx_k_norm_b": idx_k_norm_b, "w_out": w_out,
            "ln1_g": ln1_g, "ln1_b": ln1_b, "ffn_w_up": ffn_w_up, "ffn_conv_w": ffn_conv_w,
            "ffn_conv_b": ffn_conv_b, "ffn_w_down": ffn_w_down, "ln2_g": ln2_g, "ln2_b": ln2_b}


def reference(x, w_in, ssd_conv_w, ssd_conv_b, dt_bias, a_log, d_skip, ssd_norm_g,
              idx_k_norm_g, idx_k_norm_b, w_out, ln1_g, ln1_b, ffn_w_up, ffn_conv_w,
              ffn_conv_b, ffn_w_down, ln2_g, ln2_b):
    for i in range(DEPTH):
        m = _hybrid_mixer(x, w_in[i], ssd_conv_w[i], ssd_conv_b[i], dt_bias[i], a_log[i],
                          d_skip[i], ssd_norm_g[i], idx_k_norm_g[i], idx_k_norm_b[i], w_out[i])
        h = _layer_norm(DEEPNORM_ALPHA * x + m, ln1_g[i], ln1_b[i])
        f = _conv_ffn(h, ffn_w_up[i], ffn_conv_w[i], ffn_conv_b[i], ffn_w_down[i])
        x = _layer_norm(DEEPNORM_ALPHA * h + f, ln2_g[i], ln2_b[i])
    return x
```

```python
import numpy as np
import concourse.bass as bass
import concourse.mybir as mybir
from contextlib import ExitStack

F32 = mybir.dt.float32
BF16 = mybir.dt.bfloat16
AF = mybir.ActivationFunctionType
ALU = mybir.AluOpType
AX = mybir.AxisListType


class _Res:
    __slots__ = ("w", "rs")

    def __init__(self):
        self.w = None
        self.rs = []


class _Op:
    __slots__ = ("eng", "fn", "deps", "signal", "sig", "dma", "prewait")

    def __init__(self, eng, fn, dma):
        self.eng = eng
        self.fn = fn
        self.deps = set()
        self.signal = False
        self.sig = None
        self.dma = dma
        self.prewait = None


class Sched:
    ENGS = ("pe", "act", "dve", "pool", "sp")
    EPOCH = 20000

    def __init__(self, nc, n_dma_sems=16):
        self.nc = nc
        self.ops = []
        self.res = {}
        self.n_dma_sems = n_dma_sems
        self.muted = False

    def _entries(self, key, create=True):
        if isinstance(key, tuple):
            name, reg = key[0], key[1:]
        else:
            name, reg = key, None
        d = self.res.setdefault(name, {"*": _Res()})
        if reg is None:
            return list(d.values()), True, d
        if reg not in d:
            d[reg] = _Res()
        return [d["*"], d[reg]], False, d

    def op(self, eng, fn, reads=(), writes=(), dma=False):
        if self.muted:
            return None
        o = _Op(eng, fn, dma)
        writes = list(writes) + [k for k in reads if isinstance(k, tuple) and k[0] == "ps" and k not in writes]
        for k in reads:
            ents, _, _ = self._entries(k)
            for e in ents:
                if e.w is not None:
                    o.deps.add((e.w, True))
        for k in writes:
            ents, _, _ = self._entries(k)
            for e in ents:
                if e.w is not None:
                    o.deps.add((e.w, False))
                for r in e.rs:
                    o.deps.add((r, False))
        for k in reads:
            ents, whole, d = self._entries(k)
            if whole:
                for e in ents:
                    e.rs.append(o)
            else:
                ents[1].rs.append(o)
        for k in writes:
            ents, whole, d = self._entries(k)
            if whole:
                for e in ents:
                    e.w = o
                    e.rs = []
            else:
                ents[1].w = o
                ents[1].rs = []
        self.ops.append(o)
        return o

    def emit(self, stack):
        nc = self.nc
        for o in self.ops:
            keep = set()
            for (d, raw) in o.deps:
                if d is o:
                    continue
                if d.dma or o.dma:
                    keep.add(d)
                elif d.eng != o.eng:
                    keep.add(d)
                elif raw and o.eng != "pe":
                    keep.add(d)
            o.deps = keep
            for d in keep:
                d.signal = True
        cnt = {e: 0 for e in self.ENGS}
        dcnt = {e: 0 for e in self.ENGS}
        sems = {}

        def getsem(name):
            if name not in sems:
                sems[name] = stack.enter_context(nc.semaphore(name))
            return sems[name]

        for o in self.ops:
            if o.dma:
                i = dcnt[o.eng]
                dcnt[o.eng] += 1
                k = i % self.n_dma_sems
                r = i // self.n_dma_sems
                o.sig = (f"d_{o.eng}_{k}", 16 * (r + 1))
                if r > 0:
                    o.prewait = (f"d_{o.eng}_{k}", 16 * r)
            elif o.signal:
                cnt[o.eng] += 1
                t = cnt[o.eng]
                o.sig = (f"s_{o.eng}_{(t - 1) // self.EPOCH}", (t - 1) % self.EPOCH + 1)
        for o in self.ops:
            if o.sig is not None:
                getsem(o.sig[0])
        per = {e: [o for o in self.ops if o.eng == e] for e in self.ENGS}
        nwaits = [0]

        def run(engname, eng):
            seen = {}
            for o in per[engname]:
                waits = {}
                if o.prewait is not None:
                    waits[o.prewait[0]] = o.prewait[1]
                for d in o.deps:
                    s, v = d.sig
                    if waits.get(s, 0) < v:
                        waits[s] = v
                for s, v in waits.items():
                    if seen.get(s, 0) >= v:
                        continue
                    seen[s] = v
                    eng.wait_ge(sems[s], v)
                    nwaits[0] += 1
                ins = o.fn(eng)
                if o.sig is not None:
                    ins.then_inc(sems[o.sig[0]], 16 if o.dma else 1)

        block = stack.enter_context(nc.Block())
        if per["pe"]:
            @block.tensor
            def _(e):
                run("pe", e)
        if per["act"]:
            @block.scalar
            def _(e):
                run("act", e)
        if per["dve"]:
            @block.vector
            def _(e):
                run("dve", e)
        if per["pool"]:
            @block.gpsimd
            def _(e):
                run("pool", e)
        if per["sp"]:
            @block.sync
            def _(e):
                run("sp", e)
        return {e: len(per[e]) for e in self.ENGS}, nwaits[0]

from concourse.bass_utils import run_bass_kernel_spmd

NIT = 16
ALPHA = 2.0 ** 0.25
NEG = -1.0e30


def build(NB, NQ, debug=False, stop=99):
    nc = bass.Bass("TRN2", target_bir_lowering=False)
    L = NQ * 512
    dram = {}

    def din(name, shape):
        dram[name] = nc.dram_tensor(name, shape, F32, kind="ExternalInput").ap()
        return dram[name]

    x = din("x", [NB, L, 1024])
    w_in = din("w_in", [1024, 4696])
    ssd_conv_w = din("ssd_conv_w", [4, 1536])
    ssd_conv_b = din("ssd_conv_b", [1, 1536])
    dt_bias = din("dt_bias", [1, 16])
    a_log = din("a_log", [1, 16])
    d_skip = din("d_skip", [1, 16])
    ssd_norm_g = din("ssd_norm_g", [1, 1024])
    idx_k_norm_g = din("idx_k_norm_g", [1, 64])
    idx_k_norm_b = din("idx_k_norm_b", [1, 64])
    w_out = din("w_out", [2048, 1024])
    ln1_g = din("ln1_g", [1, 1024])
    ln1_b = din("ln1_b", [1, 1024])
    w_up = din("ffn_w_up", [1024, 5632])
    ffn_conv_w = din("ffn_conv_w", [3, 5632])
    ffn_conv_b = din("ffn_conv_b", [1, 5632])
    w_down = din("ffn_w_down", [2816, 1024])
    ln2_g = din("ln2_g", [1, 1024])
    ln2_b = din("ln2_b", [1, 1024])
    out = nc.dram_tensor("out", [NB, L, 1024], F32, kind="ExternalOutput").ap()
    NWB = 31
    wsc = nc.dram_tensor("wsc", [NWB, 128, 4096], BF16, kind="Internal").ap()
    dbg = {}
    if debug:
        dbg["yssd"] = nc.dram_tensor("dbg_yssd", [NB, NQ, 128, 8, 512], BF16, kind="ExternalOutput").ap()
        dbg["yatt"] = nc.dram_tensor("dbg_yatt", [NB, NQ, 128, 8, 512], BF16, kind="ExternalOutput").ap()
        dbg["h"] = nc.dram_tensor("dbg_h", [NB, L, 1024], F32, kind="ExternalOutput").ap()
        dbg["thr"] = nc.dram_tensor("dbg_thr", [NB, NQ, 4, 128, 2], F32, kind="ExternalOutput").ap()

    st = ExitStack()
    S = Sched(nc)

    def phase(n):
        if n > stop:
            S.muted = True

    BIG = 207 * 1024
    big = nc.alloc_sbuf_tensor("big", [128, BIG], mybir.dt.uint8)
    base = nc.lookup_mloc(big).addr
    cur = [base]
    uid = [0]

    def alloc_at(off, shape, dt):
        uid[0] += 1
        return nc.alloc_sbuf_tensor_at(f"t{uid[0]}", list(shape), dt, offset=off)

    def nbytes(shape, dt):
        return int(np.prod(shape[1:])) * (4 if dt == F32 else 2)

    def sb(shape, dt):
        t = alloc_at(cur[0], shape, dt)
        cur[0] += (nbytes(shape, dt) + 31) // 32 * 32
        return t

    psall = nc.alloc_psum_tensor("psall", [128, 4096], F32)
    banks = [psall[:, 512 * i:512 * i + 512] for i in range(8)]
    bankbf = [b.bitcast(BF16) for b in banks]
    BK = [("ps", i) for i in range(8)]

    def mm(out_, lhsT, rhs, start, stop, reads, writes):
        S.op("pe", lambda e: e.matmul(out_, lhsT=lhsT, rhs=rhs, start=start, stop=stop), reads, writes)

    def tr(out_, in_, idn, reads, writes):
        S.op("pe", lambda e: e.transpose(out=out_, in_=in_, identity=idn), reads, writes)

    def act(out_, in_, func, reads, writes, **kw):
        S.op("act", lambda e: e.activation(out=out_, in_=in_, func=func, **kw), reads, writes)

    def ts(eng, out_, in0, s1, s2, op0, op1, reads, writes, accum=None):
        if accum is None:
            S.op(eng, lambda e: e.tensor_scalar(out=out_, in0=in0, scalar1=s1, scalar2=s2, op0=op0, op1=op1), reads, writes)
        else:
            S.op(eng, lambda e: e.tensor_scalar(out=out_, in0=in0, scalar1=s1, scalar2=s2, op0=op0, op1=op1, accum_out=accum), reads, writes)

    def stt(eng, out_, in0, scalar, in1, op0, op1, reads, writes):
        S.op(eng, lambda e: e.scalar_tensor_tensor(out=out_, in0=in0, scalar=scalar, in1=in1, op0=op0, op1=op1), reads, writes)

    def tt(eng, out_, in0, in1, op, reads, writes):
        S.op(eng, lambda e: e.tensor_tensor(out=out_, in0=in0, in1=in1, op=op), reads, writes)

    def cp(eng, out_, in_, reads, writes):
        if eng == "act":
            S.op("act", lambda e: e.activation(out=out_, in_=in_, func=AF.Copy), reads, writes)
        else:
            S.op(eng, lambda e: e.tensor_copy(out=out_, in_=in_), reads, writes)

    def mset(eng, out_, val, writes):
        S.op(eng, lambda e: e.memset(out_, val), (), writes)

    def dma(out_, in_, reads, writes, q="sp"):
        S.op(q, lambda e: e.dma_start(out=out_, in_=in_), reads, writes, dma=True)

    def red(eng, out_, in_, op, reads, writes, absval=False):
        if absval:
            S.op(eng, lambda e: e.tensor_reduce(out=out_, in_=in_, axis=AX.X, op=op, apply_absolute_value=True), reads, writes)
        else:
            S.op(eng, lambda e: e.tensor_reduce(out=out_, in_=in_, axis=AX.X, op=op), reads, writes)

    Wr = [sb([128, 4096], BF16) for _ in range(2)]
    A = sb([128, 4, 1024], F32)
    xT = sb([128, 8, 512], BF16)
    xb16 = [sb([128, 1024], BF16) for _ in range(2)]
    qT = sb([128, 8, 512], BF16)
    qiT = sb([128, 4, 512], BF16)
    BT = sb([128, 2, 512], BF16)
    CT = sb([128, 2, 512], BF16)
    kT = sb([128, 2, L], BF16)
    V = sb([128, 4 * NQ, 384], BF16)
    kiT = sb([128, L], BF16)
    Sst = sb([128, 1024], F32)
    Sb = sb([128, 1024], BF16)
    yssdT = sb([128, 8, 512], BF16)
    yattT = sb([128, 8, 512], BF16)
    halo_s = sb([128, 12, 3], F32)
    halo_f = sb([128, 44, 2], F32)
    kif = sb([128, 4, 64], F32)
    kixc = sb([128, 4, 64], F32)
    kidup = sb([128, 4, 128], BF16)
    dtw = sb([128, 4, 24], F32)
    dtx = sb([128, 4, 16], F32)
    dtl = sb([128, 4, 16], F32)
    dt_t = sb([128, 4, 16], F32)
    dtA_t = sb([128, 4, 16], F32)
    absw = sb([128, 4, 8], F32)
    sgnw = sb([128, 4, 8], F32)
    sm4 = sb([128, 4, 8], F32)
    ident = sb([128, 128], BF16)
    identf = sb([128, 128], F32)
    triu = sb([128, 128], F32)
    strict = sb([128, 128], F32)
    negtri = sb([128, 128], F32)
    onesf = sb([128, 128], F32)
    ones_bf = sb([128, 64], BF16)
    shup = sb([128, 128], F32)
    cw = sb([128, 48], F32)
    cb = sb([128, 12], F32)
    cwf = sb([128, 132], F32)
    cbf = sb([128, 44], F32)
    dtbB = sb([128, 16], F32)
    aB = sb([128, 16], F32)
    dskB = sb([128, 16], F32)
    kgB = sb([128, 64], F32)
    kbB = sb([128, 64], F32)
    pow2c = sb([128, NIT + 2], F32)
    negthr = sb([128, 1], F32)
    negone = sb([128, 1], F32)
    fence_t = sb([128, 1], F32)
    ARENA = cur[0]
    ARENA_SZ = base + BIG - ARENA
    assert ARENA_SZ >= 62 * 1024, ARENA_SZ
    acur = [ARENA]

    def arena_reset():
        acur[0] = ARENA
        mset("pool", fence_t[:], 0.0, ["arena"])

    def ab(shape, dt):
        t = alloc_at(acur[0], shape, dt)
        acur[0] += (nbytes(shape, dt) + 31) // 32 * 32
        assert acur[0] <= base + BIG, (acur[0] - ARENA)
        return t

    mset("pool", identf[:], 1.0, ["identf"])
    S.op("pool", lambda e: e.affine_select(out=identf[:], in_=identf[:], pattern=[[-1, 128]], compare_op=ALU.is_equal, fill=0.0, base=0, channel_multiplier=1), ["identf"], ["identf"])
    cp("dve", ident[:], identf[:], ["identf"], ["ident"])
    mset("pool", triu[:], 1.0, ["triu"])
    S.op("pool", lambda e: e.affine_select(out=triu[:], in_=triu[:], pattern=[[1, 128]], compare_op=ALU.is_ge, fill=0.0, base=0, channel_multiplier=-1), ["triu"], ["triu"])
    mset("pool", strict[:], 1.0, ["strict"])
    S.op("pool", lambda e: e.affine_select(out=strict[:], in_=strict[:], pattern=[[-1, 128]], compare_op=ALU.is_gt, fill=0.0, base=0, channel_multiplier=1), ["strict"], ["strict"])
    mset("pool", negtri[:], 0.0, ["negtri"])
    S.op("pool", lambda e: e.affine_select(out=negtri[:], in_=negtri[:], pattern=[[-1, 128]], compare_op=ALU.is_ge, fill=NEG, base=0, channel_multiplier=1), ["negtri"], ["negtri"])
    mset("pool", onesf[:], 1.0, ["onesf"])
    mset("pool", ones_bf[:], 1.0, ["ones_bf"])
    mset("pool", V[:, :, 64:128], 1.0, ["Vones"])
    mset("pool", V[:, :, 256:320], 1.0, ["Vones"])
    mset("pool", shup[:], 0.0, ["shup"])
    cp("dve", shup[0:64, 64:128], identf[0:64, 0:64], ["identf", "shup"], ["shup"])
    mset("pool", negthr[:], -1.0e29, ["negthr"])
    mset("pool", negone[:], -1.0, ["negone"])
    for i in range(NIT + 2):
        mset("pool", pow2c[:, i:i + 1], 2.0 ** (-i), ["pow2c"])
    for (t, src, n) in ((dtbB, dt_bias, 16), (aB, a_log, 16), (dskB, d_skip, 16), (kgB, idx_k_norm_g, 64), (kbB, idx_k_norm_b, 64)):
        dma(t[:], src.broadcast_to([128, n]), [], ["allc"])
    act(aB[:], aB[:], AF.Exp, ["allc"], ["allc"])
    ts("dve", aB[:], aB[:], -1.0, 0.0, ALU.mult, ALU.add, ["allc"], ["allc"])

    arena_reset()
    ldtmp = ab([128, 128], F32)
    stg32 = [ab([128, 4096], F32) for _ in range(2)]
    stg16 = [ab([128, 4096], BF16) for _ in range(2)]

    def load_T(dst, src_rows, R):
        dma(ldtmp[0:R, :], src_rows, [], [("arena", "ldtmp")])
        tr(banks[0][:, 0:R], ldtmp[0:R, :], identf[0:R, 0:R], [("arena", "ldtmp"), "identf"], [BK[0]])
        cp("dve", dst, banks[0][:, 0:R], [BK[0]], ["allc"])

    load_T(cw[:, 0:48], ssd_conv_w.rearrange("k (c p) -> (k c) p", p=128), 48)
    load_T(cb[:, 0:12], ssd_conv_b.rearrange("o (c p) -> (o c) p", p=128), 12)
    for k in range(3):
        load_T(cwf[:, 44 * k:44 * k + 44], ffn_conv_w[k:k + 1, :].rearrange("o (c p) -> (o c) p", p=128), 44)
    load_T(cbf[:, 0:44], ffn_conv_b.rearrange("o (c p) -> (o c) p", p=128), 44)
    ts("dve", cw[:], cw[:], 0.5, 0.0, ALU.mult, ALU.add, ["allc"], ["allc"])
    ts("dve", cb[:], cb[:], 0.5, 0.0, ALU.mult, ALU.add, ["allc"], ["allc"])
    for k in range(3):
        ts("dve", cwf[:, 44 * k:44 * k + 22], cwf[:, 44 * k:44 * k + 22], 0.5, 0.0, ALU.mult, ALU.add, ["allc"], ["allc"])
    ts("dve", cbf[:, 0:22], cbf[:, 0:22], 0.5, 0.0, ALU.mult, ALU.add, ["allc"], ["allc"])

    phase(1)
    w_in_v = w_in.rearrange("(kc p) n -> p kc n", p=128)
    w_up_v = w_up.rearrange("(kc p) n -> p kc n", p=128)
    w_out_s = w_out[0:1024, :].rearrange("(fc p) n -> p fc n", p=128)
    w_out_a = w_out[1024:2048, :].rearrange("(h p) n -> p h n", p=64)
    w_down_v = w_down.rearrange("(fc p) n -> p fc n", p=128)

    blocks = []

    def v3(s, a, b):
        return s[:, 0:a * b].rearrange("p (a b) -> p a b", a=a)

    def v4(s):
        return s[:, 0:4096].rearrange("p (c kc m) -> p c kc m", c=4, kc=8)

    def add_tm(cols, scale):
        n = sum(c[1] for c in cols)
        loads = []
        o = 0
        for (c0, cn) in cols:
            loads.append((lambda s, o=o, cn=cn, n=n: v3(s, 8, n)[:, :, o:o + cn], w_in_v[:, :, c0:c0 + cn]))
            o += cn
        blocks.append(dict(np_=128, n=8 * n, scale=scale, loads=loads))

    def add_fm(srcv, chunks, scale):
        loads = []
        for ci, pieces in enumerate(chunks):
            m0 = 0
            for (c0, cn) in pieces:
                loads.append((lambda s, ci=ci, m0=m0, cn=cn: v4(s)[:, ci, :, m0:m0 + cn], srcv[:, :, c0:c0 + cn]))
                m0 += cn
        blocks.append(dict(np_=128, n=len(chunks) * 1024, scale=scale, loads=loads))

    WB_MISC = len(blocks)
    add_tm([(3856, 256), (4624, 64), (2560, 16), (4688, 8)], 1.0)
    XS0 = 1024
    add_fm(w_in_v, [[(XS0 + 128 * i, 128)] for i in range(0, 4)], 1.0)
    add_fm(w_in_v, [[(XS0 + 128 * i, 128)] for i in range(4, 8)], 1.0)
    add_fm(w_in_v, [[(2048, 128)], [(2176, 128)], [(2304, 128)], [(2432, 128)]], 1.0)
    QPAIRS = [(0, 4), (1, 5), (2, 6), (3, 7), (8, 12), (9, 13), (10, 14), (11, 15)]
    add_fm(w_in_v, [[(2576 + 64 * a, 64), (2576 + 64 * b, 64)] for (a, b) in QPAIRS[0:4]], 0.125)
    add_fm(w_in_v, [[(2576 + 64 * a, 64), (2576 + 64 * b, 64)] for (a, b) in QPAIRS[4:8]], 0.125)
    add_fm(w_in_v, [[(3600, 128)], [(3728, 128)], [(4112, 128)], [(4240, 128)]], 1.0)
    add_fm(w_in_v, [[(4368, 128)], [(4496, 128)]], 1.0)
    WB_FM = 1
    WB_TM = len(blocks)
    add_tm([(0, 512)], 0.5)
    add_tm([(512, 512)], 0.5)
    WB_OUT = len(blocks)
    for c in range(2):
        blocks.append(dict(np_=128, n=4096, scale=1.0, loads=[(lambda s: v3(s, 8, 512), w_out_s[:, :, c * 512:(c + 1) * 512])]))
        cols = slice(c * 512, (c + 1) * 512)
        blocks.append(dict(np_=128, n=4096, scale=1.0, loads=[
            (lambda s: v3(s, 8, 512)[0:64, 0:4, :], w_out_a[:, 0:4, cols]),
            (lambda s: v3(s, 8, 512)[0:64, 4:8, :], w_out_a[:, 8:12, cols]),
            (lambda s: v3(s, 8, 512)[64:128, 0:4, :], w_out_a[:, 4:8, cols]),
            (lambda s: v3(s, 8, 512)[64:128, 4:8, :], w_out_a[:, 12:16, cols])]))
    WB_UP = len(blocks)
    for b_ in range(11):
        add_fm(w_up_v, [[(256 * b_, 128)], [(256 * b_ + 128, 128)], [(2816 + 256 * b_, 128)], [(2816 + 256 * b_ + 128, 128)]], 1.0)
    WB_DN = len(blocks)
    FCG = [(0, 8), (8, 16), (16, 22)]
    for c in range(2):
        for (f0, f1) in FCG:
            nf = f1 - f0
            blocks.append(dict(np_=128, n=nf * 512, scale=1.0,
                               loads=[(lambda s, nf=nf: v3(s, nf, 512), w_down_v[:, f0:f1, c * 512:(c + 1) * 512])]))
    assert len(blocks) == NWB, len(blocks)

    cast_engs = ["dve", "act", "dve"]
    for k, blk in enumerate(blocks):
        s32 = stg32[k % 2]
        s16 = stg16[k % 2]
        np_, n = blk["np_"], blk["n"]
        for (dstfn, src) in blk["loads"]:
            dma(dstfn(s32), src, [], [("arena", "s32", k % 2)])
        eng = cast_engs[k % 3]
        sc = blk["scale"]
        if eng == "act":
            act(s16[0:np_, 0:n], s32[0:np_, 0:n], AF.Copy, [("arena", "s32", k % 2)], [("arena", "s16", k % 2)], scale=sc)
        else:
            ts(eng, s16[0:np_, 0:n], s32[0:np_, 0:n], sc, 0.0, ALU.mult, ALU.add, [("arena", "s32", k % 2)], [("arena", "s16", k % 2)])
        dma(wsc[k, 0:np_, 0:n], s16[0:np_, 0:n], [("arena", "s16", k % 2)], [("wsc", k)])

    wseq = [0]
    wuse = [0]

    def wload_upto(g):
        while wseq[0] <= g:
            q = wseq[0]
            k = q % NWB
            blk = blocks[k]
            np_, n = blk["np_"], blk["n"]
            dma(Wr[q % 2][0:np_, 0:n], wsc[k, 0:np_, 0:n], [("wsc", k)], [("W", q % 2)])
            wseq[0] += 1

    def wnext(expect_k):
        g = wuse[0]
        assert g % NWB == expect_k, (g % NWB, expect_k)
        wload_upto(g + 1)
        wuse[0] += 1
        return Wr[g % 2], ("W", g % 2)

    total_q = NB * NQ
    wload_upto(0)

    bank_rr = [0]

    def nb(lo=0, hi=8):
        k = lo + bank_rr[0] % (hi - lo)
        bank_rr[0] += 1
        return k

    for b in range(NB):
        mset("pool", halo_s[:], 0.0, ["halo_s"])
        mset("pool", halo_f[:], 0.0, ["halo_f"])
        mset("pool", Sst[:], 0.0, ["Sst"])
        mset("pool", Sb[:], 0.0, ["Sb"])
        for j in range(NQ):
            t0 = 512 * j
            phase(2)
            arena_reset()
            xsT = ab([128, 8, 512], F32)
            normgB = ab([128, 1024], F32)
            raw = [ab([128, 515], F32) for _ in range(2)]
            cacc = [ab([128, 512], F32) for _ in range(2)]
            ctmp = [ab([128, 512], F32) for _ in range(2)]
            LH8 = ab([128, 8, 128], F32)
            E8 = ab([128, 8, 128], F32)
            CBm = ab([128, 2, 128], F32)
            MT = ab([128, 16, 128], BF16)
            xs_tm = ab([128, 1024], F32)
            B_tm = ab([128, 2, 128], BF16)
            xdt = ab([128, 1024], BF16)
            xdtd = ab([128, 1024], BF16)
            ytmp = ab([128, 1024], F32)
            ynb = ab([128, 1024], BF16)
            junk = ab([128, 512], F32)
            ssm = ab([128, 64], F32)
            AR = lambda n, *r: ("arena", n) + tuple(r)

            dma(normgB[:], ssd_norm_g.broadcast_to([128, 1024]), [], [AR("normgB")])
            if b == 0 and j == 0:
                dma(A[:], x[b, t0:t0 + 512, :].rearrange("(i p) d -> p i d", p=128), [], [("A", i) for i in range(4)])
                for i in range(4):
                    xb = xb16[i % 2]
                    cp("pool" if i % 2 == 0 else "dve", xb[:], A[:, i, :], [("A", i)], [("xb16", i % 2)])
                    bk = i % 2
                    for kc in range(8):
                        tr(bankbf[bk][:, 128 * kc:128 * kc + 128], xb[:, 128 * kc:128 * kc + 128], ident[:], [("xb16", i % 2), "ident"], [BK[bk]])
                    cp("act", xT[:, :, 128 * i:128 * i + 128], bankbf[bk][:, 0:1024].rearrange("p (kc t) -> p kc t", kc=8), [BK[bk]], [("xT", i)])
            XTK = [("xT", i) for i in range(4)]

            phase(3)
            Wt, Wk = wnext(WB_MISC)
            Wv = v3(Wt, 8, 344)
            for i in range(4):
                bk = nb(2, 8)
                for kc in range(8):
                    mm(banks[bk][:, 0:344], xT[:, kc, 128 * i:128 * i + 128], Wv[:, kc, :], kc == 0, kc == 7, [("xT", i), Wk], [BK[bk]])
                cp("act", V[:, 4 * j + i, 0:64], banks[bk][:, 0:64], [BK[bk]], [("V", j)])
                cp("act", V[:, 4 * j + i, 128:256], banks[bk][:, 64:192], [BK[bk]], [("V", j)])
                cp("act", V[:, 4 * j + i, 320:384], banks[bk][:, 192:256], [BK[bk]], [("V", j)])
                cp("dve", kif[:, i, :], banks[bk][:, 256:320], [BK[bk]], ["kif"])
                cp("dve", dtw[:, i, :], banks[bk][:, 320:344], [BK[bk]], ["dtw"])
            tt("dve", dtx[:], dtw[:, :, 0:16], dtbB[:].unsqueeze(1).broadcast_to([128, 4, 16]), ALU.add, ["dtw", "allc"], ["dtx"])
            stt("dve", dtl[:], dtx[:], -1.0, dtx[:], ALU.mult, ALU.max, ["dtx"], ["dtl"])
            act(dtl[:], dtl[:], AF.Exp, ["dtl"], ["dtl"], scale=-1.0)
            act(dtl[:], dtl[:], AF.Ln, ["dtl"], ["dtl"], bias=1.0)
            stt("dve", dt_t[:], dtx[:], 0.0, dtl[:], ALU.max, ALU.add, ["dtx", "dtl"], ["dt_t"])
            tt("dve", dtA_t[:], dt_t[:], aB[:].unsqueeze(1).broadcast_to([128, 4, 16]), ALU.mult, ["dt_t", "allc"], ["dtA_t"])
            stt("dve", absw[:], dtw[:, :, 16:24], -1.0, dtw[:, :, 16:24], ALU.mult, ALU.max, ["dtw"], ["absw"])
            ts("dve", sgnw[:], dtw[:, :, 16:24], 0.0, 2.0, ALU.is_ge, ALU.mult, ["dtw"], ["sgnw"])
            ts("dve", sgnw[:], sgnw[:], -1.0, 0.0, ALU.add, ALU.add, ["sgnw"], ["sgnw"])
            red("dve", sm4[:, :, 0], kif[:], ALU.add, ["kif"], ["sm4"])
            ts("dve", sm4[:, :, 1], sm4[:, :, 0], 1.0 / 64, 0.0, ALU.mult, ALU.add, ["sm4"], ["sm4"])
            tt("dve", kixc[:], kif[:], sm4[:, :, 1:2].broadcast_to([128, 4, 64]), ALU.subtract, ["kif", "sm4"], ["kixc"])
            tt("dve", kif[:], kixc[:], kixc[:], ALU.mult, ["kixc"], ["kif"])
            red("dve", sm4[:, :, 2], kif[:], ALU.add, ["kif"], ["sm4"])
            ts("dve", sm4[:, :, 3], sm4[:, :, 2], 1.0 / 64, 1e-5, ALU.mult, ALU.add, ["sm4"], ["sm4"])
            act(sm4[:, :, 3], sm4[:, :, 3], AF.Ln, ["sm4"], ["sm4"])
            act(sm4[:, :, 4], sm4[:, :, 3], AF.Exp, ["sm4"], ["sm4"], scale=-0.5)
            tt("dve", kixc[:], kixc[:], sm4[:, :, 4:5].broadcast_to([128, 4, 64]), ALU.mult, ["kixc", "sm4"], ["kixc"])
            tt("dve", kixc[:], kixc[:], kgB[:].unsqueeze(1).broadcast_to([128, 4, 64]), ALU.mult, ["kixc", "allc"], ["kixc"])
            for hf in range(2):
                tt("dve", kidup[:, :, 64 * hf:64 * hf + 64], kixc[:], kbB[:].unsqueeze(1).broadcast_to([128, 4, 64]), ALU.add, ["kixc", "allc"], ["kidup"])
            def fm_chunk(Wt, Wk, ci):
                bk = nb(2, 8)
                Wv = v4(Wt)
                for kc in range(8):
                    mm(banks[bk][:, :], Wv[:, ci, kc, :], xT[:, kc, :], kc == 0, kc == 7, XTK + [Wk], [BK[bk]])
                return bk

            pend = []

            def conv_silu(bk, xi, dst, dkey):
                r = raw[xi % 2]
                rk = AR("raw", xi % 2)
                cp("act", r[:, 3:515], banks[bk][:, :], [BK[bk]], [rk])
                cp("pool", r[:, 0:3], halo_s[:, xi, :], ["halo_s"], [rk])
                cp("pool", halo_s[:, xi, :], r[:, 512:515], [rk], ["halo_s"])
                ca = cacc[xi % 2]
                ck = AR("cacc", xi % 2)
                act(ca[:], banks[bk][:, :], AF.Identity, [BK[bk], "allc"], [ck], scale=cw[:, 36 + xi:36 + xi + 1], bias=cb[:, xi:xi + 1])
                for k in range(3):
                    stt("dve", ca[:], r[:, k:k + 512], cw[:, 12 * k + xi:12 * k + xi + 1], ca[:], ALU.mult, ALU.add, [rk, ck, "allc"], [ck])
                ct = ctmp[xi % 2]
                tk = AR("ctmp", xi % 2)

                def fin(ct=ct, ca=ca, ck=ck, tk=tk, dst=dst, dkey=dkey):
                    act(ct[:], ca[:], AF.Tanh, [ck], [tk])
                    stt("dve", dst, ct[:], 1.0, ca[:], ALU.add, ALU.mult, [tk, ck], [dkey])
                if pend:
                    pend.pop(0)()
                pend.append(fin)

            for blk_i in range(2):
                Wt, Wk = wnext(WB_FM + blk_i)
                for ci in range(4):
                    xi = 4 * blk_i + ci
                    bk = fm_chunk(Wt, Wk, ci)
                    conv_silu(bk, xi, xsT[:, xi, :], AR("xsT", xi))
            Wt, Wk = wnext(WB_FM + 2)
            for ci in range(4):
                xi = 8 + ci
                bk = fm_chunk(Wt, Wk, ci)
                if ci < 2:
                    conv_silu(bk, xi, BT[:, ci, :], ("BT", ci))
                else:
                    conv_silu(bk, xi, CT[:, ci - 2, :], ("CT", ci - 2))
            while pend:
                pend.pop(0)()
            for blk_i in range(2):
                Wt, Wk = wnext(WB_FM + 3 + blk_i)
                for ci in range(4):
                    bk = fm_chunk(Wt, Wk, ci)
                    qc = 4 * blk_i + ci
                    cp("act" if ci % 2 == 0 else "dve", qT[:, qc, :], banks[bk][:, :], [BK[bk]], [("qT", qc)])
            Wt, Wk = wnext(WB_FM + 5)
            for ci in range(4):
                bk = fm_chunk(Wt, Wk, ci)
                if ci < 2:
                    cp("act", kT[:, ci, t0:t0 + 512], banks[bk][:, :], [BK[bk]], [("kT", j)])
                else:
                    cp("dve", qiT[:, ci - 2, :], banks[bk][:, :], [BK[bk]], [("qiT", ci - 2)])
            Wt, Wk = wnext(WB_FM + 6)
            for ci in range(2):
                bk = fm_chunk(Wt, Wk, ci)
                cp("act", qiT[:, 2 + ci, :], banks[bk][:, :], [BK[bk]], [("qiT", 2 + ci)])

            for c in range(2):
                Wt, Wk = wnext(WB_TM + c)
                Wv = v3(Wt, 8, 512)
                for i in range(4):
                    bk = nb(2, 8)
                    for kc in range(8):
                        mm(banks[bk][:, :], xT[:, kc, 128 * i:128 * i + 128], Wv[:, kc, :], kc == 0, kc == 7, [("xT", i), Wk], [BK[bk]])
                    dst = A[:, i, 512 * c:512 * c + 512]
                    act(dst, banks[bk][:, :], AF.Tanh, [BK[bk]], [("A", i)])
                    stt("dve", dst, dst, 1.0, banks[bk][:, :], ALU.add, ALU.mult, [("A", i), BK[bk]], [("A", i)])
            bk = nb(0, 2)
            for i in range(4):
                tr(bankbf[bk][:, 128 * i:128 * i + 128], kidup[:, i, :], ident[:], ["kidup", "ident"], [BK[bk]])
            cp("act", kiT[:, t0:t0 + 512], bankbf[bk][:, 0:512], [BK[bk]], [("kiT", j)])

            phase(4)
            for c in range(4):
                cs = slice(128 * c, 128 * c + 128)
                dtA = dtA_t[:, c, :]
                dtc = dt_t[:, c, :]
                mm(banks[3][:, 0:16], triu[:], dtA, True, True, ["triu", "dtA_t"], [BK[3]])
                mm(banks[3][:, 16:32], onesf[:], dtA, True, True, ["onesf", "dtA_t"], [BK[3]])
                act(ssm[:, 0:32], banks[3][:, 0:32], AF.Exp, [BK[3]], [AR("ssm")])
                eacs = ssm[:, 0:16]
                cdB = ssm[:, 16:32]
                for g in range(2):
                    mm(banks[2][:, 128 * g:128 * g + 128], BT[:, g, cs], CT[:, g, cs], True, True, [("BT", g), ("CT", g)], [BK[2]])
                tt("dve", CBm[:], banks[2][:, 0:256].rearrange("p (g l) -> p g l", g=2), triu[:].unsqueeze(1).broadcast_to([128, 2, 128]), ALU.mult, [BK[2], "triu"], [AR("CBm")])
                for hh in range(2):
                    tt("dve", LH8[:], strict[:].unsqueeze(1).broadcast_to([128, 8, 128]),
                       dtA[:, 8 * hh:8 * hh + 8].unsqueeze(2).broadcast_to([128, 8, 128]), ALU.mult, ["strict", "dtA_t"], [AR("LH8")])
                    for h8 in range(8):
                        bk = h8 // 4
                        mm(banks[bk][:, 128 * (h8 % 4):128 * (h8 % 4) + 128], LH8[:, h8, :], triu[:], True, True, [AR("LH8"), "triu"], [BK[bk]])
                    for bk in range(2):
                        act(E8[:, 4 * bk:4 * bk + 4, :], banks[bk][:, :].rearrange("p (h l) -> p h l", h=4), AF.Exp, [BK[bk]], [AR("E8")])
                    cp("dve", ssm[:, 32 + 8 * hh:32 + 8 * hh + 8], E8[:, :, 127], [AR("E8")], [AR("ssm")])
                    tt("dve", MT[:, 8 * hh:8 * hh + 8, :], E8[:], CBm[:, hh:hh + 1, :].broadcast_to([128, 8, 128]), ALU.mult, [AR("E8"), AR("CBm")], [AR("MT")])
                dte = ssm[:, 32:48]
                for fc in range(8):
                    bk = 4 + fc // 4
                    tr(banks[bk][:, 128 * (fc % 4):128 * (fc % 4) + 128], xsT[:, fc, cs], identf[:], [AR("xsT", fc), "identf"], [BK[bk]])
                for bk in (4, 5):
                    cp("act", xs_tm[:, 512 * (bk - 4):512 * (bk - 4) + 512], banks[bk][:, :], [BK[bk]], [AR("xs_tm")])
                for g in range(2):
                    tr(bankbf[3][:, 128 * g:128 * g + 128], BT[:, g, cs], ident[:], [("BT", g), "ident"], [BK[3]])
                cp("act", B_tm[:], bankbf[3][:, 0:256].rearrange("p (g n) -> p g n", g=2), [BK[3]], [AR("B_tm")])
                xs3 = xs_tm[:].rearrange("p (h d) -> p h d", h=16)
                tt("dve", xdt[:].rearrange("p (h d) -> p h d", h=16), xs3, dtc.unsqueeze(2).broadcast_to([128, 16, 64]), ALU.mult, [AR("xs_tm"), "dt_t"], [AR("xdt")])
                tt("dve", ssm[:, 48:64], dtc, dte, ALU.mult, ["dt_t", AR("ssm")], [AR("ssm")])
                tt("dve", xdtd[:].rearrange("p (h d) -> p h d", h=16), xs3, ssm[:, 48:64].unsqueeze(2).broadcast_to([128, 16, 64]), ALU.mult, [AR("xs_tm"), AR("ssm")], [AR("xdtd")])
                for h in range(16):
                    bk = 4 + h // 8
                    mm(banks[bk][:, 64 * (h % 8):64 * (h % 8) + 64], MT[:, h, :], xdt[:, 64 * h:64 * h + 64], True, True, [AR("MT"), AR("xdt")], [BK[bk]])
                for g in range(2):
                    mm(banks[6 + g][:, :], CT[:, g, cs], Sb[:, 512 * g:512 * g + 512], True, True, [("CT", g), "Sb"], [BK[6 + g]])
                for g in range(2):
                    mm(banks[g][:, :], B_tm[:, g, :], xdtd[:, 512 * g:512 * g + 512], True, True, [AR("B_tm"), AR("xdtd")], [BK[g]])
                S3 = Sst[:].rearrange("p (h d) -> p h d", h=16)
                tt("dve", S3, S3, cdB.unsqueeze(2).broadcast_to([128, 16, 64]), ALU.mult, ["Sst", AR("ssm")], ["Sst"])
                for g in range(2):
                    tt("dve", Sst[:, 512 * g:512 * g + 512], Sst[:, 512 * g:512 * g + 512], banks[g][:, :], ALU.add, ["Sst", BK[g]], ["Sst"])
                cp("act", Sb[:], Sst[:], ["Sst"], ["Sb"])
                for g in range(2):
                    ysl = ytmp[:, 512 * g:512 * g + 512]
                    tt("dve", ysl.rearrange("p (h d) -> p h d", h=8), banks[6 + g][:, :].rearrange("p (h d) -> p h d", h=8),
                       eacs[:, 8 * g:8 * g + 8].unsqueeze(2).broadcast_to([128, 8, 64]), ALU.mult, [BK[6 + g], AR("ssm")], [AR("ytmp")])
                    tt("dve", ysl, ysl, banks[4 + g][:, :], ALU.add, [AR("ytmp"), BK[4 + g]], [AR("ytmp")])
                tt("dve", xs3, xs3, dskB[:].unsqueeze(2).broadcast_to([128, 16, 64]), ALU.mult, [AR("xs_tm"), "allc"], [AR("xs_tm")])
                tt("dve", ytmp[:], ytmp[:], xs_tm[:], ALU.add, [AR("ytmp"), AR("xs_tm")], [AR("ytmp")])
                tt("dve", ytmp[:], ytmp[:], A[:, c, :], ALU.mult, [AR("ytmp"), ("A", c)], [AR("ytmp")])
                for g in range(2):
                    act(junk[:], ytmp[:, 512 * g:512 * g + 512], AF.Square, [AR("ytmp")], [AR("junk"), AR("ssm")], accum_out=ssm[:, 60 + g:61 + g])
                ts("dve", ssm[:, 60:62], ssm[:, 60:62], 1.0 / 512, 1e-5, ALU.mult, ALU.add, [AR("ssm")], [AR("ssm")])
                act(ssm[:, 60:62], ssm[:, 60:62], AF.Ln, [AR("ssm")], [AR("ssm")])
                act(ssm[:, 62:64], ssm[:, 60:62], AF.Exp, [AR("ssm")], [AR("ssm")], scale=-0.5)
                for g in range(2):
                    stt("dve", ynb[:, 512 * g:512 * g + 512], ytmp[:, 512 * g:512 * g + 512], ssm[:, 62 + g:63 + g], normgB[:, 512 * g:512 * g + 512],
                        ALU.mult, ALU.mult, [AR("ytmp"), AR("ssm"), AR("normgB")], [AR("ynb")])
                for fc in range(8):
                    tr(bankbf[2][:, 128 * fc:128 * fc + 128], ynb[:, 128 * fc:128 * fc + 128], ident[:], [AR("ynb"), "ident"], [BK[2]])
                cp("act", yssdT[:, :, cs], bankbf[2][:, 0:1024].rearrange("p (f t) -> p f t", f=8), [BK[2]], [("yssdT", c)])
            if debug:
                dma(dbg["yssd"][b, j], yssdT[:], [("yssdT", c) for c in range(4)], ["dbg_yssd"])

            phase(5)
            arena_reset()
            maskT = ab([128, 32, 512], BF16)
            score = ab([128, 4096], F32)
            rbuf = [ab([128, 512], BF16) for _ in range(4)]
            Dg = ab([128, 8, 128], BF16)
            mrow = [ab([128, 512], BF16) for _ in range(2)]
            qiTz = ab([128, 8, 512], BF16)
            Wcol = ab([128, NIT + 2], F32)
            cnt = ab([128, NIT + 2], F32)
            cnta = ab([128, NIT + 2], F32)
            bsm = ab([128, 8], F32)
            jnk = ab([128, 2], F32)
            NKB = 4 * j + 4
            scoreb = [score, A[:].rearrange("p a b -> p (a b)")]
            skeys = [[AR("score")], [("A", i_) for i_ in range(4)]]
            mset("pool", qiTz[:], 0.0, [AR("qiTz")])
            for h in range(8):
                hp = 64 * (h % 2)
                cp("act" if h % 2 else "pool", qiTz[hp:hp + 64, h, :], qiT[hp:hp + 64, h // 2, :], [("qiT", h // 2), AR("qiTz")], [AR("qiTz", h)])
            KIK = [("kiT", jj) for jj in range(j + 1)]

            def relu_gen(i):
                n_keys = 128 * (4 * j + i + 1)
                nkt = (n_keys + 511) // 512
                sc = scoreb[i % 2]
                sk = skeys[i % 2]
                tt("dve", Dg[:], ident[:].unsqueeze(1).broadcast_to([128, 8, 128]), sgnw[:, i, :].unsqueeze(2).broadcast_to([128, 8, 128]), ALU.mult,
                   ["ident", "sgnw"], [AR("Dg")])
                items = [(kt, h) for kt in range(nkt) for h in range(8)]

                def ifront(q):
                    kt, h = items[q]
                    w = min(512, n_keys - 512 * kt)
                    bk = q % 4
                    mm(banks[bk][:, 0:w], qiTz[:, h, 128 * i:128 * i + 128], kiT[:, 512 * kt:512 * kt + w], True, True,
                       [AR("qiTz", h)] + KIK, [BK[bk]])
                    if q % 4 != 3:
                        act(rbuf[q % 4][:, 0:w], banks[bk][:, 0:w], AF.Relu, [BK[bk], "absw"], [AR("rbuf", q % 4)], scale=absw[:, i, h:h + 1])
                    else:
                        ts("dve", rbuf[q % 4][:, 0:w], banks[bk][:, 0:w], absw[:, i, h:h + 1], 0.0, ALU.mult, ALU.max, [BK[bk], "absw"], [AR("rbuf", q % 4)])

                def iback(q):
                    kt, h = items[q]
                    w = min(512, n_keys - 512 * kt)
                    ab_ = 4 + kt % 2
                    mm(banks[ab_][:, 0:w], Dg[:, h, :], rbuf[q % 4][:, 0:w], h == 0, h == 7, [AR("Dg"), AR("rbuf", q % 4)], [BK[ab_]])
                    if h == 7:
                        cp("dve", sc[:, 512 * kt:512 * kt + w], banks[ab_][:, 0:w], [BK[ab_]], sk)

                ILA = 2
                for q in range(len(items) + ILA):
                    if q < len(items):
                        ifront(q)
                    if q - ILA >= 0:
                        iback(q - ILA)
                    yield

            def bisect_gen(i, out):
                n_kb = 4 * j + i + 1
                n_keys = 128 * n_kb
                sc = scoreb[i % 2]
                sk = skeys[i % 2]
                out["thr"] = negthr[:, 0:1]
                out["key"] = "negthr"
                if n_kb >= 3:
                    red("dve", bsm[:, 0:1], sc[:, 0:n_keys], ALU.max, sk, [AR("bsm")], absval=True)
                dsl = sc[:, 128 * (n_kb - 1):128 * n_kb]
                tt("dve", dsl, dsl, negtri[:], ALU.add, sk + ["negtri"], sk)
                if n_kb < 3:
                    return
                ts("dve", Wcol[:], pow2c[:], bsm[:, 0:1], 0.0, ALU.mult, ALU.add, ["pow2c", AR("bsm")], [AR("Wcol")])
                mset("dve", bsm[:, 1:2], 0.0, [AR("bsm")])
                mid = bsm[:, 1:2]
                n_dve = (n_keys * 40 // 100) // 64 * 64
                n_act = n_keys - n_dve
                cthr = 511.0 - n_act
                yield
                for it in range(NIT):
                    ts("dve", jnk[:, 0:1].broadcast_to([128, n_dve]), sc[:, 0:n_dve], mid, 0.0, ALU.is_ge, ALU.add, sk + [AR("bsm")], [AR("jnk", 0), AR("cnt")], accum=cnt[:, it:it + 1])
                    act(jnk[:, 1:2].broadcast_to([128, n_act]), sc[:, n_dve:n_keys], AF.Sign, sk + [AR("bsm")], [AR("jnk", 1), AR("cnta")], scale=-1.0, bias=mid, accum_out=cnta[:, it:it + 1])
                    stt("dve", bsm[:, 6:7], cnt[:, it:it + 1], 2.0, cnta[:, it:it + 1], ALU.mult, ALU.subtract, [AR("cnt"), AR("cnta")], [AR("bsm")])
                    if it < NIT - 1:
                        ts("dve", bsm[:, 2:3], bsm[:, 6:7], cthr, Wcol[:, it:it + 1], ALU.is_ge, ALU.mult, [AR("bsm"), AR("Wcol")], [AR("bsm")])
                        stt("dve", mid, bsm[:, 2:3], Wcol[:, it + 1:it + 2], mid, ALU.subtract, ALU.add, [AR("bsm"), AR("Wcol")], [AR("bsm")])
                    else:
                        ts("dve", bsm[:, 2:3], bsm[:, 6:7], cthr, -1.0, ALU.is_ge, ALU.add, [AR("bsm")], [AR("bsm")])
                        stt("dve", bsm[:, 3:4], bsm[:, 2:3], Wcol[:, it:it + 1], mid, ALU.mult, ALU.add, [AR("bsm"), AR("Wcol")], [AR("bsm")])
                    yield
                out["thr"] = bsm[:, 3:4]
                out["key"] = AR("bsm")
                if debug:
                    cp("dve", bsm[:, 4:5], bsm[:, 3:4], [AR("bsm")], [AR("bsm")])
                    cp("dve", bsm[:, 5:6], bsm[:, 6:7], [AR("bsm")], [AR("bsm")])
                    dma(dbg["thr"][b, j, i], bsm[:, 4:6], [AR("bsm")], ["dbg_thr"])

            def mask_gen(i, thr, thr_key):
                n_kb = 4 * j + i + 1
                n_keys = 128 * n_kb
                nkt = (n_keys + 511) // 512
                sc = scoreb[i % 2]
                sk = skeys[i % 2]
                for kt in range(nkt):
                    w = min(512, n_keys - 512 * kt)
                    nblk = w // 128
                    mr = mrow[kt % 2]
                    ts("dve", mr[:, 0:w], sc[:, 512 * kt:512 * kt + w], thr, 0.0, ALU.is_ge, ALU.add, sk + [thr_key], [AR("mrow", kt % 2)])
                    bk = 6 + kt % 2
                    for q in range(nblk):
                        tr(bankbf[bk][:, 128 * q:128 * q + 128], mr[:, 128 * q:128 * q + 128], ident[:], [AR("mrow", kt % 2), "ident"], [BK[bk]])
                    cp("act", maskT[:, 4 * kt:4 * kt + nblk, 128 * i:128 * i + 128], bankbf[bk][:, 0:128 * nblk].rearrange("p (q t) -> p q t", q=nblk),
                       [BK[bk]], [AR("maskT", i)])
                if n_kb < NKB:
                    mset("pool", maskT[:, n_kb:NKB, 128 * i:128 * i + 128], 0.0, [AR("maskT", i)])

            for _ in relu_gen(0):
                pass
            for i in range(4):
                res_ = {}
                bg = bisect_gen(i, res_)
                rg = relu_gen(i + 1) if i < 3 else iter(())
                n_r = (8 * ((128 * (4 * j + i + 2) + 511) // 512) + 2) if i < 3 else 0
                per = max(1, -(-n_r // (NIT + 1)))
                b_done = False
                r_done = (i == 3)
                while not (b_done and r_done):
                    if not b_done:
                        try:
                            next(bg)
                        except StopIteration:
                            b_done = True
                    for _ in range(per if not b_done else 10 ** 6):
                        if r_done:
                            break
                        try:
                            next(rg)
                        except StopIteration:
                            r_done = True
                mask_gen(i, res_["thr"], res_["key"])
            dma(A[:], x[b, t0:t0 + 512, :].rearrange("(i p) d -> p i d", p=128), [], [("A", i) for i in range(4)])
            arena_reset()
            maskT = ab([128, 32, 512], BF16)
            NEP = 4
            Eall = ab([128, 2 * NEP, 512], BF16)
            accS = [ab([128, 512], F32) for _ in range(2)]
            rq = ab([128, 8], F32)
            qTz = ab([128, 16, 512], BF16)
            mset("pool", qTz[:], 0.0, [AR("qTz")])
            for qc_ in range(8):
                for sd in range(2):
                    cp("act" if sd else "pool", qTz[64 * sd:64 * sd + 64, 2 * qc_ + sd, :], qT[64 * sd:64 * sd + 64, qc_, :], [("qT", qc_), AR("qTz")], [AR("qTz", 2 * qc_ + sd)])
            MK = [AR("maskT", i) for i in range(4)]
            KTK = [("kT", jj) for jj in range(j + 1)]
            VK = [("V", jj) for jj in range(j + 1)] + ["Vones"]
            VOFF = [0, 64, 192, 256]
            items = [(qc, kb) for qc in range(8) for kb in range(NKB)]
            LA = 2
            sctr = [0]
            islot = {}

            def front(it):
                qc, kb = items[it]
                a_, b_h = QPAIRS[qc]
                slot = (a_ // 4) // 2
                sp = sctr[0] % 3
                sctr[0] += 1
                islot[it] = sp
                for side in range(2):
                    mm(banks[2 * sp + side][:, :], kT[:, slot, 128 * kb:128 * kb + 128], qTz[:, 2 * qc + side, :], True, True,
                       KTK + [AR("qTz", 2 * qc + side)], [BK[2 * sp + side]])
                ep = it % NEP
                ek = AR("E", ep)
                e2 = Eall[:, 2 * ep:2 * ep + 2, :]
                act(e2, psall[:, 1024 * sp:1024 * sp + 1024].rearrange("p (a b) -> p a b", a=2), AF.Exp, [BK[2 * sp], BK[2 * sp + 1]], [ek])
                tt("dve", e2, e2, maskT[:, kb:kb + 1, :].broadcast_to([128, 2, 512]), ALU.mult, [ek] + MK, [ek])

            def back(it):
                qc, kb = items[it]
                ep = it % NEP
                ek = AR("E", ep)
                for side in range(2):
                    h = QPAIRS[qc][side]
                    n = h // 4
                    mm(banks[6 + side][:, :], V[:, kb, VOFF[n]:VOFF[n] + 128], Eall[:, 2 * ep + side, :], kb == 0, kb == NKB - 1, VK + [ek], [BK[6 + side]])
                if kb == NKB - 1:
                    for side in range(2):
                        cp("act", accS[side][:], banks[6 + side][:, :], [BK[6 + side]], [AR("accS", side)])
                    box = {}

                    def st_den(qc=qc, box=box):
                        sp = sctr[0] % 3
                        sctr[0] += 1
                        for side in range(2):
                            selc = identf[:, 64:65] if side == 0 else identf[:, 0:1]
                            for i4 in range(4):
                                mm(banks[2 * sp][:, 4 * side + i4:4 * side + i4 + 1], accS[side][:, 128 * i4:128 * i4 + 128], selc, True, True,
                                   ["identf", AR("accS", side)], [BK[2 * sp]])
                        S.op("dve", lambda e, sp=sp: e.reciprocal(out=rq[:, 0:8], in_=banks[2 * sp][:, 0:8]), [BK[2 * sp]], [AR("rq")])

                    def st_bc(qc=qc, box=box):
                        sp = sctr[0] % 3
                        sctr[0] += 1
                        box["sp"] = sp
                        for side in range(2):
                            M = 64 if side == 0 else 128
                            for i4 in range(4):
                                mm(banks[2 * sp + side][0:M, 128 * i4:128 * i4 + 128], rq[:, 4 * side + i4:4 * side + i4 + 1].broadcast_to([128, M]), identf[:, :], True, True,
                                   ["identf", AR("rq")], [BK[2 * sp + side]])

                    def st_fin(qc=qc, box=box):
                        sp = box["sp"]
                        for side in range(2):
                            h = QPAIRS[qc][side]
                            olo = 0 if side == 0 else 64
                            tt("dve", yattT[olo:olo + 64, qc, :], accS[side][olo:olo + 64, :], banks[2 * sp + side][olo:olo + 64, :], ALU.mult,
                               [AR("accS", side), BK[2 * sp + side]], [("yattT", h)])

                    d1 = max(1, min(3, NKB - 3))
                    d2 = max(d1 + 1, min(5, NKB - 2))
                    deferred.append((it + LA + d1, st_den))
                    deferred.append((it + LA + d2, st_bc))
                    deferred.append((it + LA + d2 + 1, st_fin))

            deferred = []
            for it in range(len(items) + LA):
                if it < len(items):
                    front(it)
                if it - LA >= 0:
                    back(it - LA)
                while deferred and deferred[0][0] <= it:
                    deferred.pop(0)[1]()
            while deferred:
                deferred.pop(0)[1]()
            if debug:
                dma(dbg["yatt"][b, j], yattT[:], [("yattT", h) for h in range(16)], ["dbg_yatt"])

            phase(6)
            arena_reset()
            lnG = ab([128, 1024], F32)
            lnB = ab([128, 1024], F32)
            junk5 = ab([128, 1024], F32)
            lsm = ab([128, 4, 8], F32)
            dma(lnG[:], ln1_g.broadcast_to([128, 1024]), [], [AR("lnG")])
            dma(lnB[:], ln1_b.broadcast_to([128, 1024]), [], [AR("lnB")])

            def layer_norm_blocks(final):
                for i in range(4):
                    act(junk5[:], A[:, i, :], AF.Identity, [("A", i)], [AR("junk5"), AR("lsm")], accum_out=lsm[:, i, 0:1])
                    act(junk5[:], A[:, i, :], AF.Square, [("A", i)], [AR("junk5"), AR("lsm")], accum_out=lsm[:, i, 1:2])
                ts("dve", lsm[:, :, 2], lsm[:, :, 0], 1.0 / 1024, 0.0, ALU.mult, ALU.add, [AR("lsm")], [AR("lsm")])
                tt("dve", lsm[:, :, 3], lsm[:, :, 2], lsm[:, :, 2], ALU.mult, [AR("lsm")], [AR("lsm")])
                stt("dve", lsm[:, :, 4], lsm[:, :, 1], 1.0 / 1024, lsm[:, :, 3], ALU.mult, ALU.subtract, [AR("lsm")], [AR("lsm")])
                ts("dve", lsm[:, :, 4], lsm[:, :, 4], 1e-5, 0.0, ALU.add, ALU.add, [AR("lsm")], [AR("lsm")])
                act(lsm[:, :, 4], lsm[:, :, 4], AF.Ln, [AR("lsm")], [AR("lsm")])
                act(lsm[:, :, 5], lsm[:, :, 4], AF.Exp, [AR("lsm")], [AR("lsm")], scale=-0.5)
                stt("dve", lsm[:, :, 6], lsm[:, :, 2], -1.0, lsm[:, :, 5], ALU.mult, ALU.mult, [AR("lsm")], [AR("lsm")])
                for i in range(4):
                    act(A[:, i, :], A[:, i, :], AF.Identity, [("A", i), AR("lsm")], [("A", i)], scale=lsm[:, i, 5:6], bias=lsm[:, i, 6:7])
                    tt("pool" if i % 2 else "dve", A[:, i, :], A[:, i, :], lnG[:], ALU.mult, [("A", i), AR("lnG")], [("A", i)])
                    tt("dve", A[:, i, :], A[:, i, :], lnB[:], ALU.add, [("A", i), AR("lnB")], [("A", i)])
                    if final:
                        dma(out[b, t0 + 128 * i:t0 + 128 * i + 128, :], A[:, i, :], [("A", i)], ["out"])

            for c in range(2):
                Wt, Wk = wnext(WB_OUT + 2 * c)
                Wv = v3(Wt, 8, 512)
                for i in range(4):
                    for fc in range(8):
                        mm(banks[i][:, :], yssdT[:, fc, 128 * i:128 * i + 128], Wv[:, fc, :], fc == 0, False, [("yssdT", i), Wk], [BK[i]])
                Wt, Wk = wnext(WB_OUT + 2 * c + 1)
                Wv = v3(Wt, 8, 512)
                for i in range(4):
                    for qc in range(8):
                        mm(banks[i][:, :], yattT[:, qc, 128 * i:128 * i + 128], Wv[:, qc, :], False, qc == 7,
                           [("yattT", QPAIRS[qc][0]), ("yattT", QPAIRS[qc][1]), Wk], [BK[i]])
                for i in range(4):
                    dst = A[:, i, 512 * c:512 * c + 512]
                    stt("dve", dst, dst, ALPHA, banks[i][:, :], ALU.mult, ALU.add, [("A", i), BK[i]], [("A", i)])
            layer_norm_blocks(False)
            if debug:
                for i in range(4):
                    dma(dbg["h"][b, t0 + 128 * i:t0 + 128 * i + 128, :], A[:, i, :], [("A", i)], ["dbg_h"])
            for i in range(4):
                xb = xb16[i % 2]
                cp("pool" if i % 2 == 0 else "dve", xb[:], A[:, i, :], [("A", i)], [("xb16", i % 2)])
                bk = 4 + i % 2
                for kc in range(8):
                    tr(bankbf[bk][:, 128 * kc:128 * kc + 128], xb[:, 128 * kc:128 * kc + 128], ident[:], [("xb16", i % 2), "ident"], [BK[bk]])
                cp("act", xT[:, :, 128 * i:128 * i + 128], bankbf[bk][:, 0:1024].rearrange("p (kc t) -> p kc t", kc=8), [BK[bk]], [("xT", i)])

            phase(7)
            arena_reset()
            lnG = ab([128, 1024], F32)
            lnB = ab([128, 1024], F32)
            junk5 = ab([128, 1024], F32)
            lsm = ab([128, 4, 8], F32)
            actT = ab([128, 22, 512], BF16)
            rawf = [ab([128, 514], F32) for _ in range(2)]
            facc = [ab([128, 512], F32) for _ in range(2)]
            ftmp = [ab([128, 512], F32) for _ in range(2)]
            gbuf = [ab([128, 512], F32) for _ in range(2)]
            dma(lnG[:], ln2_g.broadcast_to([128, 1024]), [], [AR("lnG")])
            dma(lnB[:], ln2_b.broadcast_to([128, 1024]), [], [AR("lnB")])
            fcount = 0
            fpend = []
            for b_ in range(11):
                Wt, Wk = wnext(WB_UP + b_)
                for ci in range(4):
                    isup = ci >= 2
                    fchunk = 2 * b_ + (ci % 2)
                    cidx = fchunk + (22 if isup else 0)
                    bk = fm_chunk(Wt, Wk, ci)
                    r = rawf[fcount % 2]
                    rk = AR("rawf", fcount % 2)
                    cp("act", r[:, 2:514], banks[bk][:, :], [BK[bk]], [rk])
                    cp("pool", r[:, 0:2], halo_f[:, cidx, :], ["halo_f"], [rk])
                    cp("pool", halo_f[:, cidx, :], r[:, 512:514], [rk], ["halo_f"])
                    fa = facc[fcount % 2]
                    fk = AR("facc", fcount % 2)
                    act(fa[:], banks[bk][:, :], AF.Identity, [BK[bk], "allc"], [fk], scale=cwf[:, 88 + cidx:89 + cidx], bias=cbf[:, cidx:cidx + 1])
                    for k in range(2):
                        stt("dve", fa[:], r[:, k:k + 512], cwf[:, 44 * k + cidx:44 * k + cidx + 1], fa[:], ALU.mult, ALU.add, [rk, fk, "allc"], [fk])
                    if fpend:
                        fpend.pop(0)()
                    if not isup:
                        ft = ftmp[fcount % 2]

                        def ffin(ft=ft, fa=fa, fk=fk, ci=ci, fc_=fcount):
                            act(ft[:], fa[:], AF.Tanh, [fk], [AR("ftmp", fc_ % 2)])
                            stt("dve", gbuf[ci][:], ft[:], 1.0, fa[:], ALU.add, ALU.mult, [AR("ftmp", fc_ % 2), fk], [AR("gbuf", ci)])
                        fpend.append(ffin)
                    else:
                        def ufin(fa=fa, fk=fk, ci=ci, fchunk=fchunk):
                            tt("dve", actT[:, fchunk, :], fa[:], gbuf[ci - 2][:], ALU.mult, [fk, AR("gbuf", ci - 2)], [AR("actT", fchunk)])
                        fpend.append(ufin)
                    fcount += 1
            while fpend:
                fpend.pop(0)()
            nb_, nj_ = (b, j + 1) if j + 1 < NQ else (b + 1, 0)
            if nb_ < NB:
                xstage = [ab([128, 1024], F32) for _ in range(2)]
                for i in range(4):
                    xs_ = xstage[i % 2]
                    dma(xs_[:], x[nb_, 512 * nj_ + 128 * i:512 * nj_ + 128 * i + 128, :], [], [AR("xstage", i % 2)])
                    xb = xb16[i % 2]
                    cp("pool", xb[:], xs_[:], [AR("xstage", i % 2)], [("xb16", i % 2)])
                    bk = 4 + i % 2
                    for kc in range(8):
                        tr(bankbf[bk][:, 128 * kc:128 * kc + 128], xb[:, 128 * kc:128 * kc + 128], ident[:], [("xb16", i % 2), "ident"], [BK[bk]])
                    cp("act", xT[:, :, 128 * i:128 * i + 128], bankbf[bk][:, 0:1024].rearrange("p (kc t) -> p kc t", kc=8), [BK[bk]], [("xT", i)])
            for c in range(2):
                for gi, (f0, f1) in enumerate(FCG):
                    Wt, Wk = wnext(WB_DN + 3 * c + gi)
                    Wv = v3(Wt, f1 - f0, 512)
                    for i in range(4):
                        for fc in range(f0, f1):
                            mm(banks[i][:, :], actT[:, fc, 128 * i:128 * i + 128], Wv[:, fc - f0, :], fc == 0, fc == 21, [AR("actT", fc), Wk], [BK[i]])
                for i in range(4):
                    dst = A[:, i, 512 * c:512 * c + 512]
                    stt("dve", dst, dst, ALPHA, banks[i][:, :], ALU.mult, ALU.add, [("A", i), BK[i]], [("A", i)])
            layer_norm_blocks(True)

    S.muted = False
    S.op("sp", lambda e: e.nop(), ["out"] + (["dbg_yssd", "dbg_yatt", "dbg_h", "dbg_thr"] if debug else []), [])
    stats = S.emit(st)
    return nc, st, stats


_PARAM_SHAPES = {
    "w_in": (1024, 4696), "ssd_conv_w": (4, 1536), "ssd_conv_b": (1, 1536), "dt_bias": (1, 16), "a_log": (1, 16),
    "d_skip": (1, 16), "ssd_norm_g": (1, 1024), "idx_k_norm_g": (1, 64), "idx_k_norm_b": (1, 64), "w_out": (2048, 1024),
    "ln1_g": (1, 1024), "ln1_b": (1, 1024), "ffn_w_up": (1024, 5632), "ffn_conv_w": (3, 5632), "ffn_conv_b": (1, 5632),
    "ffn_w_down": (2816, 1024), "ln2_g": (1, 1024), "ln2_b": (1, 1024),
}


def run(inputs, n_cores, NB, NQ, debug=False, trace=False, stop=99):
    nc, st, stats = build(NB, NQ, debug, stop)
    L = NQ * 512
    params = {k: np.ascontiguousarray(np.asarray(inputs[k], dtype=np.float32).reshape(shp)) for k, shp in _PARAM_SHAPES.items()}
    x = np.asarray(inputs["x"], dtype=np.float32)
    in_maps = []
    for c in range(n_cores):
        m = dict(params)
        m["x"] = np.ascontiguousarray(x[c * NB:(c + 1) * NB, :L, :])
        in_maps.append(m)
    res = run_bass_kernel_spmd(nc, in_maps, core_ids=list(range(n_cores)), **({"trace": True} if trace else {}))
    st.close()
    return res, stats


def kernel(**inputs):
    res, _ = run(inputs, 8, 2, 8)
    return np.concatenate([r["out"] for r in res.results], axis=0).astype(np.float32)
```

```python
import numpy as np
import concourse.bass as bass
import concourse.mybir as mybir
from contextlib import ExitStack

F32 = mybir.dt.float32
BF16 = mybir.dt.bfloat16
AF = mybir.ActivationFunctionType
ALU = mybir.AluOpType
AX = mybir.AxisListType


class _Res:
    __slots__ = ("w", "rs")

    def __init__(self):
        self.w = None
        self.rs = []


class _Op:
    __slots__ = ("eng", "fn", "deps", "signal", "sig", "dma", "prewait")

    def __init__(self, eng, fn, dma):
        self.eng = eng
        self.fn = fn
        self.deps = set()
        self.signal = False
        self.sig = None
        self.dma = dma
        self.prewait = None


class Sched:
    ENGS = ("pe", "act", "dve", "pool", "sp")
    EPOCH = 20000

    def __init__(self, nc, n_dma_sems=16):
        self.nc = nc
        self.ops = []
        self.res = {}
        self.n_dma_sems = n_dma_sems
        self.muted = False

    def _entries(self, key, create=True):
        if isinstance(key, tuple):
            name, reg = key[0], key[1:]
        else:
            name, reg = key, None
        d = self.res.setdefault(name, {"*": _Res()})
        if reg is None:
            return list(d.values()), True, d
        if reg not in d:
            d[reg] = _Res()
        return [d["*"], d[reg]], False, d

    def op(self, eng, fn, reads=(), writes=(), dma=False):
        if self.muted:
            return None
        o = _Op(eng, fn, dma)
        writes = list(writes) + [k for k in reads if isinstance(k, tuple) and k[0] == "ps" and k not in writes]
        for k in reads:
            ents, _, _ = self._entries(k)
            for e in ents:
                if e.w is not None:
                    o.deps.add((e.w, True))
        for k in writes:
            ents, _, _ = self._entries(k)
            for e in ents:
                if e.w is not None:
                    o.deps.add((e.w, False))
                for r in e.rs:
                    o.deps.add((r, False))
        for k in reads:
            ents, whole, d = self._entries(k)
            if whole:
                for e in ents:
                    e.rs.append(o)
            else:
                ents[1].rs.append(o)
        for k in writes:
            ents, whole, d = self._entries(k)
            if whole:
                for e in ents:
                    e.w = o
                    e.rs = []
            else:
                ents[1].w = o
                ents[1].rs = []
        self.ops.append(o)
        return o

    def emit(self, stack):
        nc = self.nc
        for o in self.ops:
            keep = set()
            for (d, raw) in o.deps:
                if d is o:
                    continue
                if d.dma or o.dma:
                    keep.add(d)
                elif d.eng != o.eng:
                    keep.add(d)
                elif raw and o.eng != "pe":
                    keep.add(d)
            o.deps = keep
            for d in keep:
                d.signal = True
        cnt = {e: 0 for e in self.ENGS}
        dcnt = {e: 0 for e in self.ENGS}
        sems = {}

        def getsem(name):
            if name not in sems:
                sems[name] = stack.enter_context(nc.semaphore(name))
            return sems[name]

        for o in self.ops:
            if o.dma:
                i = dcnt[o.eng]
                dcnt[o.eng] += 1
                k = i % self.n_dma_sems
                r = i // self.n_dma_sems
                o.sig = (f"d_{o.eng}_{k}", 16 * (r + 1))
                if r > 0:
                    o.prewait = (f"d_{o.eng}_{k}", 16 * r)
            elif o.signal:
                cnt[o.eng] += 1
                t = cnt[o.eng]
                o.sig = (f"s_{o.eng}_{(t - 1) // self.EPOCH}", (t - 1) % self.EPOCH + 1)
        for o in self.ops:
            if o.sig is not None:
                getsem(o.sig[0])
        per = {e: [o for o in self.ops if o.eng == e] for e in self.ENGS}
        nwaits = [0]

        def run(engname, eng):
            seen = {}
            for o in per[engname]:
                waits = {}
                if o.prewait is not None:
                    waits[o.prewait[0]] = o.prewait[1]
                for d in o.deps:
                    s, v = d.sig
                    if waits.get(s, 0) < v:
                        waits[s] = v
                for s, v in waits.items():
                    if seen.get(s, 0) >= v:
                        continue
                    seen[s] = v
                    eng.wait_ge(sems[s], v)
                    nwaits[0] += 1
                ins = o.fn(eng)
                if o.sig is not None:
                    ins.then_inc(sems[o.sig[0]], 16 if o.dma else 1)

        block = stack.enter_context(nc.Block())
        if per["pe"]:
            @block.tensor
            def _(e):
                run("pe", e)
        if per["act"]:
            @block.scalar
            def _(e):
                run("act", e)
        if per["dve"]:
            @block.vector
            def _(e):
                run("dve", e)
        if per["pool"]:
            @block.gpsimd
            def _(e):
                run("pool", e)
        if per["sp"]:
            @block.sync
            def _(e):
                run("sp", e)
        return {e: len(per[e]) for e in self.ENGS}, nwaits[0]

from concourse.bass_utils import run_bass_kernel_spmd

NIT = 16
ALPHA = 2.0 ** 0.25
NEG = -1.0e30


def build(NB, NQ, debug=False, stop=99):
    nc = bass.Bass("TRN2", target_bir_lowering=False)
    L = NQ * 512
    dram = {}

    def din(name, shape):
        dram[name] = nc.dram_tensor(name, shape, F32, kind="ExternalInput").ap()
        return dram[name]

    x = din("x", [NB, L, 1024])
    w_in = din("w_in", [1024, 4696])
    ssd_conv_w = din("ssd_conv_w", [4, 1536])
    ssd_conv_b = din("ssd_conv_b", [1, 1536])
    dt_bias = din("dt_bias", [1, 16])
    a_log = din("a_log", [1, 16])
    d_skip = din("d_skip", [1, 16])
    ssd_norm_g = din("ssd_norm_g", [1, 1024])
    idx_k_norm_g = din("idx_k_norm_g", [1, 64])
    idx_k_norm_b = din("idx_k_norm_b", [1, 64])
    w_out = din("w_out", [2048, 1024])
    ln1_g = din("ln1_g", [1, 1024])
    ln1_b = din("ln1_b", [1, 1024])
    w_up = din("ffn_w_up", [1024, 5632])
    ffn_conv_w = din("ffn_conv_w", [3, 5632])
    ffn_conv_b = din("ffn_conv_b", [1, 5632])
    w_down = din("ffn_w_down", [2816, 1024])
    ln2_g = din("ln2_g", [1, 1024])
    ln2_b = din("ln2_b", [1, 1024])
    out = nc.dram_tensor("out", [NB, L, 1024], F32, kind="ExternalOutput").ap()
    NWB = 31
    wsc = nc.dram_tensor("wsc", [NWB, 128, 4096], BF16, kind="Internal").ap()
    dbg = {}
    if debug:
        dbg["yssd"] = nc.dram_tensor("dbg_yssd", [NB, NQ, 128, 8, 512], BF16, kind="ExternalOutput").ap()
        dbg["yatt"] = nc.dram_tensor("dbg_yatt", [NB, NQ, 128, 8, 512], BF16, kind="ExternalOutput").ap()
        dbg["h"] = nc.dram_tensor("dbg_h", [NB, L, 1024], F32, kind="ExternalOutput").ap()
        dbg["thr"] = nc.dram_tensor("dbg_thr", [NB, NQ, 4, 128, 2], F32, kind="ExternalOutput").ap()

    st = ExitStack()
    S = Sched(nc)

    def phase(n):
        if n > stop:
            S.muted = True

    BIG = 207 * 1024
    big = nc.alloc_sbuf_tensor("big", [128, BIG], mybir.dt.uint8)
    base = nc.lookup_mloc(big).addr
    cur = [base]
    uid = [0]

    def alloc_at(off, shape, dt):
        uid[0] += 1
        return nc.alloc_sbuf_tensor_at(f"t{uid[0]}", list(shape), dt, offset=off)

    def nbytes(shape, dt):
        return int(np.prod(shape[1:])) * (4 if dt == F32 else 2)

    def sb(shape, dt):
        t = alloc_at(cur[0], shape, dt)
        cur[0] += (nbytes(shape, dt) + 31) // 32 * 32
        return t

    psall = nc.alloc_psum_tensor("psall", [128, 4096], F32)
    banks = [psall[:, 512 * i:512 * i + 512] for i in range(8)]
    bankbf = [b.bitcast(BF16) for b in banks]
    BK = [("ps", i) for i in range(8)]

    def mm(out_, lhsT, rhs, start, stop, reads, writes):
        S.op("pe", lambda e: e.matmul(out_, lhsT=lhsT, rhs=rhs, start=start, stop=stop), reads, writes)

    def tr(out_, in_, idn, reads, writes):
        S.op("pe", lambda e: e.transpose(out=out_, in_=in_, identity=idn), reads, writes)

    def act(out_, in_, func, reads, writes, **kw):
        S.op("act", lambda e: e.activation(out=out_, in_=in_, func=func, **kw), reads, writes)

    def ts(eng, out_, in0, s1, s2, op0, op1, reads, writes, accum=None):
        if accum is None:
            S.op(eng, lambda e: e.tensor_scalar(out=out_, in0=in0, scalar1=s1, scalar2=s2, op0=op0, op1=op1), reads, writes)
        else:
            S.op(eng, lambda e: e.tensor_scalar(out=out_, in0=in0, scalar1=s1, scalar2=s2, op0=op0, op1=op1, accum_out=accum), reads, writes)

    def stt(eng, out_, in0, scalar, in1, op0, op1, reads, writes):
        S.op(eng, lambda e: e.scalar_tensor_tensor(out=out_, in0=in0, scalar=scalar, in1=in1, op0=op0, op1=op1), reads, writes)

    def tt(eng, out_, in0, in1, op, reads, writes):
        S.op(eng, lambda e: e.tensor_tensor(out=out_, in0=in0, in1=in1, op=op), reads, writes)

    def cp(eng, out_, in_, reads, writes):
        if eng == "act":
            S.op("act", lambda e: e.activation(out=out_, in_=in_, func=AF.Copy), reads, writes)
        else:
            S.op(eng, lambda e: e.tensor_copy(out=out_, in_=in_), reads, writes)

    def mset(eng, out_, val, writes):
        S.op(eng, lambda e: e.memset(out_, val), (), writes)

    def dma(out_, in_, reads, writes, q="sp"):
        S.op(q, lambda e: e.dma_start(out=out_, in_=in_), reads, writes, dma=True)

    def red(eng, out_, in_, op, reads, writes, absval=False):
        if absval:
            S.op(eng, lambda e: e.tensor_reduce(out=out_, in_=in_, axis=AX.X, op=op, apply_absolute_value=True), reads, writes)
        else:
            S.op(eng, lambda e: e.tensor_reduce(out=out_, in_=in_, axis=AX.X, op=op), reads, writes)

    Wr = [sb([128, 4096], BF16) for _ in range(2)]
    A = sb([128, 4, 1024], F32)
    xT = sb([128, 8, 512], BF16)
    xb16 = [sb([128, 1024], BF16) for _ in range(2)]
    qT = sb([128, 8, 512], BF16)
    qiT = sb([128, 4, 512], BF16)
    BT = sb([128, 2, 512], BF16)
    CT = sb([128, 2, 512], BF16)
    kT = sb([128, 2, L], BF16)
    V = sb([128, 4 * NQ, 384], BF16)
    kiT = sb([128, L], BF16)
    Sst = sb([128, 1024], F32)
    Sb = sb([128, 1024], BF16)
    yssdT = sb([128, 8, 512], BF16)
    yattT = sb([128, 8, 512], BF16)
    halo_s = sb([128, 12, 3], F32)
    halo_f = sb([128, 44, 2], F32)
    kif = sb([128, 4, 64], F32)
    kixc = sb([128, 4, 64], F32)
    kidup = sb([128, 4, 128], BF16)
    dtw = sb([128, 4, 24], F32)
    dtx = sb([128, 4, 16], F32)
    dtl = sb([128, 4, 16], F32)
    dt_t = sb([128, 4, 16], F32)
    dtA_t = sb([128, 4, 16], F32)
    absw = sb([128, 4, 8], F32)
    sgnw = sb([128, 4, 8], F32)
    sm4 = sb([128, 4, 8], F32)
    ident = sb([128, 128], BF16)
    identf = sb([128, 128], F32)
    triu = sb([128, 128], F32)
    strict = sb([128, 128], F32)
    negtri = sb([128, 128], F32)
    onesf = sb([128, 128], F32)
    ones_bf = sb([128, 64], BF16)
    shup = sb([128, 128], F32)
    cw = sb([128, 48], F32)
    cb = sb([128, 12], F32)
    cwf = sb([128, 132], F32)
    cbf = sb([128, 44], F32)
    dtbB = sb([128, 16], F32)
    aB = sb([128, 16], F32)
    dskB = sb([128, 16], F32)
    kgB = sb([128, 64], F32)
    kbB = sb([128, 64], F32)
    pow2c = sb([128, NIT + 2], F32)
    negthr = sb([128, 1], F32)
    negone = sb([128, 1], F32)
    fence_t = sb([128, 1], F32)
    ARENA = cur[0]
    ARENA_SZ = base + BIG - ARENA
    assert ARENA_SZ >= 62 * 1024, ARENA_SZ
    acur = [ARENA]

    def arena_reset():
        acur[0] = ARENA
        mset("pool", fence_t[:], 0.0, ["arena"])

    def ab(shape, dt):
        t = alloc_at(acur[0], shape, dt)
        acur[0] += (nbytes(shape, dt) + 31) // 32 * 32
        assert acur[0] <= base + BIG, (acur[0] - ARENA)
        return t

    mset("pool", identf[:], 1.0, ["identf"])
    S.op("pool", lambda e: e.affine_select(out=identf[:], in_=identf[:], pattern=[[-1, 128]], compare_op=ALU.is_equal, fill=0.0, base=0, channel_multiplier=1), ["identf"], ["identf"])
    cp("dve", ident[:], identf[:], ["identf"], ["ident"])
    mset("pool", triu[:], 1.0, ["triu"])
    S.op("pool", lambda e: e.affine_select(out=triu[:], in_=triu[:], pattern=[[1, 128]], compare_op=ALU.is_ge, fill=0.0, base=0, channel_multiplier=-1), ["triu"], ["triu"])
    mset("pool", strict[:], 1.0, ["strict"])
    S.op("pool", lambda e: e.affine_select(out=strict[:], in_=strict[:], pattern=[[-1, 128]], compare_op=ALU.is_gt, fill=0.0, base=0, channel_multiplier=1), ["strict"], ["strict"])
    mset("pool", negtri[:], 0.0, ["negtri"])
    S.op("pool", lambda e: e.affine_select(out=negtri[:], in_=negtri[:], pattern=[[-1, 128]], compare_op=ALU.is_ge, fill=NEG, base=0, channel_multiplier=1), ["negtri"], ["negtri"])
    mset("pool", onesf[:], 1.0, ["onesf"])
    mset("pool", ones_bf[:], 1.0, ["ones_bf"])
    mset("pool", V[:, :, 64:128], 1.0, ["Vones"])
    mset("pool", V[:, :, 256:320], 1.0, ["Vones"])
    mset("pool", shup[:], 0.0, ["shup"])
    cp("dve", shup[0:64, 64:128], identf[0:64, 0:64], ["identf", "shup"], ["shup"])
    mset("pool", negthr[:], -1.0e29, ["negthr"])
    mset("pool", negone[:], -1.0, ["negone"])
    for i in range(NIT + 2):
        mset("pool", pow2c[:, i:i + 1], 2.0 ** (-i), ["pow2c"])
    for (t, src, n) in ((dtbB, dt_bias, 16), (aB, a_log, 16), (dskB, d_skip, 16), (kgB, idx_k_norm_g, 64), (kbB, idx_k_norm_b, 64)):
        dma(t[:], src.broadcast_to([128, n]), [], ["allc"])
    act(aB[:], aB[:], AF.Exp, ["allc"], ["allc"])
    ts("dve", aB[:], aB[:], -1.0, 0.0, ALU.mult, ALU.add, ["allc"], ["allc"])

    arena_reset()
    ldtmp = ab([128, 128], F32)
    stg32 = [ab([128, 4096], F32) for _ in range(2)]
    stg16 = [ab([128, 4096], BF16) for _ in range(2)]

    def load_T(dst, src_rows, R):
        dma(ldtmp[0:R, :], src_rows, [], [("arena", "ldtmp")])
        tr(banks[0][:, 0:R], ldtmp[0:R, :], identf[0:R, 0:R], [("arena", "ldtmp"), "identf"], [BK[0]])
        cp("dve", dst, banks[0][:, 0:R], [BK[0]], ["allc"])

    load_T(cw[:, 0:48], ssd_conv_w.rearrange("k (c p) -> (k c) p", p=128), 48)
    load_T(cb[:, 0:12], ssd_conv_b.rearrange("o (c p) -> (o c) p", p=128), 12)
    for k in range(3):
        load_T(cwf[:, 44 * k:44 * k + 44], ffn_conv_w[k:k + 1, :].rearrange("o (c p) -> (o c) p", p=128), 44)
    load_T(cbf[:, 0:44], ffn_conv_b.rearrange("o (c p) -> (o c) p", p=128), 44)
    ts("dve", cw[:], cw[:], 0.5, 0.0, ALU.mult, ALU.add, ["allc"], ["allc"])
    ts("dve", cb[:], cb[:], 0.5, 0.0, ALU.mult, ALU.add, ["allc"], ["allc"])
    for k in range(3):
        ts("dve", cwf[:, 44 * k:44 * k + 22], cwf[:, 44 * k:44 * k + 22], 0.5, 0.0, ALU.mult, ALU.add, ["allc"], ["allc"])
    ts("dve", cbf[:, 0:22], cbf[:, 0:22], 0.5, 0.0, ALU.mult, ALU.add, ["allc"], ["allc"])

    phase(1)
    w_in_v = w_in.rearrange("(kc p) n -> p kc n", p=128)
    w_up_v = w_up.rearrange("(kc p) n -> p kc n", p=128)
    w_out_s = w_out[0:1024, :].rearrange("(fc p) n -> p fc n", p=128)
    w_out_a = w_out[1024:2048, :].rearrange("(h p) n -> p h n", p=64)
    w_down_v = w_down.rearrange("(fc p) n -> p fc n", p=128)

    blocks = []

    def v3(s, a, b):
        return s[:, 0:a * b].rearrange("p (a b) -> p a b", a=a)

    def v4(s):
        return s[:, 0:4096].rearrange("p (c kc m) -> p c kc m", c=4, kc=8)

    def add_tm(cols, scale):
        n = sum(c[1] for c in cols)
        loads = []
        o = 0
        for (c0, cn) in cols:
            loads.append((lambda s, o=o, cn=cn, n=n: v3(s, 8, n)[:, :, o:o + cn], w_in_v[:, :, c0:c0 + cn]))
            o += cn
        blocks.append(dict(np_=128, n=8 * n, scale=scale, loads=loads))

    def add_fm(srcv, chunks, scale):
        loads = []
        for ci, pieces in enumerate(chunks):
            m0 = 0
            for (c0, cn) in pieces:
                loads.append((lambda s, ci=ci, m0=m0, cn=cn: v4(s)[:, ci, :, m0:m0 + cn], srcv[:, :, c0:c0 + cn]))
                m0 += cn
        blocks.append(dict(np_=128, n=len(chunks) * 1024, scale=scale, loads=loads))

    WB_MISC = len(blocks)
    add_tm([(3856, 256), (4624, 64), (2560, 16), (4688, 8)], 1.0)
    XS0 = 1024
    add_fm(w_in_v, [[(XS0 + 128 * i, 128)] for i in range(0, 4)], 1.0)
    add_fm(w_in_v, [[(XS0 + 128 * i, 128)] for i in range(4, 8)], 1.0)
    add_fm(w_in_v, [[(2048, 128)], [(2176, 128)], [(2304, 128)], [(2432, 128)]], 1.0)
    QPAIRS = [(0, 4), (1, 5), (2, 6), (3, 7), (8, 12), (9, 13), (10, 14), (11, 15)]
    add_fm(w_in_v, [[(2576 + 64 * a, 64), (2576 + 64 * b, 64)] for (a, b) in QPAIRS[0:4]], 0.125)
    add_fm(w_in_v, [[(2576 + 64 * a, 64), (2576 + 64 * b, 64)] for (a, b) in QPAIRS[4:8]], 0.125)
    add_fm(w_in_v, [[(3600, 128)], [(3728, 128)], [(4112, 128)], [(4240, 128)]], 1.0)
    add_fm(w_in_v, [[(4368, 128)], [(4496, 128)]], 1.0)
    WB_FM = 1
    WB_TM = len(blocks)
    add_tm([(0, 512)], 0.5)
    add_tm([(512, 512)], 0.5)
    WB_OUT = len(blocks)
    for c in range(2):
        blocks.append(dict(np_=128, n=4096, scale=1.0, loads=[(lambda s: v3(s, 8, 512), w_out_s[:, :, c * 512:(c + 1) * 512])]))
        cols = slice(c * 512, (c + 1) * 512)
        blocks.append(dict(np_=128, n=4096, scale=1.0, loads=[
            (lambda s: v3(s, 8, 512)[0:64, 0:4, :], w_out_a[:, 0:4, cols]),
            (lambda s: v3(s, 8, 512)[0:64, 4:8, :], w_out_a[:, 8:12, cols]),
            (lambda s: v3(s, 8, 512)[64:128, 0:4, :], w_out_a[:, 4:8, cols]),
            (lambda s: v3(s, 8, 512)[64:128, 4:8, :], w_out_a[:, 12:16, cols])]))
    WB_UP = len(blocks)
    for b_ in range(11):
        add_fm(w_up_v, [[(256 * b_, 128)], [(256 * b_ + 128, 128)], [(2816 + 256 * b_, 128)], [(2816 + 256 * b_ + 128, 128)]], 1.0)
    WB_DN = len(blocks)
    FCG = [(0, 8), (8, 16), (16, 22)]
    for c in range(2):
        for (f0, f1) in FCG:
            nf = f1 - f0
            blocks.append(dict(np_=128, n=nf * 512, scale=1.0,
                               loads=[(lambda s, nf=nf: v3(s, nf, 512), w_down_v[:, f0:f1, c * 512:(c + 1) * 512])]))
    assert len(blocks) == NWB, len(blocks)

    cast_engs = ["dve", "act", "dve"]
    for k, blk in enumerate(blocks):
        s32 = stg32[k % 2]
        s16 = stg16[k % 2]
        np_, n = blk["np_"], blk["n"]
        for (dstfn, src) in blk["loads"]:
            dma(dstfn(s32), src, [], [("arena", "s32", k % 2)])
        eng = cast_engs[k % 3]
        sc = blk["scale"]
        if eng == "act":
            act(s16[0:np_, 0:n], s32[0:np_, 0:n], AF.Copy, [("arena", "s32", k % 2)], [("arena", "s16", k % 2)], scale=sc)
        else:
            ts(eng, s16[0:np_, 0:n], s32[0:np_, 0:n], sc, 0.0, ALU.mult, ALU.add, [("arena", "s32", k % 2)], [("arena", "s16", k % 2)])
        dma(wsc[k, 0:np_, 0:n], s16[0:np_, 0:n], [("arena", "s16", k % 2)], [("wsc", k)])

    wseq = [0]
    wuse = [0]

    def wload_upto(g):
        while wseq[0] <= g:
            q = wseq[0]
            k = q % NWB
            blk = blocks[k]
            np_, n = blk["np_"], blk["n"]
            dma(Wr[q % 2][0:np_, 0:n], wsc[k, 0:np_, 0:n], [("wsc", k)], [("W", q % 2)])
            wseq[0] += 1

    def wnext(expect_k):
        g = wuse[0]
        assert g % NWB == expect_k, (g % NWB, expect_k)
        wload_upto(g + 1)
        wuse[0] += 1
        return Wr[g % 2], ("W", g % 2)

    total_q = NB * NQ
    wload_upto(0)

    bank_rr = [0]

    def nb(lo=0, hi=8):
        k = lo + bank_rr[0] % (hi - lo)
        bank_rr[0] += 1
        return k

    for b in range(NB):
        mset("pool", halo_s[:], 0.0, ["halo_s"])
        mset("pool", halo_f[:], 0.0, ["halo_f"])
        mset("pool", Sst[:], 0.0, ["Sst"])
        mset("pool", Sb[:], 0.0, ["Sb"])
        for j in range(NQ):
            t0 = 512 * j
            phase(2)
            arena_reset()
            xsT = ab([128, 8, 512], F32)
            normgB = ab([128, 1024], F32)
            raw = [ab([128, 515], F32) for _ in range(2)]
            cacc = [ab([128, 512], F32) for _ in range(2)]
            ctmp = [ab([128, 512], F32) for _ in range(2)]
            LH8 = ab([128, 8, 128], F32)
            E8 = ab([128, 8, 128], F32)
            CBm = ab([128, 2, 128], F32)
            MT = ab([128, 16, 128], BF16)
            xs_tm = ab([128, 1024], F32)
            B_tm = ab([128, 2, 128], BF16)
            xdt = ab([128, 1024], BF16)
            xdtd = ab([128, 1024], BF16)
            ytmp = ab([128, 1024], F32)
            ynb = ab([128, 1024], BF16)
            junk = ab([128, 512], F32)
            ssm = ab([128, 64], F32)
            AR = lambda n, *r: ("arena", n) + tuple(r)

            dma(normgB[:], ssd_norm_g.broadcast_to([128, 1024]), [], [AR("normgB")])
            if b == 0 and j == 0:
                dma(A[:], x[b, t0:t0 + 512, :].rearrange("(i p) d -> p i d", p=128), [], [("A", i) for i in range(4)])
                for i in range(4):
                    xb = xb16[i % 2]
                    cp("pool" if i % 2 == 0 else "dve", xb[:], A[:, i, :], [("A", i)], [("xb16", i % 2)])
                    bk = i % 2
                    for kc in range(8):
                        tr(bankbf[bk][:, 128 * kc:128 * kc + 128], xb[:, 128 * kc:128 * kc + 128], ident[:], [("xb16", i % 2), "ident"], [BK[bk]])
                    cp("act", xT[:, :, 128 * i:128 * i + 128], bankbf[bk][:, 0:1024].rearrange("p (kc t) -> p kc t", kc=8), [BK[bk]], [("xT", i)])
            XTK = [("xT", i) for i in range(4)]

            phase(3)
            Wt, Wk = wnext(WB_MISC)
            Wv = v3(Wt, 8, 344)
            for i in range(4):
                bk = nb(2, 8)
                for kc in range(8):
                    mm(banks[bk][:, 0:344], xT[:, kc, 128 * i:128 * i + 128], Wv[:, kc, :], kc == 0, kc == 7, [("xT", i), Wk], [BK[bk]])
                cp("act", V[:, 4 * j + i, 0:64], banks[bk][:, 0:64], [BK[bk]], [("V", j)])
                cp("act", V[:, 4 * j + i, 128:256], banks[bk][:, 64:192], [BK[bk]], [("V", j)])
                cp("act", V[:, 4 * j + i, 320:384], banks[bk][:, 192:256], [BK[bk]], [("V", j)])
                cp("dve", kif[:, i, :], banks[bk][:, 256:320], [BK[bk]], ["kif"])
                cp("dve", dtw[:, i, :], banks[bk][:, 320:344], [BK[bk]], ["dtw"])
            tt("dve", dtx[:], dtw[:, :, 0:16], dtbB[:].unsqueeze(1).broadcast_to([128, 4, 16]), ALU.add, ["dtw", "allc"], ["dtx"])
            stt("dve", dtl[:], dtx[:], -1.0, dtx[:], ALU.mult, ALU.max, ["dtx"], ["dtl"])
            act(dtl[:], dtl[:], AF.Exp, ["dtl"], ["dtl"], scale=-1.0)
            act(dtl[:], dtl[:], AF.Ln, ["dtl"], ["dtl"], bias=1.0)
            stt("dve", dt_t[:], dtx[:], 0.0, dtl[:], ALU.max, ALU.add, ["dtx", "dtl"], ["dt_t"])
            tt("dve", dtA_t[:], dt_t[:], aB[:].unsqueeze(1).broadcast_to([128, 4, 16]), ALU.mult, ["dt_t", "allc"], ["dtA_t"])
            stt("dve", absw[:], dtw[:, :, 16:24], -1.0, dtw[:, :, 16:24], ALU.mult, ALU.max, ["dtw"], ["absw"])
            ts("dve", sgnw[:], dtw[:, :, 16:24], 0.0, 2.0, ALU.is_ge, ALU.mult, ["dtw"], ["sgnw"])
            ts("dve", sgnw[:], sgnw[:], -1.0, 0.0, ALU.add, ALU.add, ["sgnw"], ["sgnw"])
            red("dve", sm4[:, :, 0], kif[:], ALU.add, ["kif"], ["sm4"])
            ts("dve", sm4[:, :, 1], sm4[:, :, 0], 1.0 / 64, 0.0, ALU.mult, ALU.add, ["sm4"], ["sm4"])
            tt("dve", kixc[:], kif[:], sm4[:, :, 1:2].broadcast_to([128, 4, 64]), ALU.subtract, ["kif", "sm4"], ["kixc"])
            tt("dve", kif[:], kixc[:], kixc[:], ALU.mult, ["kixc"], ["kif"])
            red("dve", sm4[:, :, 2], kif[:], ALU.add, ["kif"], ["sm4"])
            ts("dve", sm4[:, :, 3], sm4[:, :, 2], 1.0 / 64, 1e-5, ALU.mult, ALU.add, ["sm4"], ["sm4"])
            act(sm4[:, :, 3], sm4[:, :, 3], AF.Ln, ["sm4"], ["sm4"])
            act(sm4[:, :, 4], sm4[:, :, 3], AF.Exp, ["sm4"], ["sm4"], scale=-0.5)
            tt("dve", kixc[:], kixc[:], sm4[:, :, 4:5].broadcast_to([128, 4, 64]), ALU.mult, ["kixc", "sm4"], ["kixc"])
            tt("dve", kixc[:], kixc[:], kgB[:].unsqueeze(1).broadcast_to([128, 4, 64]), ALU.mult, ["kixc", "allc"], ["kixc"])
            for hf in range(2):
                tt("dve", kidup[:, :, 64 * hf:64 * hf + 64], kixc[:], kbB[:].unsqueeze(1).broadcast_to([128, 4, 64]), ALU.add, ["kixc", "allc"], ["kidup"])
            def fm_chunk(Wt, Wk, ci):
                bk = nb(2, 8)
                Wv = v4(Wt)
                for kc in range(8):
                    mm(banks[bk][:, :], Wv[:, ci, kc, :], xT[:, kc, :], kc == 0, kc == 7, XTK + [Wk], [BK[bk]])
                return bk

            pend = []

            def conv_silu(bk, xi, dst, dkey):
                r = raw[xi % 2]
                rk = AR("raw", xi % 2)
                cp("act", r[:, 3:515], banks[bk][:, :], [BK[bk]], [rk])
                cp("pool", r[:, 0:3], halo_s[:, xi, :], ["halo_s"], [rk])
                cp("pool", halo_s[:, xi, :], r[:, 512:515], [rk], ["halo_s"])
                ca = cacc[xi % 2]
                ck = AR("cacc", xi % 2)
                act(ca[:], banks[bk][:, :], AF.Identity, [BK[bk], "allc"], [ck], scale=cw[:, 36 + xi:36 + xi + 1], bias=cb[:, xi:xi + 1])
                for k in range(3):
                    stt("dve", ca[:], r[:, k:k + 512], cw[:, 12 * k + xi:12 * k + xi + 1], ca[:], ALU.mult, ALU.add, [rk, ck, "allc"], [ck])
                ct = ctmp[xi % 2]
                tk = AR("ctmp", xi % 2)

                def fin(ct=ct, ca=ca, ck=ck, tk=tk, dst=dst, dkey=dkey):
                    act(ct[:], ca[:], AF.Tanh, [ck], [tk])
                    stt("dve", dst, ct[:], 1.0, ca[:], ALU.add, ALU.mult, [tk, ck], [dkey])
                if pend:
                    pend.pop(0)()
                pend.append(fin)

            for blk_i in range(2):
                Wt, Wk = wnext(WB_FM + blk_i)
                for ci in range(4):
                    xi = 4 * blk_i + ci
                    bk = fm_chunk(Wt, Wk, ci)
                    conv_silu(bk, xi, xsT[:, xi, :], AR("xsT", xi))
            Wt, Wk = wnext(WB_FM + 2)
            for ci in range(4):
                xi = 8 + ci
                bk = fm_chunk(Wt, Wk, ci)
                if ci < 2:
                    conv_silu(bk, xi, BT[:, ci, :], ("BT", ci))
                else:
                    conv_silu(bk, xi, CT[:, ci - 2, :], ("CT", ci - 2))
            while pend:
                pend.pop(0)()
            for blk_i in range(2):
                Wt, Wk = wnext(WB_FM + 3 + blk_i)
                for ci in range(4):
                    bk = fm_chunk(Wt, Wk, ci)
                    qc = 4 * blk_i + ci
                    cp("act" if ci % 2 == 0 else "dve", qT[:, qc, :], banks[bk][:, :], [BK[bk]], [("qT", qc)])
            Wt, Wk = wnext(WB_FM + 5)
            for ci in range(4):
                bk = fm_chunk(Wt, Wk, ci)
                if ci < 2:
                    cp("act", kT[:, ci, t0:t0 + 512], banks[bk][:, :], [BK[bk]], [("kT", j)])
                else:
                    cp("dve", qiT[:, ci - 2, :], banks[bk][:, :], [BK[bk]], [("qiT", ci - 2)])
            Wt, Wk = wnext(WB_FM + 6)
            for ci in range(2):
                bk = fm_chunk(Wt, Wk, ci)
                cp("act", qiT[:, 2 + ci, :], banks[bk][:, :], [BK[bk]], [("qiT", 2 + ci)])

            for c in range(2):
                Wt, Wk = wnext(WB_TM + c)
                Wv = v3(Wt, 8, 512)
                for i in range(4):
                    bk = nb(2, 8)
                    for kc in range(8):
                        mm(banks[bk][:, :], xT[:, kc, 128 * i:128 * i + 128], Wv[:, kc, :], kc == 0, kc == 7, [("xT", i), Wk], [BK[bk]])
                    dst = A[:, i, 512 * c:512 * c + 512]
                    act(dst, banks[bk][:, :], AF.Tanh, [BK[bk]], [("A", i)])
                    stt("dve", dst, dst, 1.0, banks[bk][:, :], ALU.add, ALU.mult, [("A", i), BK[bk]], [("A", i)])
            bk = nb(0, 2)
            for i in range(4):
                tr(bankbf[bk][:, 128 * i:128 * i + 128], kidup[:, i, :], ident[:], ["kidup", "ident"], [BK[bk]])
            cp("act", kiT[:, t0:t0 + 512], bankbf[bk][:, 0:512], [BK[bk]], [("kiT", j)])

            phase(4)
            for c in range(4):
                cs = slice(128 * c, 128 * c + 128)
                dtA = dtA_t[:, c, :]
                dtc = dt_t[:, c, :]
                mm(banks[3][:, 0:16], triu[:], dtA, True, True, ["triu", "dtA_t"], [BK[3]])
                mm(banks[3][:, 16:32], onesf[:], dtA, True, True, ["onesf", "dtA_t"], [BK[3]])
                act(ssm[:, 0:32], banks[3][:, 0:32], AF.Exp, [BK[3]], [AR("ssm")])
                eacs = ssm[:, 0:16]
                cdB = ssm[:, 16:32]
                for g in range(2):
                    mm(banks[2][:, 128 * g:128 * g + 128], BT[:, g, cs], CT[:, g, cs], True, True, [("BT", g), ("CT", g)], [BK[2]])
                tt("dve", CBm[:], banks[2][:, 0:256].rearrange("p (g l) -> p g l", g=2), triu[:].unsqueeze(1).broadcast_to([128, 2, 128]), ALU.mult, [BK[2], "triu"], [AR("CBm")])
                for hh in range(2):
                    tt("dve", LH8[:], strict[:].unsqueeze(1).broadcast_to([128, 8, 128]),
                       dtA[:, 8 * hh:8 * hh + 8].unsqueeze(2).broadcast_to([128, 8, 128]), ALU.mult, ["strict", "dtA_t"], [AR("LH8")])
                    for h8 in range(8):
                        bk = h8 // 4
                        mm(banks[bk][:, 128 * (h8 % 4):128 * (h8 % 4) + 128], LH8[:, h8, :], triu[:], True, True, [AR("LH8"), "triu"], [BK[bk]])
                    for bk in range(2):
                        act(E8[:, 4 * bk:4 * bk + 4, :], banks[bk][:, :].rearrange("p (h l) -> p h l", h=4), AF.Exp, [BK[bk]], [AR("E8")])
                    cp("dve", ssm[:, 32 + 8 * hh:32 + 8 * hh + 8], E8[:, :, 127], [AR("E8")], [AR("ssm")])
                    tt("dve", MT[:, 8 * hh:8 * hh + 8, :], E8[:], CBm[:, hh:hh + 1, :].broadcast_to([128, 8, 128]), ALU.mult, [AR("E8"), AR("CBm")], [AR("MT")])
                dte = ssm[:, 32:48]
                for fc in range(8):
                    bk = 4 + fc // 4
                    tr(banks[bk][:, 128 * (fc % 4):128 * (fc % 4) + 128], xsT[:, fc, cs], identf[:], [AR("xsT", fc), "identf"], [BK[bk]])
                for bk in (4, 5):
                    cp("act", xs_tm[:, 512 * (bk - 4):512 * (bk - 4) + 512], banks[bk][:, :], [BK[bk]], [AR("xs_tm")])
                for g in range(2):
                    tr(bankbf[3][:, 128 * g:128 * g + 128], BT[:, g, cs], ident[:], [("BT", g), "ident"], [BK[3]])
                cp("act", B_tm[:], bankbf[3][:, 0:256].rearrange("p (g n) -> p g n", g=2), [BK[3]], [AR("B_tm")])
                xs3 = xs_tm[:].rearrange("p (h d) -> p h d", h=16)
                tt("dve", xdt[:].rearrange("p (h d) -> p h d", h=16), xs3, dtc.unsqueeze(2).broadcast_to([128, 16, 64]), ALU.mult, [AR("xs_tm"), "dt_t"], [AR("xdt")])
                tt("dve", ssm[:, 48:64], dtc, dte, ALU.mult, ["dt_t", AR("ssm")], [AR("ssm")])
                tt("dve", xdtd[:].rearrange("p (h d) -> p h d", h=16), xs3, ssm[:, 48:64].unsqueeze(2).broadcast_to([128, 16, 64]), ALU.mult, [AR("xs_tm"), AR("ssm")], [AR("xdtd")])
                for h in range(16):
                    bk = 4 + h // 8
                    mm(banks[bk][:, 64 * (h % 8):64 * (h % 8) + 64], MT[:, h, :], xdt[:, 64 * h:64 * h + 64], True, True, [AR("MT"), AR("xdt")], [BK[bk]])
                for g in range(2):
                    mm(banks[6 + g][:, :], CT[:, g, cs], Sb[:, 512 * g:512 * g + 512], True, True, [("CT", g), "Sb"], [BK[6 + g]])
                for g in range(2):
                    mm(banks[g][:, :], B_tm[:, g, :], xdtd[:, 512 * g:512 * g + 512], True, True, [AR("B_tm"), AR("xdtd")], [BK[g]])
                S3 = Sst[:].rearrange("p (h d) -> p h d", h=16)
                tt("dve", S3, S3, cdB.unsqueeze(2).broadcast_to([128, 16, 64]), ALU.mult, ["Sst", AR("ssm")], ["Sst"])
                for g in range(2):
                    tt("dve", Sst[:, 512 * g:512 * g + 512], Sst[:, 512 * g:512 * g + 512], banks[g][:, :], ALU.add, ["Sst", BK[g]], ["Sst"])
                cp("act", Sb[:], Sst[:], ["Sst"], ["Sb"])
                for g in range(2):
                    ysl = ytmp[:, 512 * g:512 * g + 512]
                    tt("dve", ysl.rearrange("p (h d) -> p h d", h=8), banks[6 + g][:, :].rearrange("p (h d) -> p h d", h=8),
                       eacs[:, 8 * g:8 * g + 8].unsqueeze(2).broadcast_to([128, 8, 64]), ALU.mult, [BK[6 + g], AR("ssm")], [AR("ytmp")])
                    tt("dve", ysl, ysl, banks[4 + g][:, :], ALU.add, [AR("ytmp"), BK[4 + g]], [AR("ytmp")])
                tt("dve", xs3, xs3, dskB[:].unsqueeze(2).broadcast_to([128, 16, 64]), ALU.mult, [AR("xs_tm"), "allc"], [AR("xs_tm")])
                tt("dve", ytmp[:], ytmp[:], xs_tm[:], ALU.add, [AR("ytmp"), AR("xs_tm")], [AR("ytmp")])
                tt("dve", ytmp[:], ytmp[:], A[:, c, :], ALU.mult, [AR("ytmp"), ("A", c)], [AR("ytmp")])
                for g in range(2):
                    act(junk[:], ytmp[:, 512 * g:512 * g + 512], AF.Square, [AR("ytmp")], [AR("junk"), AR("ssm")], accum_out=ssm[:, 60 + g:61 + g])
                ts("dve", ssm[:, 60:62], ssm[:, 60:62], 1.0 / 512, 1e-5, ALU.mult, ALU.add, [AR("ssm")], [AR("ssm")])
                act(ssm[:, 60:62], ssm[:, 60:62], AF.Ln, [AR("ssm")], [AR("ssm")])
                act(ssm[:, 62:64], ssm[:, 60:62], AF.Exp, [AR("ssm")], [AR("ssm")], scale=-0.5)
                for g in range(2):
                    stt("dve", ynb[:, 512 * g:512 * g + 512], ytmp[:, 512 * g:512 * g + 512], ssm[:, 62 + g:63 + g], normgB[:, 512 * g:512 * g + 512],
                        ALU.mult, ALU.mult, [AR("ytmp"), AR("ssm"), AR("normgB")], [AR("ynb")])
                for fc in range(8):
                    tr(bankbf[2][:, 128 * fc:128 * fc + 128], ynb[:, 128 * fc:128 * fc + 128], ident[:], [AR("ynb"), "ident"], [BK[2]])
                cp("act", yssdT[:, :, cs], bankbf[2][:, 0:1024].rearrange("p (f t) -> p f t", f=8), [BK[2]], [("yssdT", c)])
            if debug:
                dma(dbg["yssd"][b, j], yssdT[:], [("yssdT", c) for c in range(4)], ["dbg_yssd"])

            phase(5)
            arena_reset()
            maskT = ab([128, 32, 512], BF16)
            score = ab([128, 4096], F32)
            rbuf = [ab([128, 512], BF16) for _ in range(4)]
            Dg = ab([128, 8, 128], BF16)
            mrow = [ab([128, 512], BF16) for _ in range(2)]
            qiTz = ab([128, 8, 512], BF16)
            Wcol = ab([128, NIT + 2], F32)
            cnt = ab([128, NIT + 2], F32)
            cnta = ab([128, NIT + 2], F32)
            bsm = ab([128, 8], F32)
            jnk = ab([128, 2], F32)
            NKB = 4 * j + 4
            scoreb = [score, A[:].rearrange("p a b -> p (a b)")]
            skeys = [[AR("score")], [("A", i_) for i_ in range(4)]]
            mset("pool", qiTz[:], 0.0, [AR("qiTz")])
            for h in range(8):
                hp = 64 * (h % 2)
                cp("act" if h % 2 else "pool", qiTz[hp:hp + 64, h, :], qiT[hp:hp + 64, h // 2, :], [("qiT", h // 2), AR("qiTz")], [AR("qiTz", h)])
            KIK = [("kiT", jj) for jj in range(j + 1)]

            def relu_gen(i):
                n_keys = 128 * (4 * j + i + 1)
                nkt = (n_keys + 511) // 512
                sc = scoreb[i % 2]
                sk = skeys[i % 2]
                tt("dve", Dg[:], ident[:].unsqueeze(1).broadcast_to([128, 8, 128]), sgnw[:, i, :].unsqueeze(2).broadcast_to([128, 8, 128]), ALU.mult,
                   ["ident", "sgnw"], [AR("Dg")])
                items = [(kt, h) for kt in range(nkt) for h in range(8)]

                def ifront(q):
                    kt, h = items[q]
                    w = min(512, n_keys - 512 * kt)
                    bk = q % 4
                    mm(banks[bk][:, 0:w], qiTz[:, h, 128 * i:128 * i + 128], kiT[:, 512 * kt:512 * kt + w], True, True,
                       [AR("qiTz", h)] + KIK, [BK[bk]])
                    if q % 4 != 3:
                        act(rbuf[q % 4][:, 0:w], banks[bk][:, 0:w], AF.Relu, [BK[bk], "absw"], [AR("rbuf", q % 4)], scale=absw[:, i, h:h + 1])
                    else:
                        ts("dve", rbuf[q % 4][:, 0:w], banks[bk][:, 0:w], absw[:, i, h:h + 1], 0.0, ALU.mult, ALU.max, [BK[bk], "absw"], [AR("rbuf", q % 4)])

                def iback(q):
                    kt, h = items[q]
                    w = min(512, n_keys - 512 * kt)
                    ab_ = 4 + kt % 2
                    mm(banks[ab_][:, 0:w], Dg[:, h, :], rbuf[q % 4][:, 0:w], h == 0, h == 7, [AR("Dg"), AR("rbuf", q % 4)], [BK[ab_]])
                    if h == 7:
                        cp("dve", sc[:, 512 * kt:512 * kt + w], banks[ab_][:, 0:w], [BK[ab_]], sk)

                ILA = 2
                for q in range(len(items) + ILA):
                    if q < len(items):
                        ifront(q)
                    if q - ILA >= 0:
                        iback(q - ILA)
                    yield

            def bisect_gen(i, out):
                n_kb = 4 * j + i + 1
                n_keys = 128 * n_kb
                sc = scoreb[i % 2]
                sk = skeys[i % 2]
                out["thr"] = negthr[:, 0:1]
                out["key"] = "negthr"
                if n_kb >= 3:
                    red("dve", bsm[:, 0:1], sc[:, 0:n_keys], ALU.max, sk, [AR("bsm")], absval=True)
                dsl = sc[:, 128 * (n_kb - 1):128 * n_kb]
                tt("dve", dsl, dsl, negtri[:], ALU.add, sk + ["negtri"], sk)
                if n_kb < 3:
                    return
                ts("dve", Wcol[:], pow2c[:], bsm[:, 0:1], 0.0, ALU.mult, ALU.add, ["pow2c", AR("bsm")], [AR("Wcol")])
                mset("dve", bsm[:, 1:2], 0.0, [AR("bsm")])
                mid = bsm[:, 1:2]
                n_dve = (n_keys * 40 // 100) // 64 * 64
                n_act = n_keys - n_dve
                cthr = 511.0 - n_act
                yield
                for it in range(NIT):
                    ts("dve", jnk[:, 0:1].broadcast_to([128, n_dve]), sc[:, 0:n_dve], mid, 0.0, ALU.is_ge, ALU.add, sk + [AR("bsm")], [AR("jnk", 0), AR("cnt")], accum=cnt[:, it:it + 1])
                    act(jnk[:, 1:2].broadcast_to([128, n_act]), sc[:, n_dve:n_keys], AF.Sign, sk + [AR("bsm")], [AR("jnk", 1), AR("cnta")], scale=-1.0, bias=mid, accum_out=cnta[:, it:it + 1])
                    stt("dve", bsm[:, 6:7], cnt[:, it:it + 1], 2.0, cnta[:, it:it + 1], ALU.mult, ALU.subtract, [AR("cnt"), AR("cnta")], [AR("bsm")])
                    if it < NIT - 1:
                        ts("dve", bsm[:, 2:3], bsm[:, 6:7], cthr, Wcol[:, it:it + 1], ALU.is_ge, ALU.mult, [AR("bsm"), AR("Wcol")], [AR("bsm")])
                        stt("dve", mid, bsm[:, 2:3], Wcol[:, it + 1:it + 2], mid, ALU.subtract, ALU.add, [AR("bsm"), AR("Wcol")], [AR("bsm")])
                    else:
                        ts("dve", bsm[:, 2:3], bsm[:, 6:7], cthr, -1.0, ALU.is_ge, ALU.add, [AR("bsm")], [AR("bsm")])
                        stt("dve", bsm[:, 3:4], bsm[:, 2:3], Wcol[:, it:it + 1], mid, ALU.mult, ALU.add, [AR("bsm"), AR("Wcol")], [AR("bsm")])
                    yield
                out["thr"] = bsm[:, 3:4]
                out["key"] = AR("bsm")
                if debug:
                    cp("dve", bsm[:, 4:5], bsm[:, 3:4], [AR("bsm")], [AR("bsm")])
                    cp("dve", bsm[:, 5:6], bsm[:, 6:7], [AR("bsm")], [AR("bsm")])
                    dma(dbg["thr"][b, j, i], bsm[:, 4:6], [AR("bsm")], ["dbg_thr"])

            def mask_gen(i, thr, thr_key):
                n_kb = 4 * j + i + 1
                n_keys = 128 * n_kb
                nkt = (n_keys + 511) // 512
                sc = scoreb[i % 2]
                sk = skeys[i % 2]
                for kt in range(nkt):
                    w = min(512, n_keys - 512 * kt)
                    nblk = w // 128
                    mr = mrow[kt % 2]
                    ts("dve", mr[:, 0:w], sc[:, 512 * kt:512 * kt + w], thr, 0.0, ALU.is_ge, ALU.add, sk + [thr_key], [AR("mrow", kt % 2)])
                    bk = 6 + kt % 2
                    for q in range(nblk):
                        tr(bankbf[bk][:, 128 * q:128 * q + 128], mr[:, 128 * q:128 * q + 128], ident[:], [AR("mrow", kt % 2), "ident"], [BK[bk]])
                    cp("act", maskT[:, 4 * kt:4 * kt + nblk, 128 * i:128 * i + 128], bankbf[bk][:, 0:128 * nblk].rearrange("p (q t) -> p q t", q=nblk),
                       [BK[bk]], [AR("maskT", i)])
                if n_kb < NKB:
                    mset("pool", maskT[:, n_kb:NKB, 128 * i:128 * i + 128], 0.0, [AR("maskT", i)])

            for _ in relu_gen(0):
                pass
            for i in range(4):
                res_ = {}
                bg = bisect_gen(i, res_)
                rg = relu_gen(i + 1) if i < 3 else iter(())
                n_r = (8 * ((128 * (4 * j + i + 2) + 511) // 512) + 2) if i < 3 else 0
                per = max(1, -(-n_r // (NIT + 1)))
                b_done = False
                r_done = (i == 3)
                while not (b_done and r_done):
                    if not b_done:
                        try:
                            next(bg)
                        except StopIteration:
                            b_done = True
                    for _ in range(per if not b_done else 10 ** 6):
                        if r_done:
                            break
                        try:
                            next(rg)
                        except StopIteration:
                            r_done = True
                mask_gen(i, res_["thr"], res_["key"])
            dma(A[:], x[b, t0:t0 + 512, :].rearrange("(i p) d -> p i d", p=128), [], [("A", i) for i in range(4)])
            arena_reset()
            maskT = ab([128, 32, 512], BF16)
            NEP = 4
            Eall = ab([128, 2 * NEP, 512], BF16)
            accS = [ab([128, 512], F32) for _ in range(2)]
            rq = ab([128, 8], F32)
            qTz = ab([128, 16, 512], BF16)
            mset("pool", qTz[:], 0.0, [AR("qTz")])
            for qc_ in range(8):
                for sd in range(2):
                    cp("act" if sd else "pool", qTz[64 * sd:64 * sd + 64, 2 * qc_ + sd, :], qT[64 * sd:64 * sd + 64, qc_, :], [("qT", qc_), AR("qTz")], [AR("qTz", 2 * qc_ + sd)])
            MK = [AR("maskT", i) for i in range(4)]
            KTK = [("kT", jj) for jj in range(j + 1)]
            VK = [("V", jj) for jj in range(j + 1)] + ["Vones"]
            VOFF = [0, 64, 192, 256]
            items = [(qc, kb) for qc in range(8) for kb in range(NKB)]
            LA = 2
            sctr = [0]
            islot = {}

            def front(it):
                qc, kb = items[it]
                a_, b_h = QPAIRS[qc]
                slot = (a_ // 4) // 2
                sp = sctr[0] % 3
                sctr[0] += 1
                islot[it] = sp
                c0 = 128 * max(0, kb - 4 * j)
                cw = 512 - c0
                for side in range(2):
                    mm(banks[2 * sp + side][:, c0:512], kT[:, slot, 128 * kb:128 * kb + 128], qTz[:, 2 * qc + side, c0:512], True, True,
                       KTK + [AR("qTz", 2 * qc + side)], [BK[2 * sp + side]])
                ep = it % NEP
                ek = AR("E", ep)
                e2 = Eall[:, 2 * ep:2 * ep + 2, c0:512]
                act(e2, psall[:, 1024 * sp:1024 * sp + 1024].rearrange("p (a b) -> p a b", a=2)[:, :, c0:512], AF.Exp, [BK[2 * sp], BK[2 * sp + 1]], [ek])
                tt("dve", e2, e2, maskT[:, kb:kb + 1, c0:512].broadcast_to([128, 2, cw]), ALU.mult, [ek] + MK, [ek])

            def back(it):
                qc, kb = items[it]
                ep = it % NEP
                ek = AR("E", ep)
                for side in range(2):
                    h = QPAIRS[qc][side]
                    n = h // 4
                    c0 = 128 * max(0, kb - 4 * j)
                    mm(banks[6 + side][:, c0:512], V[:, kb, VOFF[n]:VOFF[n] + 128], Eall[:, 2 * ep + side, c0:512], kb == 0, kb == NKB - 1, VK + [ek], [BK[6 + side]])
                if kb == NKB - 1:
                    for side in range(2):
                        cp("act", accS[side][:], banks[6 + side][:, :], [BK[6 + side]], [AR("accS", side)])
                    box = {}

                    def st_den(qc=qc, box=box):
                        sp = sctr[0] % 3
                        sctr[0] += 1
                        for side in range(2):
                            selc = identf[:, 64:65] if side == 0 else identf[:, 0:1]
                            for i4 in range(4):
                                mm(banks[2 * sp][:, 4 * side + i4:4 * side + i4 + 1], accS[side][:, 128 * i4:128 * i4 + 128], selc, True, True,
                                   ["identf", AR("accS", side)], [BK[2 * sp]])
                        S.op("dve", lambda e, sp=sp: e.reciprocal(out=rq[:, 0:8], in_=banks[2 * sp][:, 0:8]), [BK[2 * sp]], [AR("rq")])

                    def st_bc(qc=qc, box=box):
                        sp = sctr[0] % 3
                        sctr[0] += 1
                        box["sp"] = sp
                        for side in range(2):
                            M = 64 if side == 0 else 128
                            for i4 in range(4):
                                mm(banks[2 * sp + side][0:M, 128 * i4:128 * i4 + 128], rq[:, 4 * side + i4:4 * side + i4 + 1].broadcast_to([128, M]), identf[:, :], True, True,
                                   ["identf", AR("rq")], [BK[2 * sp + side]])

                    def st_fin(qc=qc, box=box):
                        sp = box["sp"]
                        for side in range(2):
                            h = QPAIRS[qc][side]
                            olo = 0 if side == 0 else 64
                            tt("dve", yattT[olo:olo + 64, qc, :], accS[side][olo:olo + 64, :], banks[2 * sp + side][olo:olo + 64, :], ALU.mult,
                               [AR("accS", side), BK[2 * sp + side]], [("yattT", h)])

                    d1 = max(1, min(3, NKB - 3))
                    d2 = max(d1 + 1, min(5, NKB - 2))
                    deferred.append((it + LA + d1, st_den))
                    deferred.append((it + LA + d2, st_bc))
                    deferred.append((it + LA + d2 + 1, st_fin))

            deferred = []
            for it in range(len(items) + LA):
                if it < len(items):
                    front(it)
                if it - LA >= 0:
                    back(it - LA)
                while deferred and deferred[0][0] <= it:
                    deferred.pop(0)[1]()
            while deferred:
                deferred.pop(0)[1]()
            if debug:
                dma(dbg["yatt"][b, j], yattT[:], [("yattT", h) for h in range(16)], ["dbg_yatt"])

            phase(6)
            arena_reset()
            lnG = ab([128, 1024], F32)
            lnB = ab([128, 1024], F32)
            junk5 = ab([128, 1024], F32)
            lsm = ab([128, 4, 8], F32)
            dma(lnG[:], ln1_g.broadcast_to([128, 1024]), [], [AR("lnG")])
            dma(lnB[:], ln1_b.broadcast_to([128, 1024]), [], [AR("lnB")])

            def layer_norm_blocks(final):
                for i in range(4):
                    act(junk5[:], A[:, i, :], AF.Identity, [("A", i)], [AR("junk5"), AR("lsm")], accum_out=lsm[:, i, 0:1])
                    act(junk5[:], A[:, i, :], AF.Square, [("A", i)], [AR("junk5"), AR("lsm")], accum_out=lsm[:, i, 1:2])
                ts("dve", lsm[:, :, 2], lsm[:, :, 0], 1.0 / 1024, 0.0, ALU.mult, ALU.add, [AR("lsm")], [AR("lsm")])
                tt("dve", lsm[:, :, 3], lsm[:, :, 2], lsm[:, :, 2], ALU.mult, [AR("lsm")], [AR("lsm")])
                stt("dve", lsm[:, :, 4], lsm[:, :, 1], 1.0 / 1024, lsm[:, :, 3], ALU.mult, ALU.subtract, [AR("lsm")], [AR("lsm")])
                ts("dve", lsm[:, :, 4], lsm[:, :, 4], 1e-5, 0.0, ALU.add, ALU.add, [AR("lsm")], [AR("lsm")])
                act(lsm[:, :, 4], lsm[:, :, 4], AF.Ln, [AR("lsm")], [AR("lsm")])
                act(lsm[:, :, 5], lsm[:, :, 4], AF.Exp, [AR("lsm")], [AR("lsm")], scale=-0.5)
                stt("dve", lsm[:, :, 6], lsm[:, :, 2], -1.0, lsm[:, :, 5], ALU.mult, ALU.mult, [AR("lsm")], [AR("lsm")])
                for i in range(4):
                    act(A[:, i, :], A[:, i, :], AF.Identity, [("A", i), AR("lsm")], [("A", i)], scale=lsm[:, i, 5:6], bias=lsm[:, i, 6:7])
                    tt("pool" if i % 2 else "dve", A[:, i, :], A[:, i, :], lnG[:], ALU.mult, [("A", i), AR("lnG")], [("A", i)])
                    tt("dve", A[:, i, :], A[:, i, :], lnB[:], ALU.add, [("A", i), AR("lnB")], [("A", i)])
                    if final:
                        dma(out[b, t0 + 128 * i:t0 + 128 * i + 128, :], A[:, i, :], [("A", i)], ["out"])

            for c in range(2):
                Wt, Wk = wnext(WB_OUT + 2 * c)
                Wv = v3(Wt, 8, 512)
                for i in range(4):
                    for fc in range(8):
                        mm(banks[i][:, :], yssdT[:, fc, 128 * i:128 * i + 128], Wv[:, fc, :], fc == 0, False, [("yssdT", i), Wk], [BK[i]])
                Wt, Wk = wnext(WB_OUT + 2 * c + 1)
                Wv = v3(Wt, 8, 512)
                for i in range(4):
                    for qc in range(8):
                        mm(banks[i][:, :], yattT[:, qc, 128 * i:128 * i + 128], Wv[:, qc, :], False, qc == 7,
                           [("yattT", QPAIRS[qc][0]), ("yattT", QPAIRS[qc][1]), Wk], [BK[i]])
                for i in range(4):
                    dst = A[:, i, 512 * c:512 * c + 512]
                    stt("dve", dst, dst, ALPHA, banks[i][:, :], ALU.mult, ALU.add, [("A", i), BK[i]], [("A", i)])
            layer_norm_blocks(False)
            if debug:
                for i in range(4):
                    dma(dbg["h"][b, t0 + 128 * i:t0 + 128 * i + 128, :], A[:, i, :], [("A", i)], ["dbg_h"])
            for i in range(4):
                xb = xb16[i % 2]
                cp("pool" if i % 2 == 0 else "dve", xb[:], A[:, i, :], [("A", i)], [("xb16", i % 2)])
                bk = 4 + i % 2
                for kc in range(8):
                    tr(bankbf[bk][:, 128 * kc:128 * kc + 128], xb[:, 128 * kc:128 * kc + 128], ident[:], [("xb16", i % 2), "ident"], [BK[bk]])
                cp("act", xT[:, :, 128 * i:128 * i + 128], bankbf[bk][:, 0:1024].rearrange("p (kc t) -> p kc t", kc=8), [BK[bk]], [("xT", i)])

            phase(7)
            arena_reset()
            lnG = ab([128, 1024], F32)
            lnB = ab([128, 1024], F32)
            junk5 = ab([128, 1024], F32)
            lsm = ab([128, 4, 8], F32)
            actT = ab([128, 22, 512], BF16)
            rawf = [ab([128, 514], F32) for _ in range(2)]
            facc = [ab([128, 512], F32) for _ in range(2)]
            ftmp = [ab([128, 512], F32) for _ in range(2)]
            gbuf = [ab([128, 512], F32) for _ in range(2)]
            dma(lnG[:], ln2_g.broadcast_to([128, 1024]), [], [AR("lnG")])
            dma(lnB[:], ln2_b.broadcast_to([128, 1024]), [], [AR("lnB")])
            fcount = 0
            fpend = []
            for b_ in range(11):
                Wt, Wk = wnext(WB_UP + b_)
                for ci in range(4):
                    isup = ci >= 2
                    fchunk = 2 * b_ + (ci % 2)
                    cidx = fchunk + (22 if isup else 0)
                    bk = fm_chunk(Wt, Wk, ci)
                    r = rawf[fcount % 2]
                    rk = AR("rawf", fcount % 2)
                    cp("act", r[:, 2:514], banks[bk][:, :], [BK[bk]], [rk])
                    cp("pool", r[:, 0:2], halo_f[:, cidx, :], ["halo_f"], [rk])
                    cp("pool", halo_f[:, cidx, :], r[:, 512:514], [rk], ["halo_f"])
                    fa = facc[fcount % 2]
                    fk = AR("facc", fcount % 2)
                    act(fa[:], banks[bk][:, :], AF.Identity, [BK[bk], "allc"], [fk], scale=cwf[:, 88 + cidx:89 + cidx], bias=cbf[:, cidx:cidx + 1])
                    for k in range(2):
                        stt("dve", fa[:], r[:, k:k + 512], cwf[:, 44 * k + cidx:44 * k + cidx + 1], fa[:], ALU.mult, ALU.add, [rk, fk, "allc"], [fk])
                    if fpend:
                        fpend.pop(0)()
                    if not isup:
                        ft = ftmp[fcount % 2]

                        def ffin(ft=ft, fa=fa, fk=fk, ci=ci, fc_=fcount):
                            act(ft[:], fa[:], AF.Tanh, [fk], [AR("ftmp", fc_ % 2)])
                            stt("dve", gbuf[ci][:], ft[:], 1.0, fa[:], ALU.add, ALU.mult, [AR("ftmp", fc_ % 2), fk], [AR("gbuf", ci)])
                        fpend.append(ffin)
                    else:
                        def ufin(fa=fa, fk=fk, ci=ci, fchunk=fchunk):
                            tt("dve", actT[:, fchunk, :], fa[:], gbuf[ci - 2][:], ALU.mult, [fk, AR("gbuf", ci - 2)], [AR("actT", fchunk)])
                        fpend.append(ufin)
                    fcount += 1
            while fpend:
                fpend.pop(0)()
            nb_, nj_ = (b, j + 1) if j + 1 < NQ else (b + 1, 0)
            if nb_ < NB:
                xstage = [ab([128, 1024], F32) for _ in range(2)]
                for i in range(4):
                    xs_ = xstage[i % 2]
                    dma(xs_[:], x[nb_, 512 * nj_ + 128 * i:512 * nj_ + 128 * i + 128, :], [], [AR("xstage", i % 2)])
                    xb = xb16[i % 2]
                    cp("pool", xb[:], xs_[:], [AR("xstage", i % 2)], [("xb16", i % 2)])
                    bk = 4 + i % 2
                    for kc in range(8):
                        tr(bankbf[bk][:, 128 * kc:128 * kc + 128], xb[:, 128 * kc:128 * kc + 128], ident[:], [("xb16", i % 2), "ident"], [BK[bk]])
                    cp("act", xT[:, :, 128 * i:128 * i + 128], bankbf[bk][:, 0:1024].rearrange("p (kc t) -> p kc t", kc=8), [BK[bk]], [("xT", i)])
            for c in range(2):
                for gi, (f0, f1) in enumerate(FCG):
                    Wt, Wk = wnext(WB_DN + 3 * c + gi)
                    Wv = v3(Wt, f1 - f0, 512)
                    for i in range(4):
                        for fc in range(f0, f1):
                            mm(banks[i][:, :], actT[:, fc, 128 * i:128 * i + 128], Wv[:, fc - f0, :], fc == 0, fc == 21, [AR("actT", fc), Wk], [BK[i]])
                for i in range(4):
                    dst = A[:, i, 512 * c:512 * c + 512]
                    stt("dve", dst, dst, ALPHA, banks[i][:, :], ALU.mult, ALU.add, [("A", i), BK[i]], [("A", i)])
            layer_norm_blocks(True)

    S.muted = False
    S.op("sp", lambda e: e.nop(), ["out"] + (["dbg_yssd", "dbg_yatt", "dbg_h", "dbg_thr"] if debug else []), [])
    stats = S.emit(st)
    return nc, st, stats


_PARAM_SHAPES = {
    "w_in": (1024, 4696), "ssd_conv_w": (4, 1536), "ssd_conv_b": (1, 1536), "dt_bias": (1, 16), "a_log": (1, 16),
    "d_skip": (1, 16), "ssd_norm_g": (1, 1024), "idx_k_norm_g": (1, 64), "idx_k_norm_b": (1, 64), "w_out": (2048, 1024),
    "ln1_g": (1, 1024), "ln1_b": (1, 1024), "ffn_w_up": (1024, 5632), "ffn_conv_w": (3, 5632), "ffn_conv_b": (1, 5632),
    "ffn_w_down": (2816, 1024), "ln2_g": (1, 1024), "ln2_b": (1, 1024),
}


def run(inputs, n_cores, NB, NQ, debug=False, trace=False, stop=99):
    nc, st, stats = build(NB, NQ, debug, stop)
    L = NQ * 512
    params = {k: np.ascontiguousarray(np.asarray(inputs[k], dtype=np.float32).reshape(shp)) for k, shp in _PARAM_SHAPES.items()}
    x = np.asarray(inputs["x"], dtype=np.float32)
    in_maps = []
    for c in range(n_cores):
        m = dict(params)
        m["x"] = np.ascontiguousarray(x[c * NB:(c + 1) * NB, :L, :])
        in_maps.append(m)
    res = run_bass_kernel_spmd(nc, in_maps, core_ids=list(range(n_cores)), **({"trace": True} if trace else {}))
    st.close()
    return res, stats


def kernel(**inputs):
    res, _ = run(inputs, 8, 2, 8)
    return np.concatenate([r["out"] for r in res.results], axis=0).astype(np.float32)
```

```python
import numpy as np
import concourse.bass as bass
import concourse.mybir as mybir
from contextlib import ExitStack

F32 = mybir.dt.float32
BF16 = mybir.dt.bfloat16
AF = mybir.ActivationFunctionType
ALU = mybir.AluOpType
AX = mybir.AxisListType


class _Res:
    __slots__ = ("w", "rs")

    def __init__(self):
        self.w = None
        self.rs = []


class _Op:
    __slots__ = ("eng", "fn", "deps", "signal", "sig", "dma", "prewait")

    def __init__(self, eng, fn, dma):
        self.eng = eng
        self.fn = fn
        self.deps = set()
        self.signal = False
        self.sig = None
        self.dma = dma
        self.prewait = None


class Sched:
    ENGS = ("pe", "act", "dve", "pool", "sp")
    EPOCH = 20000

    def __init__(self, nc, n_dma_sems=16):
        self.nc = nc
        self.ops = []
        self.res = {}
        self.n_dma_sems = n_dma_sems
        self.muted = False

    def _entries(self, key, create=True):
        if isinstance(key, tuple):
            name, reg = key[0], key[1:]
        else:
            name, reg = key, None
        d = self.res.setdefault(name, {"*": _Res()})
        if reg is None:
            return list(d.values()), True, d
        if reg not in d:
            d[reg] = _Res()
        return [d["*"], d[reg]], False, d

    def op(self, eng, fn, reads=(), writes=(), dma=False):
        if self.muted:
            return None
        o = _Op(eng, fn, dma)
        writes = list(writes) + [k for k in reads if isinstance(k, tuple) and k[0] == "ps" and k not in writes]
        for k in reads:
            ents, _, _ = self._entries(k)
            for e in ents:
                if e.w is not None:
                    o.deps.add((e.w, True))
        for k in writes:
            ents, _, _ = self._entries(k)
            for e in ents:
                if e.w is not None:
                    o.deps.add((e.w, False))
                for r in e.rs:
                    o.deps.add((r, False))
        for k in reads:
            ents, whole, d = self._entries(k)
            if whole:
                for e in ents:
                    e.rs.append(o)
            else:
                ents[1].rs.append(o)
        for k in writes:
            ents, whole, d = self._entries(k)
            if whole:
                for e in ents:
                    e.w = o
                    e.rs = []
            else:
                ents[1].w = o
                ents[1].rs = []
        self.ops.append(o)
        return o

    def emit(self, stack):
        nc = self.nc
        for o in self.ops:
            keep = set()
            for (d, raw) in o.deps:
                if d is o:
                    continue
                if d.dma or o.dma:
                    keep.add(d)
                elif d.eng != o.eng:
                    keep.add(d)
                elif raw and o.eng != "pe":
                    keep.add(d)
            o.deps = keep
            for d in keep:
                d.signal = True
        cnt = {e: 0 for e in self.ENGS}
        dcnt = {e: 0 for e in self.ENGS}
        sems = {}

        def getsem(name):
            if name not in sems:
                sems[name] = stack.enter_context(nc.semaphore(name))
            return sems[name]

        for o in self.ops:
            if o.dma:
                i = dcnt[o.eng]
                dcnt[o.eng] += 1
                k = i % self.n_dma_sems
                r = i // self.n_dma_sems
                o.sig = (f"d_{o.eng}_{k}", 16 * (r + 1))
                if r > 0:
                    o.prewait = (f"d_{o.eng}_{k}", 16 * r)
            elif o.signal:
                cnt[o.eng] += 1
                t = cnt[o.eng]
                o.sig = (f"s_{o.eng}_{(t - 1) // self.EPOCH}", (t - 1) % self.EPOCH + 1)
        for o in self.ops:
            if o.sig is not None:
                getsem(o.sig[0])
        per = {e: [o for o in self.ops if o.eng == e] for e in self.ENGS}
        nwaits = [0]

        def run(engname, eng):
            seen = {}
            for o in per[engname]:
                waits = {}
                if o.prewait is not None:
                    waits[o.prewait[0]] = o.prewait[1]
                for d in o.deps:
                    s, v = d.sig
                    if waits.get(s, 0) < v:
                        waits[s] = v
                for s, v in waits.items():
                    if seen.get(s, 0) >= v:
                        continue
                    seen[s] = v
                    eng.wait_ge(sems[s], v)
                    nwaits[0] += 1
                ins = o.fn(eng)
                if o.sig is not None:
                    ins.then_inc(sems[o.sig[0]], 16 if o.dma else 1)

        block = stack.enter_context(nc.Block())
        if per["pe"]:
            @block.tensor
            def _(e):
                run("pe", e)
        if per["act"]:
            @block.scalar
            def _(e):
                run("act", e)
        if per["dve"]:
            @block.vector
            def _(e):
                run("dve", e)
        if per["pool"]:
            @block.gpsimd
            def _(e):
                run("pool", e)
        if per["sp"]:
            @block.sync
            def _(e):
                run("sp", e)
        return {e: len(per[e]) for e in self.ENGS}, nwaits[0]

from concourse.bass_utils import run_bass_kernel_spmd

NIT = 16
ALPHA = 2.0 ** 0.25
NEG = -1.0e30


def build(NB, NQ, debug=False, stop=99):
    nc = bass.Bass("TRN2", target_bir_lowering=False)
    L = NQ * 512
    dram = {}

    def din(name, shape):
        dram[name] = nc.dram_tensor(name, shape, F32, kind="ExternalInput").ap()
        return dram[name]

    x = din("x", [NB, L, 1024])
    w_in = din("w_in", [1024, 4696])
    ssd_conv_w = din("ssd_conv_w", [4, 1536])
    ssd_conv_b = din("ssd_conv_b", [1, 1536])
    dt_bias = din("dt_bias", [1, 16])
    a_log = din("a_log", [1, 16])
    d_skip = din("d_skip", [1, 16])
    ssd_norm_g = din("ssd_norm_g", [1, 1024])
    idx_k_norm_g = din("idx_k_norm_g", [1, 64])
    idx_k_norm_b = din("idx_k_norm_b", [1, 64])
    w_out = din("w_out", [2048, 1024])
    ln1_g = din("ln1_g", [1, 1024])
    ln1_b = din("ln1_b", [1, 1024])
    w_up = din("ffn_w_up", [1024, 5632])
    ffn_conv_w = din("ffn_conv_w", [3, 5632])
    ffn_conv_b = din("ffn_conv_b", [1, 5632])
    w_down = din("ffn_w_down", [2816, 1024])
    ln2_g = din("ln2_g", [1, 1024])
    ln2_b = din("ln2_b", [1, 1024])
    out = nc.dram_tensor("out", [NB, L, 1024], F32, kind="ExternalOutput").ap()
    NWB = 31
    wsc = nc.dram_tensor("wsc", [NWB, 128, 4096], BF16, kind="Internal").ap()
    dbg = {}
    if debug:
        dbg["yssd"] = nc.dram_tensor("dbg_yssd", [NB, NQ, 128, 8, 512], BF16, kind="ExternalOutput").ap()
        dbg["yatt"] = nc.dram_tensor("dbg_yatt", [NB, NQ, 128, 8, 512], BF16, kind="ExternalOutput").ap()
        dbg["h"] = nc.dram_tensor("dbg_h", [NB, L, 1024], F32, kind="ExternalOutput").ap()
        dbg["thr"] = nc.dram_tensor("dbg_thr", [NB, NQ, 4, 128, 2], F32, kind="ExternalOutput").ap()

    st = ExitStack()
    S = Sched(nc)

    def phase(n):
        if n > stop:
            S.muted = True

    BIG = 207 * 1024
    big = nc.alloc_sbuf_tensor("big", [128, BIG], mybir.dt.uint8)
    base = nc.lookup_mloc(big).addr
    cur = [base]
    uid = [0]

    def alloc_at(off, shape, dt):
        uid[0] += 1
        return nc.alloc_sbuf_tensor_at(f"t{uid[0]}", list(shape), dt, offset=off)

    def nbytes(shape, dt):
        return int(np.prod(shape[1:])) * (4 if dt == F32 else 2)

    def sb(shape, dt):
        t = alloc_at(cur[0], shape, dt)
        cur[0] += (nbytes(shape, dt) + 31) // 32 * 32
        return t

    psall = nc.alloc_psum_tensor("psall", [128, 4096], F32)
    banks = [psall[:, 512 * i:512 * i + 512] for i in range(8)]
    bankbf = [b.bitcast(BF16) for b in banks]
    BK = [("ps", i) for i in range(8)]

    def mm(out_, lhsT, rhs, start, stop, reads, writes):
        S.op("pe", lambda e: e.matmul(out_, lhsT=lhsT, rhs=rhs, start=start, stop=stop), reads, writes)

    def tr(out_, in_, idn, reads, writes):
        S.op("pe", lambda e: e.transpose(out=out_, in_=in_, identity=idn), reads, writes)

    def act(out_, in_, func, reads, writes, **kw):
        S.op("act", lambda e: e.activation(out=out_, in_=in_, func=func, **kw), reads, writes)

    def ts(eng, out_, in0, s1, s2, op0, op1, reads, writes, accum=None):
        if accum is None:
            S.op(eng, lambda e: e.tensor_scalar(out=out_, in0=in0, scalar1=s1, scalar2=s2, op0=op0, op1=op1), reads, writes)
        else:
            S.op(eng, lambda e: e.tensor_scalar(out=out_, in0=in0, scalar1=s1, scalar2=s2, op0=op0, op1=op1, accum_out=accum), reads, writes)

    def stt(eng, out_, in0, scalar, in1, op0, op1, reads, writes):
        S.op(eng, lambda e: e.scalar_tensor_tensor(out=out_, in0=in0, scalar=scalar, in1=in1, op0=op0, op1=op1), reads, writes)

    def tt(eng, out_, in0, in1, op, reads, writes):
        S.op(eng, lambda e: e.tensor_tensor(out=out_, in0=in0, in1=in1, op=op), reads, writes)

    def cp(eng, out_, in_, reads, writes):
        if eng == "act":
            S.op("act", lambda e: e.activation(out=out_, in_=in_, func=AF.Copy), reads, writes)
        else:
            S.op(eng, lambda e: e.tensor_copy(out=out_, in_=in_), reads, writes)

    def mset(eng, out_, val, writes):
        S.op(eng, lambda e: e.memset(out_, val), (), writes)

    def dma(out_, in_, reads, writes, q="sp"):
        S.op(q, lambda e: e.dma_start(out=out_, in_=in_), reads, writes, dma=True)

    def red(eng, out_, in_, op, reads, writes, absval=False):
        if absval:
            S.op(eng, lambda e: e.tensor_reduce(out=out_, in_=in_, axis=AX.X, op=op, apply_absolute_value=True), reads, writes)
        else:
            S.op(eng, lambda e: e.tensor_reduce(out=out_, in_=in_, axis=AX.X, op=op), reads, writes)

    Wr = [sb([128, 4096], BF16) for _ in range(2)]
    A = sb([128, 4, 1024], F32)
    xT = sb([128, 8, 512], BF16)
    xb16 = [sb([128, 1024], BF16) for _ in range(2)]
    qT = sb([128, 8, 512], BF16)
    qiT = sb([128, 4, 512], BF16)
    BT = sb([128, 2, 512], BF16)
    CT = sb([128, 2, 512], BF16)
    kT = sb([128, 2, L], BF16)
    V = sb([128, 4 * NQ, 384], BF16)
    kiT = sb([128, L], BF16)
    Sst = sb([128, 1024], F32)
    Sb = sb([128, 1024], BF16)
    yssdT = sb([128, 8, 512], BF16)
    yattT = sb([128, 8, 512], BF16)
    halo_s = sb([128, 12, 3], F32)
    halo_f = sb([128, 44, 2], F32)
    kif = sb([128, 4, 64], F32)
    kixc = sb([128, 4, 64], F32)
    kidup = sb([128, 4, 128], BF16)
    dtw = sb([128, 4, 24], F32)
    dtx = sb([128, 4, 16], F32)
    dtl = sb([128, 4, 16], F32)
    dt_t = sb([128, 4, 16], F32)
    dtA_t = sb([128, 4, 16], F32)
    absw = sb([128, 4, 8], F32)
    sgnw = sb([128, 4, 8], F32)
    sm4 = sb([128, 4, 8], F32)
    ident = sb([128, 128], BF16)
    identf = sb([128, 128], F32)
    triu = sb([128, 128], F32)
    strict = sb([128, 128], F32)
    negtri = sb([128, 128], F32)
    onesf = sb([128, 128], F32)
    ones_bf = sb([128, 64], BF16)
    shup = sb([128, 128], F32)
    cw = sb([128, 48], F32)
    cb = sb([128, 12], F32)
    cwf = sb([128, 132], F32)
    cbf = sb([128, 44], F32)
    dtbB = sb([128, 16], F32)
    aB = sb([128, 16], F32)
    dskB = sb([128, 16], F32)
    kgB = sb([128, 64], F32)
    kbB = sb([128, 64], F32)
    pow2c = sb([128, NIT + 2], F32)
    negthr = sb([128, 1], F32)
    negone = sb([128, 1], F32)
    fence_t = sb([128, 1], F32)
    ARENA = cur[0]
    ARENA_SZ = base + BIG - ARENA
    assert ARENA_SZ >= 62 * 1024, ARENA_SZ
    acur = [ARENA]

    def arena_reset():
        acur[0] = ARENA
        mset("pool", fence_t[:], 0.0, ["arena"])

    def ab(shape, dt):
        t = alloc_at(acur[0], shape, dt)
        acur[0] += (nbytes(shape, dt) + 31) // 32 * 32
        assert acur[0] <= base + BIG, (acur[0] - ARENA)
        return t

    mset("pool", identf[:], 1.0, ["identf"])
    S.op("pool", lambda e: e.affine_select(out=identf[:], in_=identf[:], pattern=[[-1, 128]], compare_op=ALU.is_equal, fill=0.0, base=0, channel_multiplier=1), ["identf"], ["identf"])
    cp("dve", ident[:], identf[:], ["identf"], ["ident"])
    mset("pool", triu[:], 1.0, ["triu"])
    S.op("pool", lambda e: e.affine_select(out=triu[:], in_=triu[:], pattern=[[1, 128]], compare_op=ALU.is_ge, fill=0.0, base=0, channel_multiplier=-1), ["triu"], ["triu"])
    mset("pool", strict[:], 1.0, ["strict"])
    S.op("pool", lambda e: e.affine_select(out=strict[:], in_=strict[:], pattern=[[-1, 128]], compare_op=ALU.is_gt, fill=0.0, base=0, channel_multiplier=1), ["strict"], ["strict"])
    mset("pool", negtri[:], 0.0, ["negtri"])
    S.op("pool", lambda e: e.affine_select(out=negtri[:], in_=negtri[:], pattern=[[-1, 128]], compare_op=ALU.is_ge, fill=NEG, base=0, channel_multiplier=1), ["negtri"], ["negtri"])
    mset("pool", onesf[:], 1.0, ["onesf"])
    mset("pool", ones_bf[:], 1.0, ["ones_bf"])
    mset("pool", V[:, :, 64:128], 1.0, ["Vones"])
    mset("pool", V[:, :, 256:320], 1.0, ["Vones"])
    mset("pool", shup[:], 0.0, ["shup"])
    cp("dve", shup[0:64, 64:128], identf[0:64, 0:64], ["identf", "shup"], ["shup"])
    mset("pool", negthr[:], -1.0e29, ["negthr"])
    mset("pool", negone[:], -1.0, ["negone"])
    for i in range(NIT + 2):
        mset("pool", pow2c[:, i:i + 1], 2.0 ** (-i), ["pow2c"])
    for (t, src, n) in ((dtbB, dt_bias, 16), (aB, a_log, 16), (dskB, d_skip, 16), (kgB, idx_k_norm_g, 64), (kbB, idx_k_norm_b, 64)):
        dma(t[:], src.broadcast_to([128, n]), [], ["allc"])
    act(aB[:], aB[:], AF.Exp, ["allc"], ["allc"])
    ts("dve", aB[:], aB[:], -1.0, 0.0, ALU.mult, ALU.add, ["allc"], ["allc"])

    arena_reset()
    ldtmp = ab([128, 128], F32)
    stg32 = [ab([128, 4096], F32) for _ in range(2)]
    stg16 = [ab([128, 4096], BF16) for _ in range(2)]

    def load_T(dst, src_rows, R):
        dma(ldtmp[0:R, :], src_rows, [], [("arena", "ldtmp")])
        tr(banks[0][:, 0:R], ldtmp[0:R, :], identf[0:R, 0:R], [("arena", "ldtmp"), "identf"], [BK[0]])
        cp("dve", dst, banks[0][:, 0:R], [BK[0]], ["allc"])

    load_T(cw[:, 0:48], ssd_conv_w.rearrange("k (c p) -> (k c) p", p=128), 48)
    load_T(cb[:, 0:12], ssd_conv_b.rearrange("o (c p) -> (o c) p", p=128), 12)
    for k in range(3):
        load_T(cwf[:, 44 * k:44 * k + 44], ffn_conv_w[k:k + 1, :].rearrange("o (c p) -> (o c) p", p=128), 44)
    load_T(cbf[:, 0:44], ffn_conv_b.rearrange("o (c p) -> (o c) p", p=128), 44)
    ts("dve", cw[:], cw[:], 0.5, 0.0, ALU.mult, ALU.add, ["allc"], ["allc"])
    ts("dve", cb[:], cb[:], 0.5, 0.0, ALU.mult, ALU.add, ["allc"], ["allc"])
    for k in range(3):
        ts("dve", cwf[:, 44 * k:44 * k + 22], cwf[:, 44 * k:44 * k + 22], 0.5, 0.0, ALU.mult, ALU.add, ["allc"], ["allc"])
    ts("dve", cbf[:, 0:22], cbf[:, 0:22], 0.5, 0.0, ALU.mult, ALU.add, ["allc"], ["allc"])

    phase(1)
    w_in_v = w_in.rearrange("(kc p) n -> p kc n", p=128)
    w_up_v = w_up.rearrange("(kc p) n -> p kc n", p=128)
    w_out_s = w_out[0:1024, :].rearrange("(fc p) n -> p fc n", p=128)
    w_out_a = w_out[1024:2048, :].rearrange("(h p) n -> p h n", p=64)
    w_down_v = w_down.rearrange("(fc p) n -> p fc n", p=128)

    blocks = []

    def v3(s, a, b):
        return s[:, 0:a * b].rearrange("p (a b) -> p a b", a=a)

    def v4(s):
        return s[:, 0:4096].rearrange("p (c kc m) -> p c kc m", c=4, kc=8)

    def add_tm(cols, scale):
        n = sum(c[1] for c in cols)
        loads = []
        o = 0
        for (c0, cn) in cols:
            loads.append((lambda s, o=o, cn=cn, n=n: v3(s, 8, n)[:, :, o:o + cn], w_in_v[:, :, c0:c0 + cn]))
            o += cn
        blocks.append(dict(np_=128, n=8 * n, scale=scale, loads=loads))

    def add_fm(srcv, chunks, scale):
        loads = []
        for ci, pieces in enumerate(chunks):
            m0 = 0
            for (c0, cn) in pieces:
                loads.append((lambda s, ci=ci, m0=m0, cn=cn: v4(s)[:, ci, :, m0:m0 + cn], srcv[:, :, c0:c0 + cn]))
                m0 += cn
        blocks.append(dict(np_=128, n=len(chunks) * 1024, scale=scale, loads=loads))

    WB_MISC = len(blocks)
    add_tm([(3856, 256), (4624, 64), (2560, 16), (4688, 8)], 1.0)
    XS0 = 1024
    add_fm(w_in_v, [[(XS0 + 128 * i, 128)] for i in range(0, 4)], 1.0)
    add_fm(w_in_v, [[(XS0 + 128 * i, 128)] for i in range(4, 8)], 1.0)
    add_fm(w_in_v, [[(2048, 128)], [(2176, 128)], [(2304, 128)], [(2432, 128)]], 1.0)
    QPAIRS = [(0, 4), (1, 5), (2, 6), (3, 7), (8, 12), (9, 13), (10, 14), (11, 15)]
    add_fm(w_in_v, [[(2576 + 64 * a, 64), (2576 + 64 * b, 64)] for (a, b) in QPAIRS[0:4]], 0.125)
    add_fm(w_in_v, [[(2576 + 64 * a, 64), (2576 + 64 * b, 64)] for (a, b) in QPAIRS[4:8]], 0.125)
    add_fm(w_in_v, [[(3600, 128)], [(3728, 128)], [(4112, 128)], [(4240, 128)]], 1.0)
    add_fm(w_in_v, [[(4368, 128)], [(4496, 128)]], 1.0)
    WB_FM = 1
    WB_TM = len(blocks)
    add_tm([(0, 512)], 0.5)
    add_tm([(512, 512)], 0.5)
    WB_OUT = len(blocks)
    for c in range(2):
        blocks.append(dict(np_=128, n=4096, scale=1.0, loads=[(lambda s: v3(s, 8, 512), w_out_s[:, :, c * 512:(c + 1) * 512])]))
        cols = slice(c * 512, (c + 1) * 512)
        blocks.append(dict(np_=128, n=4096, scale=1.0, loads=[
            (lambda s: v3(s, 8, 512)[0:64, 0:4, :], w_out_a[:, 0:4, cols]),
            (lambda s: v3(s, 8, 512)[0:64, 4:8, :], w_out_a[:, 8:12, cols]),
            (lambda s: v3(s, 8, 512)[64:128, 0:4, :], w_out_a[:, 4:8, cols]),
            (lambda s: v3(s, 8, 512)[64:128, 4:8, :], w_out_a[:, 12:16, cols])]))
    WB_UP = len(blocks)
    for b_ in range(11):
        add_fm(w_up_v, [[(256 * b_, 128)], [(256 * b_ + 128, 128)], [(2816 + 256 * b_, 128)], [(2816 + 256 * b_ + 128, 128)]], 1.0)
    WB_DN = len(blocks)
    FCG = [(0, 8), (8, 16), (16, 22)]
    for c in range(2):
        for (f0, f1) in FCG:
            nf = f1 - f0
            blocks.append(dict(np_=128, n=nf * 512, scale=1.0,
                               loads=[(lambda s, nf=nf: v3(s, nf, 512), w_down_v[:, f0:f1, c * 512:(c + 1) * 512])]))
    assert len(blocks) == NWB, len(blocks)

    cast_engs = ["dve", "act", "dve"]
    for k, blk in enumerate(blocks):
        s32 = stg32[k % 2]
        s16 = stg16[k % 2]
        np_, n = blk["np_"], blk["n"]
        for (dstfn, src) in blk["loads"]:
            dma(dstfn(s32), src, [], [("arena", "s32", k % 2)])
        eng = cast_engs[k % 3]
        sc = blk["scale"]
        if eng == "act":
            act(s16[0:np_, 0:n], s32[0:np_, 0:n], AF.Copy, [("arena", "s32", k % 2)], [("arena", "s16", k % 2)], scale=sc)
        else:
            ts(eng, s16[0:np_, 0:n], s32[0:np_, 0:n], sc, 0.0, ALU.mult, ALU.add, [("arena", "s32", k % 2)], [("arena", "s16", k % 2)])
        dma(wsc[k, 0:np_, 0:n], s16[0:np_, 0:n], [("arena", "s16", k % 2)], [("wsc", k)])

    wseq = [0]
    wuse = [0]

    def wload_upto(g):
        while wseq[0] <= g:
            q = wseq[0]
            k = q % NWB
            blk = blocks[k]
            np_, n = blk["np_"], blk["n"]
            dma(Wr[q % 2][0:np_, 0:n], wsc[k, 0:np_, 0:n], [("wsc", k)], [("W", q % 2)])
            wseq[0] += 1

    def wnext(expect_k):
        g = wuse[0]
        assert g % NWB == expect_k, (g % NWB, expect_k)
        wload_upto(g + 1)
        wuse[0] += 1
        return Wr[g % 2], ("W", g % 2)

    total_q = NB * NQ
    wload_upto(0)

    bank_rr = [0]

    def nb(lo=0, hi=8):
        k = lo + bank_rr[0] % (hi - lo)
        bank_rr[0] += 1
        return k

    for b in range(NB):
        mset("pool", halo_s[:], 0.0, ["halo_s"])
        mset("pool", halo_f[:], 0.0, ["halo_f"])
        mset("pool", Sst[:], 0.0, ["Sst"])
        mset("pool", Sb[:], 0.0, ["Sb"])
        for j in range(NQ):
            t0 = 512 * j
            phase(2)
            arena_reset()
            xsT = ab([128, 8, 512], F32)
            normgB = ab([128, 1024], F32)
            raw = [ab([128, 515], F32) for _ in range(2)]
            cacc = [ab([128, 512], F32) for _ in range(2)]
            ctmp = [ab([128, 512], F32) for _ in range(2)]
            LH8 = ab([128, 8, 128], F32)
            E8 = ab([128, 8, 128], F32)
            CBm = ab([128, 2, 128], F32)
            MT = ab([128, 16, 128], BF16)
            xs_tm = ab([128, 1024], F32)
            B_tm = ab([128, 2, 128], BF16)
            xdt = ab([128, 1024], BF16)
            xdtd = ab([128, 1024], BF16)
            ytmp = ab([128, 1024], F32)
            ynb = ab([128, 1024], BF16)
            junk = ab([128, 512], F32)
            ssm = ab([128, 64], F32)
            AR = lambda n, *r: ("arena", n) + tuple(r)

            dma(normgB[:], ssd_norm_g.broadcast_to([128, 1024]), [], [AR("normgB")])
            if b == 0 and j == 0:
                dma(A[:], x[b, t0:t0 + 512, :].rearrange("(i p) d -> p i d", p=128), [], [("A", i) for i in range(4)])
                for i in range(4):
                    xb = xb16[i % 2]
                    cp("pool" if i % 2 == 0 else "dve", xb[:], A[:, i, :], [("A", i)], [("xb16", i % 2)])
                    bk = i % 2
                    for kc in range(8):
                        tr(bankbf[bk][:, 128 * kc:128 * kc + 128], xb[:, 128 * kc:128 * kc + 128], ident[:], [("xb16", i % 2), "ident"], [BK[bk]])
                    cp("act", xT[:, :, 128 * i:128 * i + 128], bankbf[bk][:, 0:1024].rearrange("p (kc t) -> p kc t", kc=8), [BK[bk]], [("xT", i)])
            XTK = [("xT", i) for i in range(4)]

            phase(3)
            Wt, Wk = wnext(WB_MISC)
            Wv = v3(Wt, 8, 344)
            for i in range(4):
                bk = nb(2, 8)
                for kc in range(8):
                    mm(banks[bk][:, 0:344], xT[:, kc, 128 * i:128 * i + 128], Wv[:, kc, :], kc == 0, kc == 7, [("xT", i), Wk], [BK[bk]])
                cp("act", V[:, 4 * j + i, 0:64], banks[bk][:, 0:64], [BK[bk]], [("V", j)])
                cp("act", V[:, 4 * j + i, 128:256], banks[bk][:, 64:192], [BK[bk]], [("V", j)])
                cp("act", V[:, 4 * j + i, 320:384], banks[bk][:, 192:256], [BK[bk]], [("V", j)])
                cp("dve", kif[:, i, :], banks[bk][:, 256:320], [BK[bk]], ["kif"])
                cp("dve", dtw[:, i, :], banks[bk][:, 320:344], [BK[bk]], ["dtw"])
            tt("dve", dtx[:], dtw[:, :, 0:16], dtbB[:].unsqueeze(1).broadcast_to([128, 4, 16]), ALU.add, ["dtw", "allc"], ["dtx"])
            stt("dve", dtl[:], dtx[:], -1.0, dtx[:], ALU.mult, ALU.max, ["dtx"], ["dtl"])
            act(dtl[:], dtl[:], AF.Exp, ["dtl"], ["dtl"], scale=-1.0)
            act(dtl[:], dtl[:], AF.Ln, ["dtl"], ["dtl"], bias=1.0)
            stt("dve", dt_t[:], dtx[:], 0.0, dtl[:], ALU.max, ALU.add, ["dtx", "dtl"], ["dt_t"])
            tt("dve", dtA_t[:], dt_t[:], aB[:].unsqueeze(1).broadcast_to([128, 4, 16]), ALU.mult, ["dt_t", "allc"], ["dtA_t"])
            stt("dve", absw[:], dtw[:, :, 16:24], -1.0, dtw[:, :, 16:24], ALU.mult, ALU.max, ["dtw"], ["absw"])
            ts("dve", sgnw[:], dtw[:, :, 16:24], 0.0, 2.0, ALU.is_ge, ALU.mult, ["dtw"], ["sgnw"])
            ts("dve", sgnw[:], sgnw[:], -1.0, 0.0, ALU.add, ALU.add, ["sgnw"], ["sgnw"])
            red("dve", sm4[:, :, 0], kif[:], ALU.add, ["kif"], ["sm4"])
            ts("dve", sm4[:, :, 1], sm4[:, :, 0], 1.0 / 64, 0.0, ALU.mult, ALU.add, ["sm4"], ["sm4"])
            tt("dve", kixc[:], kif[:], sm4[:, :, 1:2].broadcast_to([128, 4, 64]), ALU.subtract, ["kif", "sm4"], ["kixc"])
            tt("dve", kif[:], kixc[:], kixc[:], ALU.mult, ["kixc"], ["kif"])
            red("dve", sm4[:, :, 2], kif[:], ALU.add, ["kif"], ["sm4"])
            ts("dve", sm4[:, :, 3], sm4[:, :, 2], 1.0 / 64, 1e-5, ALU.mult, ALU.add, ["sm4"], ["sm4"])
            act(sm4[:, :, 3], sm4[:, :, 3], AF.Ln, ["sm4"], ["sm4"])
            act(sm4[:, :, 4], sm4[:, :, 3], AF.Exp, ["sm4"], ["sm4"], scale=-0.5)
            tt("dve", kixc[:], kixc[:], sm4[:, :, 4:5].broadcast_to([128, 4, 64]), ALU.mult, ["kixc", "sm4"], ["kixc"])
            tt("dve", kixc[:], kixc[:], kgB[:].unsqueeze(1).broadcast_to([128, 4, 64]), ALU.mult, ["kixc", "allc"], ["kixc"])
            for hf in range(2):
                tt("dve", kidup[:, :, 64 * hf:64 * hf + 64], kixc[:], kbB[:].unsqueeze(1).broadcast_to([128, 4, 64]), ALU.add, ["kixc", "allc"], ["kidup"])
            def fm_chunk(Wt, Wk, ci):
                bk = nb(2, 8)
                Wv = v4(Wt)
                for kc in range(8):
                    mm(banks[bk][:, :], Wv[:, ci, kc, :], xT[:, kc, :], kc == 0, kc == 7, XTK + [Wk], [BK[bk]])
                return bk

            pend = []

            def conv_silu(bk, xi, dst, dkey):
                r = raw[xi % 2]
                rk = AR("raw", xi % 2)
                cp("act", r[:, 3:515], banks[bk][:, :], [BK[bk]], [rk])
                cp("pool", r[:, 0:3], halo_s[:, xi, :], ["halo_s"], [rk])
                cp("pool", halo_s[:, xi, :], r[:, 512:515], [rk], ["halo_s"])
                ca = cacc[xi % 2]
                ck = AR("cacc", xi % 2)
                act(ca[:], banks[bk][:, :], AF.Identity, [BK[bk], "allc"], [ck], scale=cw[:, 36 + xi:36 + xi + 1], bias=cb[:, xi:xi + 1])
                for k in range(3):
                    stt("dve", ca[:], r[:, k:k + 512], cw[:, 12 * k + xi:12 * k + xi + 1], ca[:], ALU.mult, ALU.add, [rk, ck, "allc"], [ck])
                ct = ctmp[xi % 2]
                tk = AR("ctmp", xi % 2)

                def fin(ct=ct, ca=ca, ck=ck, tk=tk, dst=dst, dkey=dkey):
                    act(ct[:], ca[:], AF.Tanh, [ck], [tk])
                    stt("dve", dst, ct[:], 1.0, ca[:], ALU.add, ALU.mult, [tk, ck], [dkey])
                if pend:
                    pend.pop(0)()
                pend.append(fin)

            for blk_i in range(2):
                Wt, Wk = wnext(WB_FM + blk_i)
                for ci in range(4):
                    xi = 4 * blk_i + ci
                    bk = fm_chunk(Wt, Wk, ci)
                    conv_silu(bk, xi, xsT[:, xi, :], AR("xsT", xi))
            Wt, Wk = wnext(WB_FM + 2)
            for ci in range(4):
                xi = 8 + ci
                bk = fm_chunk(Wt, Wk, ci)
                if ci < 2:
                    conv_silu(bk, xi, BT[:, ci, :], ("BT", ci))
                else:
                    conv_silu(bk, xi, CT[:, ci - 2, :], ("CT", ci - 2))
            while pend:
                pend.pop(0)()
            for blk_i in range(2):
                Wt, Wk = wnext(WB_FM + 3 + blk_i)
                for ci in range(4):
                    bk = fm_chunk(Wt, Wk, ci)
                    qc = 4 * blk_i + ci
                    cp("act" if ci % 2 == 0 else "dve", qT[:, qc, :], banks[bk][:, :], [BK[bk]], [("qT", qc)])
            Wt, Wk = wnext(WB_FM + 5)
            for ci in range(4):
                bk = fm_chunk(Wt, Wk, ci)
                if ci < 2:
                    cp("act", kT[:, ci, t0:t0 + 512], banks[bk][:, :], [BK[bk]], [("kT", j)])
                else:
                    cp("dve", qiT[:, ci - 2, :], banks[bk][:, :], [BK[bk]], [("qiT", ci - 2)])
            Wt, Wk = wnext(WB_FM + 6)
            for ci in range(2):
                bk = fm_chunk(Wt, Wk, ci)
                cp("act", qiT[:, 2 + ci, :], banks[bk][:, :], [BK[bk]], [("qiT", 2 + ci)])

            for c in range(2):
                Wt, Wk = wnext(WB_TM + c)
                Wv = v3(Wt, 8, 512)
                for i in range(4):
                    bk = nb(2, 8)
                    for kc in range(8):
                        mm(banks[bk][:, :], xT[:, kc, 128 * i:128 * i + 128], Wv[:, kc, :], kc == 0, kc == 7, [("xT", i), Wk], [BK[bk]])
                    dst = A[:, i, 512 * c:512 * c + 512]
                    act(dst, banks[bk][:, :], AF.Tanh, [BK[bk]], [("A", i)])
                    stt("dve", dst, dst, 1.0, banks[bk][:, :], ALU.add, ALU.mult, [("A", i), BK[bk]], [("A", i)])
            bk = nb(0, 2)
            for i in range(4):
                tr(bankbf[bk][:, 128 * i:128 * i + 128], kidup[:, i, :], ident[:], ["kidup", "ident"], [BK[bk]])
            cp("act", kiT[:, t0:t0 + 512], bankbf[bk][:, 0:512], [BK[bk]], [("kiT", j)])

            phase(4)
            for c in range(4):
                cs = slice(128 * c, 128 * c + 128)
                dtA = dtA_t[:, c, :]
                dtc = dt_t[:, c, :]
                mm(banks[3][:, 0:16], triu[:], dtA, True, True, ["triu", "dtA_t"], [BK[3]])
                mm(banks[3][:, 16:32], onesf[:], dtA, True, True, ["onesf", "dtA_t"], [BK[3]])
                act(ssm[:, 0:32], banks[3][:, 0:32], AF.Exp, [BK[3]], [AR("ssm")])
                eacs = ssm[:, 0:16]
                cdB = ssm[:, 16:32]
                for g in range(2):
                    mm(banks[2][:, 128 * g:128 * g + 128], BT[:, g, cs], CT[:, g, cs], True, True, [("BT", g), ("CT", g)], [BK[2]])
                tt("dve", CBm[:], banks[2][:, 0:256].rearrange("p (g l) -> p g l", g=2), triu[:].unsqueeze(1).broadcast_to([128, 2, 128]), ALU.mult, [BK[2], "triu"], [AR("CBm")])
                for hh in range(2):
                    tt("dve", LH8[:], strict[:].unsqueeze(1).broadcast_to([128, 8, 128]),
                       dtA[:, 8 * hh:8 * hh + 8].unsqueeze(2).broadcast_to([128, 8, 128]), ALU.mult, ["strict", "dtA_t"], [AR("LH8")])
                    for h8 in range(8):
                        bk = h8 // 4
                        mm(banks[bk][:, 128 * (h8 % 4):128 * (h8 % 4) + 128], LH8[:, h8, :], triu[:], True, True, [AR("LH8"), "triu"], [BK[bk]])
                    for bk in range(2):
                        act(E8[:, 4 * bk:4 * bk + 4, :], banks[bk][:, :].rearrange("p (h l) -> p h l", h=4), AF.Exp, [BK[bk]], [AR("E8")])
                    cp("dve", ssm[:, 32 + 8 * hh:32 + 8 * hh + 8], E8[:, :, 127], [AR("E8")], [AR("ssm")])
                    tt("dve", MT[:, 8 * hh:8 * hh + 8, :], E8[:], CBm[:, hh:hh + 1, :].broadcast_to([128, 8, 128]), ALU.mult, [AR("E8"), AR("CBm")], [AR("MT")])
                dte = ssm[:, 32:48]
                for fc in range(8):
                    bk = 4 + fc // 4
                    tr(banks[bk][:, 128 * (fc % 4):128 * (fc % 4) + 128], xsT[:, fc, cs], identf[:], [AR("xsT", fc), "identf"], [BK[bk]])
                for bk in (4, 5):
                    cp("act", xs_tm[:, 512 * (bk - 4):512 * (bk - 4) + 512], banks[bk][:, :], [BK[bk]], [AR("xs_tm")])
                for g in range(2):
                    tr(bankbf[3][:, 128 * g:128 * g + 128], BT[:, g, cs], ident[:], [("BT", g), "ident"], [BK[3]])
                cp("act", B_tm[:], bankbf[3][:, 0:256].rearrange("p (g n) -> p g n", g=2), [BK[3]], [AR("B_tm")])
                xs3 = xs_tm[:].rearrange("p (h d) -> p h d", h=16)
                tt("dve", xdt[:].rearrange("p (h d) -> p h d", h=16), xs3, dtc.unsqueeze(2).broadcast_to([128, 16, 64]), ALU.mult, [AR("xs_tm"), "dt_t"], [AR("xdt")])
                tt("dve", ssm[:, 48:64], dtc, dte, ALU.mult, ["dt_t", AR("ssm")], [AR("ssm")])
                tt("dve", xdtd[:].rearrange("p (h d) -> p h d", h=16), xs3, ssm[:, 48:64].unsqueeze(2).broadcast_to([128, 16, 64]), ALU.mult, [AR("xs_tm"), AR("ssm")], [AR("xdtd")])
                for h in range(16):
                    bk = 4 + h // 8
                    mm(banks[bk][:, 64 * (h % 8):64 * (h % 8) + 64], MT[:, h, :], xdt[:, 64 * h:64 * h + 64], True, True, [AR("MT"), AR("xdt")], [BK[bk]])
                for g in range(2):
                    mm(banks[6 + g][:, :], CT[:, g, cs], Sb[:, 512 * g:512 * g + 512], True, True, [("CT", g), "Sb"], [BK[6 + g]])
                for g in range(2):
                    mm(banks[g][:, :], B_tm[:, g, :], xdtd[:, 512 * g:512 * g + 512], True, True, [AR("B_tm"), AR("xdtd")], [BK[g]])
                S3 = Sst[:].rearrange("p (h d) -> p h d", h=16)
                tt("dve", S3, S3, cdB.unsqueeze(2).broadcast_to([128, 16, 64]), ALU.mult, ["Sst", AR("ssm")], ["Sst"])
                for g in range(2):
                    tt("dve", Sst[:, 512 * g:512 * g + 512], Sst[:, 512 * g:512 * g + 512], banks[g][:, :], ALU.add, ["Sst", BK[g]], ["Sst"])
                cp("act", Sb[:], Sst[:], ["Sst"], ["Sb"])
                for g in range(2):
                    ysl = ytmp[:, 512 * g:512 * g + 512]
                    tt("dve", ysl.rearrange("p (h d) -> p h d", h=8), banks[6 + g][:, :].rearrange("p (h d) -> p h d", h=8),
                       eacs[:, 8 * g:8 * g + 8].unsqueeze(2).broadcast_to([128, 8, 64]), ALU.mult, [BK[6 + g], AR("ssm")], [AR("ytmp")])
                    tt("dve", ysl, ysl, banks[4 + g][:, :], ALU.add, [AR("ytmp"), BK[4 + g]], [AR("ytmp")])
                tt("dve", xs3, xs3, dskB[:].unsqueeze(2).broadcast_to([128, 16, 64]), ALU.mult, [AR("xs_tm"), "allc"], [AR("xs_tm")])
                tt("dve", ytmp[:], ytmp[:], xs_tm[:], ALU.add, [AR("ytmp"), AR("xs_tm")], [AR("ytmp")])
                tt("dve", ytmp[:], ytmp[:], A[:, c, :], ALU.mult, [AR("ytmp"), ("A", c)], [AR("ytmp")])
                for g in range(2):
                    act(junk[:], ytmp[:, 512 * g:512 * g + 512], AF.Square, [AR("ytmp")], [AR("junk"), AR("ssm")], accum_out=ssm[:, 60 + g:61 + g])
                ts("dve", ssm[:, 60:62], ssm[:, 60:62], 1.0 / 512, 1e-5, ALU.mult, ALU.add, [AR("ssm")], [AR("ssm")])
                act(ssm[:, 60:62], ssm[:, 60:62], AF.Ln, [AR("ssm")], [AR("ssm")])
                act(ssm[:, 62:64], ssm[:, 60:62], AF.Exp, [AR("ssm")], [AR("ssm")], scale=-0.5)
                for g in range(2):
                    stt("dve", ynb[:, 512 * g:512 * g + 512], ytmp[:, 512 * g:512 * g + 512], ssm[:, 62 + g:63 + g], normgB[:, 512 * g:512 * g + 512],
                        ALU.mult, ALU.mult, [AR("ytmp"), AR("ssm"), AR("normgB")], [AR("ynb")])
                for fc in range(8):
                    tr(bankbf[2][:, 128 * fc:128 * fc + 128], ynb[:, 128 * fc:128 * fc + 128], ident[:], [AR("ynb"), "ident"], [BK[2]])
                cp("act", yssdT[:, :, cs], bankbf[2][:, 0:1024].rearrange("p (f t) -> p f t", f=8), [BK[2]], [("yssdT", c)])
            if debug:
                dma(dbg["yssd"][b, j], yssdT[:], [("yssdT", c) for c in range(4)], ["dbg_yssd"])

            phase(5)
            arena_reset()
            maskT = ab([128, 32, 512], BF16)
            score = ab([128, 4096], F32)
            rbuf = [ab([128, 512], BF16) for _ in range(4)]
            Dg = ab([128, 8, 128], BF16)
            mrow = [ab([128, 512], BF16) for _ in range(2)]
            qiTz = ab([128, 8, 512], BF16)
            Wcol = ab([128, NIT + 2], F32)
            cnt = ab([128, NIT + 2], F32)
            cnta = ab([128, NIT + 2], F32)
            bsm = ab([128, 8], F32)
            jnk = ab([128, 2], F32)
            NKB = 4 * j + 4
            scoreb = [score, A[:].rearrange("p a b -> p (a b)")]
            skeys = [[AR("score")], [("A", i_) for i_ in range(4)]]
            mset("pool", qiTz[:], 0.0, [AR("qiTz")])
            for h in range(8):
                hp = 64 * (h % 2)
                cp("act" if h % 2 else "pool", qiTz[hp:hp + 64, h, :], qiT[hp:hp + 64, h // 2, :], [("qiT", h // 2), AR("qiTz")], [AR("qiTz", h)])
            KIK = [("kiT", jj) for jj in range(j + 1)]

            def relu_gen(i):
                n_keys = 128 * (4 * j + i + 1)
                nkt = (n_keys + 511) // 512
                sc = scoreb[i % 2]
                sk = skeys[i % 2]
                tt("pool", Dg[:], ident[:].unsqueeze(1).broadcast_to([128, 8, 128]), sgnw[:, i, :].unsqueeze(2).broadcast_to([128, 8, 128]), ALU.mult,
                   ["ident", "sgnw"], [AR("Dg")])
                items = [(kt, h) for kt in range(nkt) for h in range(8)]

                def ifront(q):
                    kt, h = items[q]
                    w = min(512, n_keys - 512 * kt)
                    bk = q % 4
                    mm(banks[bk][:, 0:w], qiTz[:, h, 128 * i:128 * i + 128], kiT[:, 512 * kt:512 * kt + w], True, True,
                       [AR("qiTz", h)] + KIK, [BK[bk]])
                    if q % 4 != 3:
                        act(rbuf[q % 4][:, 0:w], banks[bk][:, 0:w], AF.Relu, [BK[bk], "absw"], [AR("rbuf", q % 4)], scale=absw[:, i, h:h + 1])
                    else:
                        ts("dve", rbuf[q % 4][:, 0:w], banks[bk][:, 0:w], absw[:, i, h:h + 1], 0.0, ALU.mult, ALU.max, [BK[bk], "absw"], [AR("rbuf", q % 4)])

                def iback(q):
                    kt, h = items[q]
                    w = min(512, n_keys - 512 * kt)
                    ab_ = 4 + kt % 2
                    mm(banks[ab_][:, 0:w], Dg[:, h, :], rbuf[q % 4][:, 0:w], h == 0, h == 7, [AR("Dg"), AR("rbuf", q % 4)], [BK[ab_]])
                    if h == 7:
                        cp("dve", sc[:, 512 * kt:512 * kt + w], banks[ab_][:, 0:w], [BK[ab_]], sk)

                ILA = 2
                for q in range(len(items) + ILA):
                    if q < len(items):
                        ifront(q)
                    if q - ILA >= 0:
                        iback(q - ILA)
                    yield

            def bisect_gen(i, out):
                n_kb = 4 * j + i + 1
                n_keys = 128 * n_kb
                sc = scoreb[i % 2]
                sk = skeys[i % 2]
                out["thr"] = negthr[:, 0:1]
                out["key"] = "negthr"
                if n_kb >= 3:
                    red("dve", bsm[:, 0:1], sc[:, 0:n_keys], ALU.max, sk, [AR("bsm")], absval=True)
                dsl = sc[:, 128 * (n_kb - 1):128 * n_kb]
                tt("dve", dsl, dsl, negtri[:], ALU.add, sk + ["negtri"], sk)
                if n_kb < 3:
                    return
                ts("dve", Wcol[:], pow2c[:], bsm[:, 0:1], 0.0, ALU.mult, ALU.add, ["pow2c", AR("bsm")], [AR("Wcol")])
                mset("dve", bsm[:, 1:2], 0.0, [AR("bsm")])
                mid = bsm[:, 1:2]
                n_dve = (n_keys * 40 // 100) // 64 * 64
                n_act = n_keys - n_dve
                cthr = 511.0 - n_act
                yield
                for it in range(NIT):
                    ts("dve", jnk[:, 0:1].broadcast_to([128, n_dve]), sc[:, 0:n_dve], mid, 0.0, ALU.is_ge, ALU.add, sk + [AR("bsm")], [AR("jnk", 0), AR("cnt")], accum=cnt[:, it:it + 1])
                    act(jnk[:, 1:2].broadcast_to([128, n_act]), sc[:, n_dve:n_keys], AF.Sign, sk + [AR("bsm")], [AR("jnk", 1), AR("cnta")], scale=-1.0, bias=mid, accum_out=cnta[:, it:it + 1])
                    stt("dve", bsm[:, 6:7], cnt[:, it:it + 1], 2.0, cnta[:, it:it + 1], ALU.mult, ALU.subtract, [AR("cnt"), AR("cnta")], [AR("bsm")])
                    if it < NIT - 1:
                        ts("dve", bsm[:, 2:3], bsm[:, 6:7], cthr, Wcol[:, it:it + 1], ALU.is_ge, ALU.mult, [AR("bsm"), AR("Wcol")], [AR("bsm")])
                        stt("dve", mid, bsm[:, 2:3], Wcol[:, it + 1:it + 2], mid, ALU.subtract, ALU.add, [AR("bsm"), AR("Wcol")], [AR("bsm")])
                    else:
                        ts("dve", bsm[:, 2:3], bsm[:, 6:7], cthr, -1.0, ALU.is_ge, ALU.add, [AR("bsm")], [AR("bsm")])
                        stt("dve", bsm[:, 3:4], bsm[:, 2:3], Wcol[:, it:it + 1], mid, ALU.mult, ALU.add, [AR("bsm"), AR("Wcol")], [AR("bsm")])
                    yield
                out["thr"] = bsm[:, 3:4]
                out["key"] = AR("bsm")
                if debug:
                    cp("dve", bsm[:, 4:5], bsm[:, 3:4], [AR("bsm")], [AR("bsm")])
                    cp("dve", bsm[:, 5:6], bsm[:, 6:7], [AR("bsm")], [AR("bsm")])
                    dma(dbg["thr"][b, j, i], bsm[:, 4:6], [AR("bsm")], ["dbg_thr"])

            def mask_gen(i, thr, thr_key):
                n_kb = 4 * j + i + 1
                n_keys = 128 * n_kb
                nkt = (n_keys + 511) // 512
                sc = scoreb[i % 2]
                sk = skeys[i % 2]
                for kt in range(nkt):
                    w = min(512, n_keys - 512 * kt)
                    nblk = w // 128
                    mr = mrow[kt % 2]
                    ts("dve", mr[:, 0:w], sc[:, 512 * kt:512 * kt + w], thr, 0.0, ALU.is_ge, ALU.add, sk + [thr_key], [AR("mrow", kt % 2)])
                    bk = 6 + kt % 2
                    for q in range(nblk):
                        tr(bankbf[bk][:, 128 * q:128 * q + 128], mr[:, 128 * q:128 * q + 128], ident[:], [AR("mrow", kt % 2), "ident"], [BK[bk]])
                    cp("act", maskT[:, 4 * kt:4 * kt + nblk, 128 * i:128 * i + 128], bankbf[bk][:, 0:128 * nblk].rearrange("p (q t) -> p q t", q=nblk),
                       [BK[bk]], [AR("maskT", i)])
                if n_kb < NKB:
                    mset("pool", maskT[:, n_kb:NKB, 128 * i:128 * i + 128], 0.0, [AR("maskT", i)])

            for _ in relu_gen(0):
                pass
            for i in range(4):
                res_ = {}
                bg = bisect_gen(i, res_)
                rg = relu_gen(i + 1) if i < 3 else iter(())
                n_r = (8 * ((128 * (4 * j + i + 2) + 511) // 512) + 2) if i < 3 else 0
                per = max(1, -(-n_r // (NIT + 1)))
                b_done = False
                r_done = (i == 3)
                while not (b_done and r_done):
                    if not b_done:
                        try:
                            next(bg)
                        except StopIteration:
                            b_done = True
                    for _ in range(per if not b_done else 10 ** 6):
                        if r_done:
                            break
                        try:
                            next(rg)
                        except StopIteration:
                            r_done = True
                mask_gen(i, res_["thr"], res_["key"])
            dma(A[:], x[b, t0:t0 + 512, :].rearrange("(i p) d -> p i d", p=128), [], [("A", i) for i in range(4)])
            arena_reset()
            maskT = ab([128, 32, 512], BF16)
            NEP = 4
            Eall = ab([128, 2 * NEP, 512], BF16)
            accS = [ab([128, 512], F32) for _ in range(2)]
            rq = ab([128, 8], F32)
            qTz = ab([128, 16, 512], BF16)
            mset("pool", qTz[:], 0.0, [AR("qTz")])
            for qc_ in range(8):
                for sd in range(2):
                    cp("act" if sd else "pool", qTz[64 * sd:64 * sd + 64, 2 * qc_ + sd, :], qT[64 * sd:64 * sd + 64, qc_, :], [("qT", qc_), AR("qTz")], [AR("qTz", 2 * qc_ + sd)])
            MK = [AR("maskT", i) for i in range(4)]
            KTK = [("kT", jj) for jj in range(j + 1)]
            VK = [("V", jj) for jj in range(j + 1)] + ["Vones"]
            VOFF = [0, 64, 192, 256]
            items = [(qc, kb) for qc in range(8) for kb in range(NKB)]
            LA = 2
            sctr = [0]
            islot = {}

            def front(it):
                qc, kb = items[it]
                a_, b_h = QPAIRS[qc]
                slot = (a_ // 4) // 2
                sp = sctr[0] % 3
                sctr[0] += 1
                islot[it] = sp
                c0 = 128 * max(0, kb - 4 * j)
                cw = 512 - c0
                for side in range(2):
                    mm(banks[2 * sp + side][:, c0:512], kT[:, slot, 128 * kb:128 * kb + 128], qTz[:, 2 * qc + side, c0:512], True, True,
                       KTK + [AR("qTz", 2 * qc + side)], [BK[2 * sp + side]])
                ep = it % NEP
                ek = AR("E", ep)
                e2 = Eall[:, 2 * ep:2 * ep + 2, c0:512]
                act(e2, psall[:, 1024 * sp:1024 * sp + 1024].rearrange("p (a b) -> p a b", a=2)[:, :, c0:512], AF.Exp, [BK[2 * sp], BK[2 * sp + 1]], [ek])
                tt("dve", e2, e2, maskT[:, kb:kb + 1, c0:512].broadcast_to([128, 2, cw]), ALU.mult, [ek] + MK, [ek])

            def back(it):
                qc, kb = items[it]
                ep = it % NEP
                ek = AR("E", ep)
                for side in range(2):
                    h = QPAIRS[qc][side]
                    n = h // 4
                    c0 = 128 * max(0, kb - 4 * j)
                    mm(banks[6 + side][:, c0:512], V[:, kb, VOFF[n]:VOFF[n] + 128], Eall[:, 2 * ep + side, c0:512], kb == 0, kb == NKB - 1, VK + [ek], [BK[6 + side]])
                if kb == NKB - 1:
                    for side in range(2):
                        cp("act", accS[side][:], banks[6 + side][:, :], [BK[6 + side]], [AR("accS", side)])
                    box = {}

                    def st_den(qc=qc, box=box):
                        sp = sctr[0] % 3
                        sctr[0] += 1
                        for side in range(2):
                            selc = identf[:, 64:65] if side == 0 else identf[:, 0:1]
                            for i4 in range(4):
                                mm(banks[2 * sp][:, 4 * side + i4:4 * side + i4 + 1], accS[side][:, 128 * i4:128 * i4 + 128], selc, True, True,
                                   ["identf", AR("accS", side)], [BK[2 * sp]])
                        S.op("dve", lambda e, sp=sp: e.reciprocal(out=rq[:, 0:8], in_=banks[2 * sp][:, 0:8]), [BK[2 * sp]], [AR("rq")])

                    def st_bc(qc=qc, box=box):
                        sp = sctr[0] % 3
                        sctr[0] += 1
                        box["sp"] = sp
                        for side in range(2):
                            M = 64 if side == 0 else 128
                            for i4 in range(4):
                                mm(banks[2 * sp + side][0:M, 128 * i4:128 * i4 + 128], rq[:, 4 * side + i4:4 * side + i4 + 1].broadcast_to([128, M]), identf[:, :], True, True,
                                   ["identf", AR("rq")], [BK[2 * sp + side]])

                    def st_fin(qc=qc, box=box):
                        sp = box["sp"]
                        for side in range(2):
                            h = QPAIRS[qc][side]
                            olo = 0 if side == 0 else 64
                            tt("dve", yattT[olo:olo + 64, qc, :], accS[side][olo:olo + 64, :], banks[2 * sp + side][olo:olo + 64, :], ALU.mult,
                               [AR("accS", side), BK[2 * sp + side]], [("yattT", h)])

                    d1 = max(1, min(3, NKB - 3))
                    d2 = max(d1 + 1, min(5, NKB - 2))
                    deferred.append((it + LA + d1, st_den))
                    deferred.append((it + LA + d2, st_bc))
                    deferred.append((it + LA + d2 + 1, st_fin))

            deferred = []
            for it in range(len(items) + LA):
                if it < len(items):
                    front(it)
                if it - LA >= 0:
                    back(it - LA)
                while deferred and deferred[0][0] <= it:
                    deferred.pop(0)[1]()
            while deferred:
                deferred.pop(0)[1]()
            if debug:
                dma(dbg["yatt"][b, j], yattT[:], [("yattT", h) for h in range(16)], ["dbg_yatt"])

            phase(6)
            arena_reset()
            lnG = ab([128, 1024], F32)
            lnB = ab([128, 1024], F32)
            junk5 = ab([128, 1024], F32)
            lsm = ab([128, 4, 8], F32)
            dma(lnG[:], ln1_g.broadcast_to([128, 1024]), [], [AR("lnG")])
            dma(lnB[:], ln1_b.broadcast_to([128, 1024]), [], [AR("lnB")])

            def layer_norm_blocks(final):
                for i in range(4):
                    act(junk5[:], A[:, i, :], AF.Identity, [("A", i)], [AR("junk5"), AR("lsm")], accum_out=lsm[:, i, 0:1])
                    act(junk5[:], A[:, i, :], AF.Square, [("A", i)], [AR("junk5"), AR("lsm")], accum_out=lsm[:, i, 1:2])
                ts("dve", lsm[:, :, 2], lsm[:, :, 0], 1.0 / 1024, 0.0, ALU.mult, ALU.add, [AR("lsm")], [AR("lsm")])
                tt("dve", lsm[:, :, 3], lsm[:, :, 2], lsm[:, :, 2], ALU.mult, [AR("lsm")], [AR("lsm")])
                stt("dve", lsm[:, :, 4], lsm[:, :, 1], 1.0 / 1024, lsm[:, :, 3], ALU.mult, ALU.subtract, [AR("lsm")], [AR("lsm")])
                ts("dve", lsm[:, :, 4], lsm[:, :, 4], 1e-5, 0.0, ALU.add, ALU.add, [AR("lsm")], [AR("lsm")])
                act(lsm[:, :, 4], lsm[:, :, 4], AF.Ln, [AR("lsm")], [AR("lsm")])
                act(lsm[:, :, 5], lsm[:, :, 4], AF.Exp, [AR("lsm")], [AR("lsm")], scale=-0.5)
                stt("dve", lsm[:, :, 6], lsm[:, :, 2], -1.0, lsm[:, :, 5], ALU.mult, ALU.mult, [AR("lsm")], [AR("lsm")])
                for i in range(4):
                    act(A[:, i, :], A[:, i, :], AF.Identity, [("A", i), AR("lsm")], [("A", i)], scale=lsm[:, i, 5:6], bias=lsm[:, i, 6:7])
                    tt("pool" if i % 2 else "dve", A[:, i, :], A[:, i, :], lnG[:], ALU.mult, [("A", i), AR("lnG")], [("A", i)])
                    tt("pool" if i == 3 else "dve", A[:, i, :], A[:, i, :], lnB[:], ALU.add, [("A", i), AR("lnB")], [("A", i)])
                    if final:
                        dma(out[b, t0 + 128 * i:t0 + 128 * i + 128, :], A[:, i, :], [("A", i)], ["out"])

            for c in range(2):
                Wt, Wk = wnext(WB_OUT + 2 * c)
                Wv = v3(Wt, 8, 512)
                for i in range(4):
                    for fc in range(8):
                        mm(banks[i][:, :], yssdT[:, fc, 128 * i:128 * i + 128], Wv[:, fc, :], fc == 0, False, [("yssdT", i), Wk], [BK[i]])
                Wt, Wk = wnext(WB_OUT + 2 * c + 1)
                Wv = v3(Wt, 8, 512)
                for i in range(4):
                    for qc in range(8):
                        mm(banks[i][:, :], yattT[:, qc, 128 * i:128 * i + 128], Wv[:, qc, :], False, qc == 7,
                           [("yattT", QPAIRS[qc][0]), ("yattT", QPAIRS[qc][1]), Wk], [BK[i]])
                for i in range(4):
                    dst = A[:, i, 512 * c:512 * c + 512]
                    stt("dve", dst, dst, ALPHA, banks[i][:, :], ALU.mult, ALU.add, [("A", i), BK[i]], [("A", i)])
            layer_norm_blocks(False)
            if debug:
                for i in range(4):
                    dma(dbg["h"][b, t0 + 128 * i:t0 + 128 * i + 128, :], A[:, i, :], [("A", i)], ["dbg_h"])
            for i in range(4):
                xb = xb16[i % 2]
                cp("pool" if i % 2 == 0 else "dve", xb[:], A[:, i, :], [("A", i)], [("xb16", i % 2)])
                bk = 4 + i % 2
                for kc in range(8):
                    tr(bankbf[bk][:, 128 * kc:128 * kc + 128], xb[:, 128 * kc:128 * kc + 128], ident[:], [("xb16", i % 2), "ident"], [BK[bk]])
                cp("act", xT[:, :, 128 * i:128 * i + 128], bankbf[bk][:, 0:1024].rearrange("p (kc t) -> p kc t", kc=8), [BK[bk]], [("xT", i)])

            phase(7)
            arena_reset()
            lnG = ab([128, 1024], F32)
            lnB = ab([128, 1024], F32)
            junk5 = ab([128, 1024], F32)
            lsm = ab([128, 4, 8], F32)
            actT = ab([128, 22, 512], BF16)
            rawf = [ab([128, 514], F32) for _ in range(2)]
            facc = [ab([128, 512], F32) for _ in range(2)]
            ftmp = [ab([128, 512], F32) for _ in range(2)]
            gbuf = [ab([128, 512], F32) for _ in range(2)]
            dma(lnG[:], ln2_g.broadcast_to([128, 1024]), [], [AR("lnG")])
            dma(lnB[:], ln2_b.broadcast_to([128, 1024]), [], [AR("lnB")])
            fcount = 0
            fpend = []
            for b_ in range(11):
                Wt, Wk = wnext(WB_UP + b_)
                for ci in range(4):
                    isup = ci >= 2
                    fchunk = 2 * b_ + (ci % 2)
                    cidx = fchunk + (22 if isup else 0)
                    bk = fm_chunk(Wt, Wk, ci)
                    r = rawf[fcount % 2]
                    rk = AR("rawf", fcount % 2)
                    cp("act", r[:, 2:514], banks[bk][:, :], [BK[bk]], [rk])
                    cp("pool", r[:, 0:2], halo_f[:, cidx, :], ["halo_f"], [rk])
                    cp("pool", halo_f[:, cidx, :], r[:, 512:514], [rk], ["halo_f"])
                    fa = facc[fcount % 2]
                    fk = AR("facc", fcount % 2)
                    act(fa[:], banks[bk][:, :], AF.Identity, [BK[bk], "allc"], [fk], scale=cwf[:, 88 + cidx:89 + cidx], bias=cbf[:, cidx:cidx + 1])
                    for k in range(2):
                        stt("dve", fa[:], r[:, k:k + 512], cwf[:, 44 * k + cidx:44 * k + cidx + 1], fa[:], ALU.mult, ALU.add, [rk, fk, "allc"], [fk])
                    if fpend:
                        fpend.pop(0)()
                    if not isup:
                        ft = ftmp[fcount % 2]

                        def ffin(ft=ft, fa=fa, fk=fk, ci=ci, fc_=fcount):
                            act(ft[:], fa[:], AF.Tanh, [fk], [AR("ftmp", fc_ % 2)])
                            stt("dve", gbuf[ci][:], ft[:], 1.0, fa[:], ALU.add, ALU.mult, [AR("ftmp", fc_ % 2), fk], [AR("gbuf", ci)])
                        fpend.append(ffin)
                    else:
                        def ufin(fa=fa, fk=fk, ci=ci, fchunk=fchunk):
                            tt("pool", actT[:, fchunk, :], fa[:], gbuf[ci - 2][:], ALU.mult, [fk, AR("gbuf", ci - 2)], [AR("actT", fchunk)])
                        fpend.append(ufin)
                    fcount += 1
            while fpend:
                fpend.pop(0)()
            nb_, nj_ = (b, j + 1) if j + 1 < NQ else (b + 1, 0)
            if nb_ < NB:
                xstage = [ab([128, 1024], F32) for _ in range(2)]
                for i in range(4):
                    xs_ = xstage[i % 2]
                    dma(xs_[:], x[nb_, 512 * nj_ + 128 * i:512 * nj_ + 128 * i + 128, :], [], [AR("xstage", i % 2)])
                    xb = xb16[i % 2]
                    cp("pool", xb[:], xs_[:], [AR("xstage", i % 2)], [("xb16", i % 2)])
                    bk = 4 + i % 2
                    for kc in range(8):
                        tr(bankbf[bk][:, 128 * kc:128 * kc + 128], xb[:, 128 * kc:128 * kc + 128], ident[:], [("xb16", i % 2), "ident"], [BK[bk]])
                    cp("act", xT[:, :, 128 * i:128 * i + 128], bankbf[bk][:, 0:1024].rearrange("p (kc t) -> p kc t", kc=8), [BK[bk]], [("xT", i)])
            for c in range(2):
                for gi, (f0, f1) in enumerate(FCG):
                    Wt, Wk = wnext(WB_DN + 3 * c + gi)
                    Wv = v3(Wt, f1 - f0, 512)
                    for i in range(4):
                        for fc in range(f0, f1):
                            mm(banks[i][:, :], actT[:, fc, 128 * i:128 * i + 128], Wv[:, fc - f0, :], fc == 0, fc == 21, [AR("actT", fc), Wk], [BK[i]])
                for i in range(4):
                    dst = A[:, i, 512 * c:512 * c + 512]
                    stt("dve", dst, dst, ALPHA, banks[i][:, :], ALU.mult, ALU.add, [("A", i), BK[i]], [("A", i)])
            layer_norm_blocks(True)

    S.muted = False
    S.op("sp", lambda e: e.nop(), ["out"] + (["dbg_yssd", "dbg_yatt", "dbg_h", "dbg_thr"] if debug else []), [])
    stats = S.emit(st)
    return nc, st, stats


_PARAM_SHAPES = {
    "w_in": (1024, 4696), "ssd_conv_w": (4, 1536), "ssd_conv_b": (1, 1536), "dt_bias": (1, 16), "a_log": (1, 16),
    "d_skip": (1, 16), "ssd_norm_g": (1, 1024), "idx_k_norm_g": (1, 64), "idx_k_norm_b": (1, 64), "w_out": (2048, 1024),
    "ln1_g": (1, 1024), "ln1_b": (1, 1024), "ffn_w_up": (1024, 5632), "ffn_conv_w": (3, 5632), "ffn_conv_b": (1, 5632),
    "ffn_w_down": (2816, 1024), "ln2_g": (1, 1024), "ln2_b": (1, 1024),
}


def run(inputs, n_cores, NB, NQ, debug=False, trace=False, stop=99):
    nc, st, stats = build(NB, NQ, debug, stop)
    L = NQ * 512
    params = {k: np.ascontiguousarray(np.asarray(inputs[k], dtype=np.float32).reshape(shp)) for k, shp in _PARAM_SHAPES.items()}
    x = np.asarray(inputs["x"], dtype=np.float32)
    in_maps = []
    for c in range(n_cores):
        m = dict(params)
        m["x"] = np.ascontiguousarray(x[c * NB:(c + 1) * NB, :L, :])
        in_maps.append(m)
    res = run_bass_kernel_spmd(nc, in_maps, core_ids=list(range(n_cores)), **({"trace": True} if trace else {}))
    st.close()
    return res, stats


def kernel(**inputs):
    res, _ = run(inputs, 8, 2, 8)
    return np.concatenate([r["out"] for r in res.results], axis=0).astype(np.float32)
```
